# Optimizing a Trainium2 kernel written in Bass

```python
import math
import jax, jax.numpy as jnp
from jax import lax
import numpy as np

D_MODEL = 1024
BATCH = 2
SEQ = 8192
DEPTH = 2
DEC_BATCH = 128
DEC_SEQ = 4
PAST_LEN = 16384
PAGE_SIZE = 128

N_META = 16
N_A_LAYERS = DEPTH // 2
N_B_LAYERS = DEPTH - N_A_LAYERS
CONV_A_WIDTH = 31
FFN_CONV_WIDTH = 3
D_FF = ((8 * D_MODEL // 3 + 127) // 128) * 128
N_HEADS = D_MODEL // 64
QK_NOPE = 64
QK_ROPE = 32
V_HEAD = 64
KV_LORA = D_MODEL // 8
Q_LORA = D_MODEL // 4
ROPE_THETA = 10000.0
NORM_EPS = 1e-6
Q_BLOCK = 128
SM_SCALE = 1.0 / math.sqrt(QK_NOPE + QK_ROPE)
NEG = -1e30

kernel_name = 'yoco_conformer_conv_mla_convffn_step'


def rms_norm(x, g):
    xf = x.astype(jnp.float32)
    y = xf * lax.rsqrt(jnp.mean(xf * xf, axis=-1, keepdims=True) + NORM_EPS)
    return (y * g.astype(jnp.float32)).astype(x.dtype)


def layer_norm(x, g, b):
    xf = x.astype(jnp.float32)
    mu = jnp.mean(xf, axis=-1, keepdims=True)
    var = jnp.mean(jnp.square(xf - mu), axis=-1, keepdims=True)
    y = (xf - mu) * lax.rsqrt(var + NORM_EPS) * g.astype(jnp.float32) + b.astype(jnp.float32)
    return y.astype(x.dtype)


def causal_dwconv(x, prefix, w):
    width = w.shape[0]
    xp = jnp.concatenate([prefix.astype(x.dtype), x], axis=1)
    y = lax.conv_general_dilated(xp, w[:, None, :].astype(x.dtype), window_strides=(1,), padding='VALID',
                                 dimension_numbers=('NWC', 'WIO', 'NWC'), feature_group_count=x.shape[-1])
    return y, xp[:, xp.shape[1] - (width - 1):]


def rope(x, pos):
    half = x.shape[-1] // 2
    inv = ROPE_THETA ** (-jnp.arange(half, dtype=jnp.float32) / half)
    ang = pos.astype(jnp.float32)[:, None] * inv[None, :]
    shape = (1, pos.shape[0]) + (1,) * (x.ndim - 3) + (half,)
    cos = jnp.cos(ang).reshape(shape)
    sin = jnp.sin(ang).reshape(shape)
    xf = x.astype(jnp.float32)
    x1, x2 = xf[..., :half], xf[..., half:]
    return jnp.concatenate([x1 * cos - x2 * sin, x1 * sin + x2 * cos], axis=-1).astype(x.dtype)


def conv_module(x, prefix, w_pw1, b_pw1, w_dw, b_dw, ln_g, ln_b, w_pw2, b_pw2):
    a, g = jnp.split(x @ w_pw1 + b_pw1, 2, axis=-1)
    u = a * jax.nn.sigmoid(g)
    y, tail = causal_dwconv(u, prefix, w_dw)
    y = jax.nn.silu(layer_norm(y + b_dw, ln_g, ln_b))
    return y @ w_pw2 + b_pw2, tail


def conv_ffn(x, prefix, w_up, w_dw, w_down):
    h, tail = causal_dwconv(x @ w_up, prefix, w_dw)
    g, u = jnp.split(h, 2, axis=-1)
    return (jax.nn.silu(g) * u) @ w_down, tail


def mla_shared_kv(h, pos, kv_norm, w_dkv, lat_norm, w_kr, knorm_rope):
    hn = rms_norm(h, kv_norm)
    c = rms_norm(hn @ w_dkv, lat_norm)
    kr = rope(rms_norm(hn @ w_kr, knorm_rope), pos)
    return c, kr


def mla_queries(xn, pos, w_dq, q_lat_norm, w_uq, qnorm_nope, qnorm_rope):
    q = jnp.einsum('btl,lhd->bthd', rms_norm(xn @ w_dq, q_lat_norm), w_uq)
    q_nope = rms_norm(q[..., :QK_NOPE], qnorm_nope)
    q_rope = rope(rms_norm(q[..., QK_NOPE:], qnorm_rope), pos)
    return q_nope, q_rope


def key_nope(c, w_uk, knorm_nope):
    return rms_norm(jnp.einsum('btc,chd->bthd', c, w_uk), knorm_nope)


def mla_attend_prompt(q_nope, q_rope, c, kr, w_uk, w_uv, knorm_nope):
    b, t = q_nope.shape[0], q_nope.shape[1]
    k_nope = key_nope(c, w_uk, knorm_nope)
    v = jnp.einsum('btc,chd->bthd', c, w_uv)
    n_blk = -(-t // Q_BLOCK)
    pad = n_blk * Q_BLOCK - t

    def blocks(q):
        q = jnp.pad(q, ((0, 0), (0, pad), (0, 0), (0, 0)))
        return q.reshape(b, n_blk, Q_BLOCK, q.shape[2], q.shape[3]).transpose(1, 0, 2, 3, 4)

    kpos = jnp.arange(t)

    def block(args):
        qn_b, qr_b, start = args
        s = (jnp.einsum('bqhd,bkhd->bhqk', qn_b, k_nope) +
             jnp.einsum('bqhr,bkr->bhqk', qr_b, kr)).astype(jnp.float32) * SM_SCALE
        qpos = start + jnp.arange(Q_BLOCK)
        s = jnp.where(kpos[None, :] <= qpos[:, None], s, NEG)
        p = jax.nn.softmax(s, axis=-1).astype(v.dtype)
        return jnp.einsum('bhqk,bkhd->bqhd', p, v)

    o = lax.map(block, (blocks(q_nope), blocks(q_rope), jnp.arange(n_blk) * Q_BLOCK))
    o = o.transpose(1, 0, 2, 3, 4).reshape(b, n_blk * Q_BLOCK, N_HEADS * V_HEAD)
    return o[:, :t]


def mla_attend_sample(q_nope, q_rope, c_new, kr_new, cache_lat, cache_kr, page_table, w_uk, w_uv, knorm_nope):
    db, q = q_nope.shape[0], q_nope.shape[1]
    k_new = key_nope(c_new, w_uk, knorm_nope)
    s = (jnp.einsum('bqhd,bkhd->bhqk', q_nope, k_new) +
         jnp.einsum('bqhr,bkr->bhqk', q_rope, kr_new)).astype(jnp.float32) * SM_SCALE
    causal = jnp.arange(q)[None, :] <= jnp.arange(q)[:, None]
    s = jnp.where(causal, s, NEG)
    m = jnp.max(s, axis=-1)
    p = jnp.exp(s - m[..., None])
    l = jnp.sum(p, axis=-1)
    acc = jnp.einsum('bhqk,bkc->bhqc', p, c_new.astype(jnp.float32))

    def step(carry, phys):
        m, l, acc = carry
        c = cache_lat[phys]
        kr = cache_kr[phys]
        k = key_nope(c, w_uk, knorm_nope)
        s = (jnp.einsum('bqhd,bphd->bhqp', q_nope, k) +
             jnp.einsum('bqhr,bpr->bhqp', q_rope, kr)).astype(jnp.float32) * SM_SCALE
        m_new = jnp.maximum(m, jnp.max(s, axis=-1))
        alpha = jnp.exp(m - m_new)
        p = jnp.exp(s - m_new[..., None])
        l = l * alpha + jnp.sum(p, axis=-1)
        acc = acc * alpha[..., None] + jnp.einsum('bhqp,bpc->bhqc', p, c.astype(jnp.float32))
        return (m_new, l, acc), None

    (m, l, acc), _ = lax.scan(step, (m, l, acc), page_table.T)
    o_lat = acc / l[..., None]
    o = jnp.einsum('bhqc,chv->bqhv', o_lat, w_uv.astype(jnp.float32)).astype(q_nope.dtype)
    return o.reshape(db, q, N_HEADS * V_HEAD)


def setup_inputs(seed: int = 0) -> dict:
    key = jax.random.key(seed)
    ks = iter(jax.random.split(key, 48))

    def nrm(shape, scale):
        return jax.random.normal(next(ks), shape, jnp.float32) * scale

    def gain(shape):
        return 1.0 + nrm(shape, 0.05)

    n_pages = PAST_LEN // PAGE_SIZE
    n_used = DEC_BATCH * n_pages
    n_phys = n_used + max(1, n_used // 4)
    page_table = jax.random.permutation(next(ks), n_phys)[:n_used].reshape(DEC_BATCH, n_pages).astype(jnp.int32)
    f2 = 2 * D_FF
    return {
        'x_prompt': nrm((BATCH, SEQ, D_MODEL), 1.0),
        'x_sample': nrm((DEC_BATCH, DEC_SEQ, D_MODEL), 1.0),
        'state_conv_a': nrm((N_A_LAYERS, DEC_BATCH, CONV_A_WIDTH - 1, D_MODEL), 0.5),
        'state_ffn_conv': nrm((DEPTH, DEC_BATCH, FFN_CONV_WIDTH - 1, f2), 1.0),
        'cache_kv_latent': nrm((n_phys, PAGE_SIZE, KV_LORA), 1.0),
        'cache_k_rope': nrm((n_phys, PAGE_SIZE, QK_ROPE), 1.0),
        'page_table': page_table,
        'meta_tokens': nrm((N_META, D_MODEL), 1.0),
        'norm_mix': gain((DEPTH, D_MODEL)),
        'norm_ffn': gain((DEPTH, D_MODEL)),
        'a_w_pw1': nrm((N_A_LAYERS, D_MODEL, 2 * D_MODEL), D_MODEL ** -0.5),
        'a_b_pw1': nrm((N_A_LAYERS, 2 * D_MODEL), 0.02),
        'a_w_dw': nrm((N_A_LAYERS, CONV_A_WIDTH, D_MODEL), CONV_A_WIDTH ** -0.5),
        'a_b_dw': nrm((N_A_LAYERS, D_MODEL), 0.02),
        'a_ln_g': gain((N_A_LAYERS, D_MODEL)),
        'a_ln_b': nrm((N_A_LAYERS, D_MODEL), 0.02),
        'a_w_pw2': nrm((N_A_LAYERS, D_MODEL, D_MODEL), D_MODEL ** -0.5),
        'a_b_pw2': nrm((N_A_LAYERS, D_MODEL), 0.02),
        'ffn_w_up': nrm((DEPTH, D_MODEL, f2), D_MODEL ** -0.5),
        'ffn_w_dw': nrm((DEPTH, FFN_CONV_WIDTH, f2), FFN_CONV_WIDTH ** -0.5),
        'ffn_w_down': nrm((DEPTH, D_FF, D_MODEL), D_FF ** -0.5),
        'kv_norm': gain((D_MODEL,)),
        'mla_w_dkv': nrm((D_MODEL, KV_LORA), D_MODEL ** -0.5),
        'mla_lat_norm': gain((KV_LORA,)),
        'mla_w_kr': nrm((D_MODEL, QK_ROPE), D_MODEL ** -0.5),
        'mla_knorm_rope': gain((QK_ROPE,)),
        'mla_w_uk': nrm((KV_LORA, N_HEADS, QK_NOPE), KV_LORA ** -0.5),
        'mla_w_uv': nrm((KV_LORA, N_HEADS, V_HEAD), KV_LORA ** -0.5),
        'mla_knorm_nope': gain((QK_NOPE,)),
        'mla_w_dq': nrm((N_B_LAYERS, D_MODEL, Q_LORA), D_MODEL ** -0.5),
        'mla_q_lat_norm': gain((N_B_LAYERS, Q_LORA)),
        'mla_w_uq': nrm((N_B_LAYERS, Q_LORA, N_HEADS, QK_NOPE + QK_ROPE), Q_LORA ** -0.5),
        'mla_qnorm_nope': gain((N_B_LAYERS, QK_NOPE)),
        'mla_qnorm_rope': gain((N_B_LAYERS, QK_ROPE)),
        'mla_w_o': nrm((N_B_LAYERS, N_HEADS * V_HEAD, D_MODEL), (N_HEADS * V_HEAD) ** -0.5),
    }


def reference(x_prompt, x_sample, state_conv_a, state_ffn_conv, cache_kv_latent, cache_k_rope, page_table,
              meta_tokens, norm_mix, norm_ffn,
              a_w_pw1, a_b_pw1, a_w_dw, a_b_dw, a_ln_g, a_ln_b, a_w_pw2, a_b_pw2,
              ffn_w_up, ffn_w_dw, ffn_w_down,
              kv_norm, mla_w_dkv, mla_lat_norm, mla_w_kr, mla_knorm_rope, mla_w_uk, mla_w_uv, mla_knorm_nope,
              mla_w_dq, mla_q_lat_norm, mla_w_uq, mla_qnorm_nope, mla_qnorm_rope, mla_w_o):
    dt = x_prompt.dtype
    b_p = x_prompt.shape[0]
    meta = jnp.broadcast_to(meta_tokens.astype(dt)[None], (b_p, N_META, D_MODEL))
    hp = jnp.concatenate([meta, x_prompt], axis=1)
    hs = x_sample
    pos_p = jnp.arange(hp.shape[1], dtype=jnp.int32)
    pos_s = PAST_LEN + jnp.arange(hs.shape[1], dtype=jnp.int32)
    zero_conv_a = jnp.zeros((b_p, CONV_A_WIDTH - 1, D_MODEL), dt)
    zero_ffn = jnp.zeros((b_p, FFN_CONV_WIDTH - 1, 2 * D_FF), dt)
    conv_a_p, conv_a_s, ffn_p, ffn_s = [], [], [], []
    for layer in range(DEPTH):
        if layer < N_A_LAYERS:
            a_params = (a_w_pw1[layer], a_b_pw1[layer], a_w_dw[layer], a_b_dw[layer],
                        a_ln_g[layer], a_ln_b[layer], a_w_pw2[layer], a_b_pw2[layer])
            d_p, t_p = conv_module(rms_norm(hp, norm_mix[layer]), zero_conv_a, *a_params)
            d_s, t_s = conv_module(rms_norm(hs, norm_mix[layer]), state_conv_a[layer], *a_params)
            hp, hs = hp + d_p, hs + d_s
            conv_a_p.append(t_p)
            conv_a_s.append(t_s)
        else:
            if layer == N_A_LAYERS:
                kv_params = (kv_norm, mla_w_dkv, mla_lat_norm, mla_w_kr, mla_knorm_rope)
                c_p, kr_p = mla_shared_kv(hp, pos_p, *kv_params)
                c_s, kr_s = mla_shared_kv(hs, pos_s, *kv_params)
            j = layer - N_A_LAYERS
            q_params = (mla_w_dq[j], mla_q_lat_norm[j], mla_w_uq[j], mla_qnorm_nope[j], mla_qnorm_rope[j])
            qn_p, qr_p = mla_queries(rms_norm(hp, norm_mix[layer]), pos_p, *q_params)
            qn_s, qr_s = mla_queries(rms_norm(hs, norm_mix[layer]), pos_s, *q_params)
            o_p = mla_attend_prompt(qn_p, qr_p, c_p, kr_p, mla_w_uk, mla_w_uv, mla_knorm_nope)
            o_s = mla_attend_sample(qn_s, qr_s, c_s, kr_s, cache_kv_latent, cache_k_rope, page_table,
                                    mla_w_uk, mla_w_uv, mla_knorm_nope)
            hp = hp + o_p @ mla_w_o[j]
            hs = hs + o_s @ mla_w_o[j]
        f_params = (ffn_w_up[layer], ffn_w_dw[layer], ffn_w_down[layer])
        d_p, t_p = conv_ffn(rms_norm(hp, norm_ffn[layer]), zero_ffn, *f_params)
        d_s, t_s = conv_ffn(rms_norm(hs, norm_ffn[layer]), state_ffn_conv[layer], *f_params)
        hp, hs = hp + d_p, hs + d_s
        ffn_p.append(t_p)
        ffn_s.append(t_s)
    return (hp[:, N_META:], hs, jnp.stack(conv_a_p), jnp.stack(conv_a_s), jnp.stack(ffn_p), jnp.stack(ffn_s),
            c_p, kr_p, c_s, kr_s)
```

```python
import contextlib
import numpy as np
import concourse.bass as bass
import concourse.mybir as mybir
from concourse.bass_utils import run_bass_kernel_spmd

F32 = mybir.dt.float32
BF16 = mybir.dt.bfloat16
I32 = mybir.dt.int32
ALU = mybir.AluOpType
AF = mybir.ActivationFunctionType
AX = mybir.AxisListType

D = 1024
KC = 8
SEQ = 8192
NMETA = 16
NPOS = SEQ + NMETA
TS_ = 342
HALO = 34
TW = TS_ + HALO
NT = 6
NQ = TW - 32
NS = 16
NSC = 64
DFF = 2816
NF = 22
NPAGE = 128
PAST = 16384
EPS = 1e-6
SM_SCALE = 1.0 / np.sqrt(96.0)
NKC = 65
EPOCH = 12000
SAME_ENGINE_SYNC = True


class Sched:
    ENG = ("pe", "act", "dve", "pool", "sp")

    def __init__(self, nc):
        self.nc = nc
        self.ops = {e: [] for e in self.ENG}
        self.count = {e: 0 for e in self.ENG}
        self.waited = {e: {} for e in self.ENG}
        self.last_write = {}
        self.readers = {}
        self.lane_count = {}
        self.sems = {}
        self.all_sem_keys = []

    def _need(self, eng, tok):
        key, val = tok
        if key[0] == eng and (not SAME_ENGINE_SYNC or eng == "pe"):
            return
        w = self.waited[eng]
        if key[0] in self.ENG:
            cur = w.get(key[0], (-1, 0))
            if (key[1], val) <= cur:
                return
            w[key[0]] = (key[1], val)
        else:
            if w.get(key, 0) >= val:
                return
            w[key] = val
        self.ops[eng].append(("wait", key, val))

    def _deps(self, eng, reads, writes):
        toks = []
        for r in reads:
            t = self.last_write.get(r)
            if t:
                toks.append(t)
            if r[0] == "B" and (r[1:].isdigit() or r == "BB"):
                toks.extend(x for x in self.readers.get(r, ()) if x[0][0] != eng)
        for r in writes:
            t = self.last_write.get(r)
            if t:
                toks.append(t)
            toks.extend(self.readers.get(r, ()))
        for t in toks:
            self._need(eng, t)

    def _commit(self, tok, reads, writes):
        for r in writes:
            self.last_write[r] = tok
            self.readers[r] = []
        for r in reads:
            if r in writes:
                continue
            self.readers.setdefault(r, []).append(tok)

    def op(self, eng, fn, reads=(), writes=()):
        self._deps(eng, reads, writes)
        self.count[eng] += 1
        idx = self.count[eng]
        key = (eng, (idx - 1) // EPOCH)
        val = (idx - 1) % EPOCH + 1
        if key not in self.sems:
            self.sems[key] = None
            self.all_sem_keys.append(key)
        self.ops[eng].append(("op", fn, key, 1))
        self._commit((key, val), reads, writes)

    def dma(self, q, fn, lane, reads=(), writes=()):
        self._deps(q, reads, writes)
        self.rr = getattr(self, "rr", {})
        self.rr[q] = self.rr.get(q, 0) + 1
        lane = "%s%d" % (q, self.rr[q] % 10)
        key = ("lane", lane)
        if self.lane_count.get(lane, 0) > 0:
            self._need(q, (key, self.lane_count[lane]))
        if key not in self.sems:
            self.sems[key] = None
            self.all_sem_keys.append(key)
        self.lane_count[lane] = self.lane_count.get(lane, 0) + 16
        self.ops[q].append(("op", fn, key, 16))
        self._commit((key, self.lane_count[lane]), reads, writes)

    def barrier(self):
        toks = []
        for e in self.ENG:
            idx = self.count[e]
            if idx > 0:
                toks.append(((e, (idx - 1) // EPOCH), (idx - 1) % EPOCH + 1))
        for lane, cnt in self.lane_count.items():
            toks.append((("lane", lane), cnt))
        for e in self.ENG:
            for t in toks:
                self._need(e, t)

    def all_wait(self, res):
        t = self.last_write.get(res)
        if t:
            for e in self.ENG:
                self._need(e, t)

    def emit(self):
        nc = self.nc
        with contextlib.ExitStack() as st:
            for i, key in enumerate(self.all_sem_keys):
                self.sems[key] = st.enter_context(nc.semaphore("s%d" % i))
            for key in self.all_sem_keys:
                if key[0] == "lane":
                    self._need("sp", (key, self.lane_count[key[1]]))
            blk = st.enter_context(nc.Block())

            def run(engname):
                def body(e):
                    for item in self.ops[engname]:
                        if item[0] == "wait":
                            e.wait_ge(self.sems[item[1]], item[2])
                        else:
                            item[1](e).then_inc(self.sems[item[2]], item[3])
                return body

            blk.tensor(run("pe"))
            blk.scalar(run("act"))
            blk.vector(run("dve"))
            blk.gpsimd(run("pool"))
            blk.sync(run("sp"))


VEC_LAYOUT = [("nm0", 8), ("nm1", 8), ("nf0", 8), ("nf1", 8), ("kvn", 8), ("b1", 16), ("bdw", 8), ("lng", 8),
              ("lnb", 8), ("b2", 8), ("wdwa", 8 * 31), ("wdwf0", 44 * 3), ("wdwf1", 44 * 3), ("latn", 1),
              ("knr", 1), ("qln", 2), ("g96", 1), ("gk64", 1)]
VOFF = {}
_o = 0
for _n, _w in VEC_LAYOUT:
    VOFF[_n] = (_o, _w)
    _o += _w
NVEC = _o

CB_LAYOUT = [("o1024", 128), ("o256", 128), ("o128", 128), ("blk96", 96), ("o64", 64), ("p96", 96), ("identb", 128),
             ("onesb", 128)]
CBOFF = {}
_o = 0
for _n, _w in CB_LAYOUT:
    CBOFF[_n] = (_o, _w)
    _o += _w
NCB = _o


def _fm(v, nchunk):
    return np.ascontiguousarray(np.asarray(v, np.float32).reshape(nchunk, 128).T)


def _consts_bf():
    c = np.zeros((128, NCB), np.float32)
    def put(name, m):
        o, w = CBOFF[name]
        c[:m.shape[0], o:o + m.shape[1]] = m
    put("o1024", np.full((128, 128), 1.0 / 1024, np.float32))
    put("o256", np.full((128, 128), 1.0 / 256, np.float32))
    put("o128", np.full((128, 128), 1.0 / 128, np.float32))
    blk = np.zeros((96, 96), np.float32)
    blk[:64, :64] = 1.0 / 64
    blk[64:, 64:] = 1.0 / 32
    put("blk96", blk)
    put("o64", np.full((64, 64), 1.0 / 64, np.float32))
    p96 = np.zeros((96, 96), np.float32)
    for m in range(16):
        p96[64 + m + 16, 64 + m] = -1.0
        p96[64 + m, 64 + m + 16] = 1.0
    put("p96", p96)
    put("identb", np.eye(128, dtype=np.float32))
    put("onesb", np.ones((128, 128), np.float32))
    return c


CF_LAYOUT = [("identf", 128), ("p32", 32), ("o32", 32), ("onesf", 128), ("iota", NQ)]
CFOFF = {}
_o = 0
for _n, _w in CF_LAYOUT:
    CFOFF[_n] = (_o, _w)
    _o += _w
NCF = _o


def _consts_f():
    c = np.zeros((128, NCF), np.float32)
    def put(name, m):
        o, w = CFOFF[name]
        c[:m.shape[0], o:o + m.shape[1]] = m
    put("identf", np.eye(128, dtype=np.float32))
    p32 = np.zeros((32, 32), np.float32)
    for m in range(16):
        p32[m + 16, m] = -1.0
        p32[m, m + 16] = 1.0
    put("p32", p32)
    put("o32", np.full((32, 32), 1.0 / 32, np.float32))
    put("onesf", np.ones((128, 128), np.float32))
    put("iota", (np.arange(NQ)[None, :] - np.arange(128)[:, None]).astype(np.float32))
    return c


def _rope_tab(pos):
    inv = (10000.0 ** (-np.arange(16, dtype=np.float32) / 16)).astype(np.float32)
    ang = pos.astype(np.float32)[None, :] * inv[:, None]
    cos = np.cos(ang).astype(np.float32)
    sin = np.sin(ang).astype(np.float32)
    return np.concatenate([cos, cos], 0), np.concatenate([sin, sin], 0)


def build_program(n_phys):
    nc = bass.Bass("TRN2", target_bir_lowering=False)
    S = Sched(nc)

    def din(name, shape, dt=F32):
        return nc.dram_tensor(name, list(shape), dt, kind="ExternalInput").ap()

    def dout(name, shape, dt=F32):
        return nc.dram_tensor(name, list(shape), dt, kind="ExternalOutput").ap()

    xp_d = din("xp", [NT, 128, KC, TW])
    xs_d = din("xs", [128, KC, NSC])
    sca_d = din("sca", [128, KC, NS, 30])
    sff_d = din("sff", [2, 128, 44, NS, 2])
    cc_d = din("cache_c", [n_phys * 8, 2048])
    ckr_d = din("cache_kr", [n_phys * 2, 2048])
    pt_d = din("ptT", [128, NS], I32)
    cmask_d = din("colmask", [128, HALO])
    cosq_d = din("cosq", [96, NT, NQ])
    sinq_d = din("sinq", [96, NT, NQ])
    cosk_d = din("cosk", [32, NT, TS_])
    sink_d = din("sink", [32, NT, TS_])
    cosqs_d = din("cosqs", [96, NSC])
    sinqs_d = din("sinqs", [96, NSC])
    cosks_d = din("cosks", [32, NSC])
    sinks_d = din("sinks", [32, NSC])
    thr_d = din("thr", [128, NT * NKC])
    mnew_d = din("masknew", [64, NS, 64])
    vec_d = din("vec", [128, NVEC])
    cb_d = din("cb", [128, NCB])
    cf_d = din("cf", [128, NCF])
    w1_d = din("w1l", [8, 128, 2048])
    w2_d = din("w2l", [4, 128, 2048])
    wup_d = din("wupl", [2, NF, 128, 2048])
    wdn_d = din("wdnl", [2, 16, 128, 1408])
    wo_d = din("wol", [8, 128, 1024])
    wdq_d = din("wdq", [128, 2048])
    wuq_d = din("wuq", [2, 128, 1536])
    wdkv_d = din("wdkv", [128, 1024])
    wkr_d = din("wkr", [128, 256])
    wuk_d = din("wuk", [128, 1024])
    wuv_d = din("wuv", [128, 1024])
    wukT_d = din("wukT", [64, 2048])

    yp_o = dout("yp", [NT, 128, KC, TS_])
    ys_o = dout("ys", [128, KC, NSC])
    cap_o = dout("cap", [128, KC, 30])
    cas_o = dout("cas", [128, KC, NS, 30])
    ffp_o = dout("ffp", [2, 128, 44, 2])
    ffs_o = dout("ffs", [2, 128, 44, NS, 2])
    cp_o = dout("cp", [NT, 128, TS_])
    krp_o = dout("krp", [NT, 32, TS_])
    cs_o = dout("cs", [128, NSC])
    krs_o = dout("krs", [32, NSC])

    xch_in = nc.dram_tensor("xch_in", [160, NT * TS_], BF16)
    xch_out = nc.dram_tensor("xch_out", [640, NT * TS_], BF16)

    def MM(out, lhsT, rhs, start, stop, R, W, **kw):
        S.op("pe", lambda e: e.matmul(out, lhsT=lhsT, rhs=rhs, start=start, stop=stop, **kw), R, W)

    def TR(out, in_, ident, R, W):
        S.op("pe", lambda e: e.transpose(out=out, in_=in_, identity=ident), R, W)

    def ACT(out, in_, func, R, W, bias=None, scale=None):
        kw = {}
        if bias is not None:
            kw["bias"] = bias
        if scale is not None:
            kw["scale"] = scale
        S.op("act", lambda e: e.activation(out=out, in_=in_, func=func, **kw), R, W)

    def TSC(eng, out, in0, s1, s2, op0, op1, R, W):
        if op1 is None:
            S.op(eng, lambda e: e.tensor_scalar(out=out, in0=in0, scalar1=s1, scalar2=None, op0=op0), R, W)
        else:
            S.op(eng, lambda e: e.tensor_scalar(out=out, in0=in0, scalar1=s1, scalar2=s2, op0=op0, op1=op1), R, W)

    def STT(eng, out, in0, scalar, in1, op0, op1, R, W):
        S.op(eng, lambda e: e.scalar_tensor_tensor(out=out, in0=in0, scalar=scalar, in1=in1, op0=op0, op1=op1), R, W)

    def TT(eng, out, in0, in1, op, R, W):
        S.op(eng, lambda e: e.tensor_tensor(out=out, in0=in0, in1=in1, op=op), R, W)

    def CP(eng, out, in_, R, W):
        if eng == "act":
            S.op("act", lambda e: e.activation(out=out, in_=in_, func=AF.Copy), R, W)
        else:
            S.op(eng, lambda e: e.tensor_copy(out=out, in_=in_), R, W)

    def RECIP(out, in_, R, W):
        S.op("dve", lambda e: e.reciprocal(out=out, in_=in_), R, W)

    def MEMSET(eng, ap, val, W):
        S.op(eng, lambda e: e.memset(ap, val), (), W)

    def RED(out, in_, op, R, W):
        S.op("dve", lambda e: e.tensor_reduce(out=out, in_=in_, axis=AX.X, op=op), R, W)

    def REDP(out, in_, op, R, W):
        S.op("pool", lambda e: e.tensor_reduce(out=out, in_=in_, axis=AX.X, op=op), R, W)

    def DMA(q, out, in_, lane, R, W):
        S.dma(q, lambda e: e.dma_start(out=out, in_=in_), lane, R, W)

    with contextlib.ExitStack() as G:
        nctr = [0]

        def sb(name, shape, dt, st=G):
            nctr[0] += 1
            return st.enter_context(nc.sbuf_tensor("%s_%d" % (name, nctr[0]), list(shape), dt))

        banks = [G.enter_context(nc.psum_tensor("B%d" % i, [128, 512], F32)) for i in range(7)]
        bankb = G.enter_context(nc.psum_tensor("BB", [128, 1024], BF16))
        bctr = [0]

        def bank(lo=0, hi=7):
            i = lo + bctr[0] % (hi - lo)
            bctr[0] += 1
            return banks[i], "B%d" % i

        VEC = sb("VEC", [128, NVEC], F32)
        CB = sb("CB", [128, NCB], BF16)
        CF = sb("CF", [128, NCF], F32)
        WDQ = sb("WDQ", [128, 2048], BF16)
        WUQ = sb("WUQ", [128, 2, 1536], BF16)
        WDKV = sb("WDKV", [128, 1024], BF16)
        WKR = sb("WKR", [128, 256], BF16)
        WUK = sb("WUK", [128, 1024], BF16)
        WUV = sb("WUV", [128, 1024], BF16)
        DMA("sp", VEC[:], vec_d, "c0", (), ["VEC"])
        DMA("sp", CF[:], cf_d, "c0", (), ["CF"])
        DMA("pool", CB[:], cb_d, "c1", (), ["CB"])
        DMA("pool", WDQ[:], wdq_d, "c1", (), ["WDQ"])
        DMA("pool", WUQ[:, 0, :], wuq_d[0], "c1", (), ["WUQ"])
        DMA("pool", WUQ[:, 1, :], wuq_d[1], "c1", (), ["WUQ"])
        DMA("pool", WDKV[:], wdkv_d, "c1", (), ["WDKV"])
        DMA("pool", WKR[:], wkr_d, "c1", (), ["WKR"])
        DMA("pool", WUK[:], wuk_d, "c1", (), ["WUK"])
        DMA("pool", WUV[:], wuv_d, "c1", (), ["WUV"])
        for r in ("VEC", "CF", "CB", "WDQ", "WUQ", "WDKV", "WKR", "WUK", "WUV"):
            S.all_wait(r)

        def V(name, a=None, b=None):
            o, w = VOFF[name]
            if a is None:
                return VEC[:, o:o + w]
            return VEC[:, o + a:o + b]

        def CBm(name, rows=128):
            o, w = CBOFF[name]
            return CB[0:rows, o:o + w]

        def CFm(name, rows=128):
            o, w = CFOFF[name]
            return CF[0:rows, o:o + w]

        NSLOT = 4
        WS = [sb("WS%d" % i, [128, 2048], BF16) for i in range(NSLOT)]
        wctr = [0]

        def wload(src, ne):
            i = wctr[0] % NSLOT
            wctr[0] += 1
            DMA("pool", WS[i][:, 0:ne], src, "w%d" % i, (), ["WS%d" % i])
            return WS[i], "WS%d" % i

        def rstd_from_ms(ps_ap, rs_ap, Rps, Wrs):
            ACT(rs_ap, ps_ap, AF.Sqrt, [Rps], [Wrs], bias=EPS, scale=1.0)
            RECIP(rs_ap, rs_ap, [Wrs], [Wrs])

        def rmsnorm(L, Hh, Hres, c0, n, gain):
            ACT(L["SQ"][:, :, 0:n], Hh[:, :, c0:c0 + n], AF.Square, [Hres], ["SQ"])
            ps, pr = bank()
            for k in range(KC):
                MM(ps[:, 0:n], CBm("o1024"), L["SQ"][:, k, 0:n], k == 0, k == KC - 1, ["SQ"], [pr])
            rstd_from_ms(ps[:, 0:n], L["RS"][:, 0:n], pr, "RS")
            for k in range(KC):
                STT("dve", L["XN"][:, k, 0:n], Hh[:, k, c0:c0 + n], gain[:, k:k + 1], L["RS"][:, 0:n],
                    ALU.mult, ALU.mult, [Hres, "RS"], ["XN"])

        def conv_module(L, Hh, Hres, N, Sq, Lw, U, mask_first, tail_out, tail_lane):
            Lo = Lw - 30
            n = Sq * Lo
            newc = N // Sq
            rmsnorm(L, Hh, Hres, 0, N, V("nm0"))
            for m in range(KC):
                wt, wr = wload(w1_d[m], 2048)
                pa, pra = bank()
                pg, prg = bank()
                for k in range(KC):
                    MM(pa[:, 0:N], wt[:, k * 256:k * 256 + 128], L["XN"][:, k, 0:N], k == 0, k == KC - 1, [wr, "XN"], [pra])
                for k in range(KC):
                    MM(pg[:, 0:N], wt[:, k * 256 + 128:k * 256 + 256], L["XN"][:, k, 0:N], k == 0, k == KC - 1, [wr, "XN"], [prg])
                ACT(L["SG"][:, 0:N], pg[:, 0:N], AF.Sigmoid, [prg], ["SG"], bias=V("b1", 8 + m, 9 + m))
                uo = U[:, m, :, Lw - newc:Lw]
                STT("dve", uo, pa[:, 0:N].rearrange("p (s l) -> p s l", s=Sq), V("b1", m, m + 1),
                    L["SG"][:, 0:N].rearrange("p (s l) -> p s l", s=Sq), ALU.add, ALU.mult, [pra, "SG"], ["U"])
                if mask_first:
                    TT("dve", U[:, m, 0, 0:HALO], U[:, m, 0, 0:HALO], L["CMASK"][:, 0:HALO], ALU.mult, ["U", "CMASK"], ["U"])
                yb = L["YB"][:, m, 0:n].rearrange("p (s l) -> p s l", s=Sq)
                o, _ = VOFF["wdwa"]
                TSC("dve", yb, U[:, m, :, 0:Lo], VEC[:, o + m * 31:o + m * 31 + 1], V("bdw", m, m + 1), ALU.mult, ALU.add,
                    ["U"], ["YB"])
                K0 = 20
                yb2 = L["YB2"][:, 0:n].rearrange("p (s l) -> p s l", s=Sq)
                TSC("pool", yb2, U[:, m, :, K0:K0 + Lo], VEC[:, o + m * 31 + K0:o + m * 31 + K0 + 1], None, ALU.mult, None, ["U"], ["YB2"])
                yb3 = L["YB3"][:, 0:n].rearrange("p (s l) -> p s l", s=Sq)
                for kk in range(K0 + 1, 31):
                    TSC("pool", yb3, U[:, m, :, kk:kk + Lo], VEC[:, o + m * 31 + kk:o + m * 31 + kk + 1], None, ALU.mult, None, ["U"], ["YB3"])
                    TT("pool", yb2, yb2, yb3, ALU.add, ["YB2", "YB3"], ["YB2"])
                for kk in range(1, K0):
                    STT("dve", yb, U[:, m, :, kk:kk + Lo], VEC[:, o + m * 31 + kk:o + m * 31 + kk + 1], yb,
                        ALU.mult, ALU.add, ["U", "YB"], ["YB"])
                TT("dve", yb, yb, yb2, ALU.add, ["YB", "YB2"], ["YB"])
            if tail_out is not None:
                for m in range(KC):
                    CP("act", tail_lane[:, m], U[:, m, :, Lw - 30:Lw], ["U"], ["TSTG"])
                DMA("sp", tail_out, tail_lane[:] if Sq > 1 else tail_lane[:, :, 0, :], "x", ["TSTG"], [])
            ACT(L["SQ"][:, :, 0:n], L["YB"][:, :, 0:n], AF.Square, ["YB"], ["SQ"])
            CP("act", L["XN"][:, :, 0:n], L["YB"][:, :, 0:n], ["YB"], ["XN"])
            pm, prm = bank()
            pq, prq = bank()
            for k in range(KC):
                MM(pm[:, 0:n], CBm("o1024"), L["XN"][:, k, 0:n], k == 0, k == KC - 1, ["XN"], [prm])
            for k in range(KC):
                MM(pq[:, 0:n], CBm("o1024"), L["SQ"][:, k, 0:n], k == 0, k == KC - 1, ["SQ"], [prq])
            CP("dve", L["MEAN"][:, 0:n], pm[:, 0:n], [prm], ["MEAN"])
            TT("dve", L["RS"][:, 0:n], L["MEAN"][:, 0:n], L["MEAN"][:, 0:n], ALU.mult, ["MEAN"], ["RS"])
            TT("dve", L["RS"][:, 0:n], pq[:, 0:n], L["RS"][:, 0:n], ALU.subtract, [prq, "RS"], ["RS"])
            TSC("dve", L["RS"][:, 0:n], L["RS"][:, 0:n], 0.0, None, ALU.max, None, ["RS"], ["RS"])
            ACT(L["RS"][:, 0:n], L["RS"][:, 0:n], AF.Sqrt, ["RS"], ["RS"], bias=EPS, scale=1.0)
            RECIP(L["RS"][:, 0:n], L["RS"][:, 0:n], ["RS"], ["RS"])
            for k in range(KC):
                TT("dve", L["YB"][:, k, 0:n], L["YB"][:, k, 0:n], L["MEAN"][:, 0:n], ALU.subtract, ["YB", "MEAN"], ["YB"])
                TT("dve", L["YB"][:, k, 0:n], L["YB"][:, k, 0:n], L["RS"][:, 0:n], ALU.mult, ["YB", "RS"], ["YB"])
                ACT(L["XN"][:, k, 0:n], L["YB"][:, k, 0:n], AF.Silu, ["YB"], ["XN"], bias=V("lnb", k, k + 1), scale=V("lng", k, k + 1))
            c0 = N - n
            for i in range(4):
                wt, wr = wload(w2_d[i], 2048)
                for mo2 in range(2):
                    mo = 2 * i + mo2
                    po, pro = bank()
                    for k in range(KC):
                        MM(po[:, 0:n], wt[:, mo2 * 1024 + k * 128:mo2 * 1024 + k * 128 + 128], L["XN"][:, k, 0:n],
                           k == 0, k == KC - 1, [wr, "XN"], [pro])
                    STT("dve", Hh[:, mo, c0:N], po[:, 0:n], V("b2", mo, mo + 1), Hh[:, mo, c0:N], ALU.add, ALU.add,
                        [pro, Hres], [Hres])

        def conv_ffn(L, l, Hh, Hres, c1, N, Sq, UPG, UPU, mask_first, tail_out, tail_lane, pre_loaded):
            n1 = N - c1
            if pre_loaded:
                Lq = n1 // Sq
                nout = n1
            else:
                Lq = n1 - 2
                nout = Lq
            rmsnorm(L, Hh, Hres, c1, n1, V("nf%d" % l))
            wo_, _ = VOFF["wdwf%d" % l]
            for f in range(NF):
                wt, wr = wload(wup_d[l, f], 2048)
                pg, prg = bank()
                pu, pru = bank()
                for k in range(KC):
                    MM(pg[:, 0:n1], wt[:, k * 256:k * 256 + 128], L["XN"][:, k, 0:n1], k == 0, k == KC - 1, [wr, "XN"], [prg])
                for k in range(KC):
                    MM(pu[:, 0:n1], wt[:, k * 256 + 128:k * 256 + 256], L["XN"][:, k, 0:n1], k == 0, k == KC - 1, [wr, "XN"], [pru])
                for (ps_, pr_, UP, ci, cres0, cv) in ((pg, prg, UPG, f, "UPG", "CG"), (pu, pru, UPU, NF + f, "UPU", "CU")):
                    fx = f if pre_loaded else f % 2
                    cres = cres0 if pre_loaded else "%s%d" % (cres0, fx)
                    if pre_loaded:
                        CP("act", UP[:, fx, :, 2:2 + Lq], ps_[:, 0:n1].rearrange("p (s l) -> p s l", s=Sq), [pr_], [cres])
                    else:
                        CP("act", UP[:, fx, 0, 0:n1], ps_[:, 0:n1], [pr_], [cres])
                        if mask_first:
                            TT("dve", UP[:, fx, 0, 0:HALO - c1], UP[:, fx, 0, 0:HALO - c1], L["CMASK"][:, c1:HALO], ALU.mult, [cres, "CMASK"], [cres])
                        if tail_out is not None:
                            CP("dve", L["TAILB"][:, ci, :], UP[:, fx, 0, Lq:Lq + 2], [cres], ["TAILB"])
                    cvv = L[cv][:, 0:nout].rearrange("p (s l) -> p s l", s=Sq)
                    wb = wo_ + ci * 3
                    if cv == "CG":
                        TSC("dve", cvv, UP[:, fx, :, 0:Lq], VEC[:, wb:wb + 1], None, ALU.mult, None, [cres], [cv])
                        STT("dve", cvv, UP[:, fx, :, 1:1 + Lq], VEC[:, wb + 1:wb + 2], cvv, ALU.mult, ALU.add, [cres, cv], [cv])
                        STT("dve", cvv, UP[:, fx, :, 2:2 + Lq], VEC[:, wb + 2:wb + 3], cvv, ALU.mult, ALU.add, [cres, cv], [cv])
                    else:
                        y3 = L["YB3"][:, 0:nout].rearrange("p (s l) -> p s l", s=Sq)
                        TSC("pool", cvv, UP[:, fx, :, 0:Lq], VEC[:, wb:wb + 1], None, ALU.mult, None, [cres], [cv])
                        for tap in (1, 2):
                            TSC("pool", y3, UP[:, fx, :, tap:tap + Lq], VEC[:, wb + tap:wb + tap + 1], None, ALU.mult, None, [cres], ["YB3"])
                            TT("pool", cvv, cvv, y3, ALU.add, [cv, "YB3"], [cv])
                ACT(L["CG"][:, 0:nout], L["CG"][:, 0:nout], AF.Silu, ["CG"], ["CG"])
                TT("dve", L["AV"][:, f, 0:nout], L["CG"][:, 0:nout], L["CU"][:, 0:nout], ALU.mult, ["CG", "CU"], ["AV"])
            if tail_out is not None:
                if pre_loaded:
                    for (UP, cres, half) in ((UPG, "UPG", 0), (UPU, "UPU", 1)):
                        CP("act", tail_lane[:, half * NF:(half + 1) * NF], UP[:, :, :, Lq:Lq + 2], [cres], ["FSTG"])
                    DMA("sp", tail_out, tail_lane[:], "x", ["FSTG"], [])
                else:
                    DMA("sp", tail_out, L["TAILB"][:], tail_lane, ["TAILB"], [])
            co = N - nout
            for mo in range(KC):
                po, pro = bank()
                for half in range(2):
                    wt, wr = wload(wdn_d[l, mo * 2 + half], 1408)
                    for fi in range(11):
                        f = half * 11 + fi
                        MM(po[:, 0:nout], wt[:, fi * 128:fi * 128 + 128], L["AV"][:, f, 0:nout], f == 0, f == NF - 1,
                           [wr, "AV"], [pro])
                TT("dve", Hh[:, mo, co:N], po[:, 0:nout], Hh[:, mo, co:N], ALU.add, [pro, Hres], [Hres])

        def shared_kv(L, Hh, Hres, c0, n, cosk, sink, CF32, KRF32, CBF, KRBF):
            rmsnorm(L, Hh, Hres, c0, n, V("kvn"))
            pc, prc = bank()
            pk, prk = bank()
            for k in range(KC):
                MM(pc[:, 0:n], WDKV[:, k * 128:k * 128 + 128], L["XN"][:, k, 0:n], k == 0, k == KC - 1, ["XN"], [prc])
            for k in range(KC):
                MM(pk[0:32, 0:n], WKR[:, k * 32:k * 32 + 32], L["XN"][:, k, 0:n], k == 0, k == KC - 1, ["XN"], [prk])
            ACT(L["SQ"][:, 0, 0:n], pc[:, 0:n], AF.Square, [prc], ["SQ"])
            pm, prm = bank()
            MM(pm[:, 0:n], CBm("o128"), L["SQ"][:, 0, 0:n], True, True, ["SQ"], [prm])
            rstd_from_ms(pm[:, 0:n], L["RS"][:, 0:n], prm, "RS")
            STT("dve", CF32, pc[:, 0:n], V("latn"), L["RS"][:, 0:n], ALU.mult, ALU.mult, [prc, "RS"], ["CF32"])
            CP("act", CBF, CF32, ["CF32"], ["CBF"])
            ACT(L["T32"][0:32, 0:n], pk[0:32, 0:n], AF.Square, [prk], ["T32"])
            pm2, prm2 = bank()
            MM(pm2[0:32, 0:n], CFm("o32", 32), L["T32"][0:32, 0:n], True, True, ["T32"], [prm2])
            rstd_from_ms(pm2[0:32, 0:n], L["RS"][0:32, 0:n], prm2, "RS")
            STT("dve", L["T32"][0:32, 0:n], pk[0:32, 0:n], VEC[0:32, VOFF["knr"][0]:VOFF["knr"][0] + 1], L["RS"][0:32, 0:n],
                ALU.mult, ALU.mult, [prk, "RS"], ["T32"])
            pr_, prr = bank()
            MM(pr_[0:32, 0:n], CFm("p32", 32), L["T32"][0:32, 0:n], True, True, ["T32"], [prr])
            TT("dve", L["T32B"][0:32, 0:n], pr_[0:32, 0:n], sink, ALU.mult, [prr, "ROPE"], ["T32B"])
            TT("dve", L["T32"][0:32, 0:n], L["T32"][0:32, 0:n], cosk, ALU.mult, ["T32", "ROPE"], ["T32"])
            TT("dve", KRF32, L["T32"][0:32, 0:n], L["T32B"][0:32, 0:n], ALU.add, ["T32", "T32B"], ["KRF32"])
            CP("act", KRBF, KRF32, ["KRF32"], ["KRBF"])

        def qlat(L, Hh, Hres, c0, n, QL):
            rmsnorm(L, Hh, Hres, c0, n, V("nm1"))
            p0, pr0 = bank()
            p1, pr1 = bank()
            for cc, (pp, prr) in enumerate(((p0, pr0), (p1, pr1))):
                for k in range(KC):
                    MM(pp[:, 0:n], WDQ[:, k * 256 + cc * 128:k * 256 + cc * 128 + 128], L["XN"][:, k, 0:n],
                       k == 0, k == KC - 1, ["XN"], [prr])
            ACT(L["SQ"][:, 0, 0:n], p0[:, 0:n], AF.Square, [pr0], ["SQ"])
            ACT(L["SQ"][:, 1, 0:n], p1[:, 0:n], AF.Square, [pr1], ["SQ"])
            pm, prm = bank()
            MM(pm[:, 0:n], CBm("o256"), L["SQ"][:, 0, 0:n], True, False, ["SQ"], [prm])
            MM(pm[:, 0:n], CBm("o256"), L["SQ"][:, 1, 0:n], False, True, ["SQ"], [prm])
            rstd_from_ms(pm[:, 0:n], L["RS"][:, 0:n], prm, "RS")
            STT("dve", QL[:, 0, :], p0[:, 0:n], V("qln", 0, 1), L["RS"][:, 0:n], ALU.mult, ALU.mult, [pr0, "RS"], ["QL"])
            STT("dve", QL[:, 1, :], p1[:, 0:n], V("qln", 1, 2), L["RS"][:, 0:n], ALU.mult, ALU.mult, [pr1, "RS"], ["QL"])

        def qhead(Q, h, QLv, n, cosq, sinq, QT, QTres, ropres):
            pq, prq = bank()
            for cc in range(2):
                MM(pq[0:96, 0:n], WUQ[:, cc, h * 96:h * 96 + 96], QLv[:, cc, :], cc == 0, cc == 1, ["QL"], [prq])
            ACT(Q["QSQ"][0:96, 0:n], pq[0:96, 0:n], AF.Square, [prq], ["QSQ"])
            pm, prm = bank()
            MM(pm[0:96, 0:n], CBm("blk96", 96), Q["QSQ"][0:96, 0:n], True, True, ["QSQ"], [prm])
            rstd_from_ms(pm[0:96, 0:n], Q["QRS"][0:96, 0:n], prm, "QRS")
            STT("dve", Q["QX"][0:96, 0:n], pq[0:96, 0:n], VEC[0:96, VOFF["g96"][0]:VOFF["g96"][0] + 1], Q["QRS"][0:96, 0:n],
                ALU.mult, ALU.mult, [prq, "QRS"], ["QX"])
            CP("act", Q["QXB"][0:96, 0:n], Q["QX"][0:96, 0:n], ["QX"], ["QXB"])
            pr_, prr = bank()
            MM(pr_[0:96, 0:n], CBm("p96", 96), Q["QXB"][0:96, 0:n], True, True, ["QXB"], [prr])
            TT("dve", Q["QRS"][0:96, 0:n], pr_[0:96, 0:n], sinq, ALU.mult, [prr, ropres, "QRS"], ["QRS"])
            TT("dve", Q["QX"][0:96, 0:n], Q["QX"][0:96, 0:n], cosq, ALU.mult, ["QX", ropres], ["QX"])
            TT("dve", QT, Q["QX"][0:96, 0:n], Q["QRS"][0:96, 0:n], ALU.add, ["QX", "QRS"], [QTres])

        def layer_bufs(st, ncol):
            L = {}
            L["XN"] = sb("L_XN", [128, KC, ncol], BF16, st)
            L["SQ"] = sb("L_SQ", [128, KC, ncol], BF16, st)
            L["RS"] = sb("L_RS", [128, ncol], F32, st)
            L["MEAN"] = sb("L_MEAN", [128, ncol], F32, st)
            L["SG"] = sb("L_SG", [128, ncol], F32, st)
            L["YB2"] = sb("L_YB2", [128, ncol], F32, st)
            L["YB3"] = sb("L_YB3", [128, ncol], F32, st)
            L["YB"] = sb("L_YB", [128, KC, ncol], F32, st)
            L["CG"] = sb("L_CG", [128, ncol], F32, st)
            L["CU"] = sb("L_CU", [128, ncol], F32, st)
            L["AV"] = sb("L_AV", [128, NF, ncol], BF16, st)
            L["T32"] = sb("L_T32", [32, ncol], F32, st)
            L["T32B"] = sb("L_T32B", [32, ncol], F32, st)
            L["TAILB"] = sb("L_TAILB", [128, 44, 2], F32, st)
            L["CMASK"] = sb("L_CMASK", [128, HALO], F32, st)
            DMA("sp", L["CMASK"][:], cmask_d, "c0", (), ["CMASK"])
            return L

        with contextlib.ExitStack() as SP:
            HS = sb("HS", [128, KC, NSC], F32, SP)
            DMA("sp", HS[:], xs_d, "ld0", (), ["HS"])
            QLS = sb("QLS", [128, 2, NSC], BF16, SP)
            CSF = sb("CSF", [128, NSC], F32, SP)
            CSB = sb("CSB", [128, NSC], BF16, SP)
            KRSF = sb("KRSF", [32, NSC], F32, SP)
            KRSB = sb("KRSB", [32, NSC], BF16, SP)
            ATS = sb("ATS", [128, KC, NSC], BF16, SP)
            with contextlib.ExitStack() as SL:
                S.barrier()
                L = layer_bufs(SL, NSC)
                US = sb("US", [128, KC, NS, 34], F32, SL)
                TSTG = sb("TSTG", [128, KC, NS, 30], F32, SL)
                FSTG = sb("FSTG", [128, 44, NS, 2], F32, SL)
                DMA("sp", TSTG[:], sca_d, "ld0", (), ["TSTG"])
                for m in range(KC):
                    CP("act", US[:, m, :, 0:30], TSTG[:, m], ["TSTG"], ["U"])
                conv_module(L, HS, "HS", NSC, NS, 34, US, False, cas_o, TSTG)
                UPG = sb("S_UPG", [128, NF, NS, 6], F32, SL)
                UPU = sb("S_UPU", [128, NF, NS, 6], F32, SL)
                DMA("sp", FSTG[:], sff_d[0], "ld0", (), ["FSTG"])
                CP("act", UPG[:, :, :, 0:2], FSTG[:, 0:NF], ["FSTG"], ["UPG"])
                CP("act", UPU[:, :, :, 0:2], FSTG[:, NF:2 * NF], ["FSTG"], ["UPU"])
                conv_ffn(L, 0, HS, "HS", 0, NSC, NS, UPG, UPU, False, ffs_o[0], FSTG, True)
                ROPS = sb("ROPS", [32, 2, NSC], F32, SL)
                DMA("sp", ROPS[:, 0, :], cosks_d, "ld0", (), ["ROPE"])
                DMA("sp", ROPS[:, 1, :], sinks_d, "ld0", (), ["ROPE"])
                shared_kv(L, HS, "HS", 0, NSC, ROPS[:, 0, :], ROPS[:, 1, :], CSF[:], KRSF[:], CSB[:], KRSB[:])
                DMA("sp", cs_o, CSF[:], "o_cs", ["CF32"], [])
                DMA("sp", krs_o, KRSF[:], "o_cs", ["KRF32"], [])
                qlat(L, HS, "HS", 0, NSC, QLS)

            with contextlib.ExitStack() as SA:
                S.barrier()
                Q = {}
                Q["QSQ"] = sb("Q_SQ", [96, NSC], BF16, SA)
                Q["QRS"] = sb("Q_RS", [96, NSC], F32, SA)
                Q["QX"] = sb("Q_X", [96, NSC], F32, SA)
                Q["QXB"] = sb("Q_XB", [96, NSC], BF16, SA)
                RQ = sb("RQ", [96, 2, NSC], F32, SA)
                DMA("sp", RQ[:, 0, :], cosqs_d, "ld0", (), ["ROPQ"])
                DMA("sp", RQ[:, 1, :], sinqs_d, "ld0", (), ["ROPQ"])
                WUKT = sb("WUKT", [64, 2048], BF16, SA)
                DMA("pool", WUKT[:], wukT_d, "c1", (), ["WUKT"])
                QTS = sb("QTS", [96, NSC], BF16, SA)
                QG = sb("QG", [64, NSC], BF16, SA)
                QABS = sb("QABS", [128, NS, 16, 4], BF16, SA)
                QRT = sb("QRT", [96, NS, 16, 4], BF16, SA)
                QR0 = sb("QR0", [32, NS, 16, 4], BF16, SA)
                o_gk = VOFF["gk64"][0]
                for h in range(16):
                    qhead(Q, h, QLS, NSC, RQ[:, 0, :], RQ[:, 1, :], QTS[:], "QTS", "ROPQ")
                    CP("act", QRT[64:96, :, h, :], QTS[64:96, :].rearrange("p (s q) -> p s q", q=4), ["QTS"], ["QRT"])
                    TSC("dve", QG[:], QTS[0:64, :], VEC[0:64, o_gk:o_gk + 1], None, ALU.mult, None, ["QTS"], ["QG"])
                    pa, pra = bank()
                    MM(pa[:, 0:NSC], WUKT[:, h * 128:h * 128 + 128], QG[:], True, True, ["QG", "WUKT"], [pra])
                    CP("act", QABS[:, :, h, :], pa[:, 0:NSC].rearrange("p (s q) -> p s q", q=4), [pra], ["QABS"])
                DMA("sp", QR0[:], QRT[64:96], "ld1", ["QRT"], ["QR0"])

                CNAT = sb("CNAT", [64, 128], BF16, SA)
                pt_, ptr_ = bank()
                TR(pt_[0:64, 0:128], CSF[:], CFm("identf"), ["CF32"], [ptr_])
                CP("act", CNAT[:], pt_[0:64, 0:128], [ptr_], ["CNAT"])
                RSTN = sb("RSTN", [64, 16], F32, SA)
                SQU = sb("SQU", [128, 1024], BF16, SA)
                KU = [banks[0], banks[1]]
                for hf in range(2):
                    MM(KU[hf][0:64, :], CSB[:], WUK[:, hf * 512:hf * 512 + 512], True, True, ["CBF"], ["B%d" % hf])
                    ACT(SQU[0:64, hf * 512:hf * 512 + 512], KU[hf][0:64, :], AF.Square, ["B%d" % hf], ["SQU"])
                RED(RSTN[:], SQU[0:64, :].rearrange("p (h d) -> p h d", d=64), ALU.add, ["SQU"], ["RSTN"])
                TSC("dve", RSTN[:], RSTN[:], 1.0 / 64, None, ALU.mult, None, ["RSTN"], ["RSTN"])
                ACT(RSTN[:], RSTN[:], AF.Sqrt, ["RSTN"], ["RSTN"], bias=EPS, scale=1.0)
                RECIP(RSTN[:], RSTN[:], ["RSTN"], ["RSTN"])
                MNEW = sb("MNEW", [64, NS, 64], F32, SA)
                DMA("sp", MNEW[:], mnew_d, "ld1", (), ["MNEW"])

                PT = sb("PT", [128, NS], I32, SA)
                DMA("sp", PT[:], pt_d, "ld1", (), ["PT"])
                IDXC = sb("IDXC", [128, NS, 8], I32, SA)
                IDXK = sb("IDXK", [128, NS, 2], I32, SA)
                for c8 in range(8):
                    TSC("dve", IDXC[:, :, c8], PT[:], 8, c8, ALU.mult, ALU.add, ["PT"], ["IDXC"])
                for c2 in range(2):
                    TSC("dve", IDXK[:, :, c2], PT[:], 2, c2, ALU.mult, ALU.add, ["PT"], ["IDXK"])

                GC = [sb("GC%d" % i, [128, 128, 128], BF16, SA) for i in range(2)]
                GK = [sb("GK%d" % i, [128, 128, 32], BF16, SA) for i in range(2)]
                SS_ = sb("SSEQ", [128, 129, 64], F32, SA)
                PP = sb("PSEQ", [128, 129, 64], BF16, SA)
                SSQ2 = [sb("SSQ%d" % i, [128, 4, 16], F32, SA) for i in range(2)]
                SQU2 = [sb("SQU2_%d" % i, [128, 1024], BF16, SA) for i in range(2)]
                CTS = [sb("CTS%d" % i, [128, 512], BF16, SA) for i in range(2)]
                KTS = [sb("KTS%d" % i, [32, 512], BF16, SA) for i in range(2)]
                MX = sb("MX", [128, 64], F32, SA)
                MXC = sb("MXC", [64, 1], F32, SA)
                DG = sb("DG", [64, 64], F32, SA)
                MB = sb("MB", [128, 64], F32, SA)
                RLB = sb("RLB", [128, 64], F32, SA)
                OLT = sb("OLT", [128, NS, 16, 4], BF16, SA)
                MEMSET("dve", SS_[:, 128, :], -2000.0, ["SSEQ"])

                for s in range(NS):
                    g = s % 2
                    for c8 in range(8):
                        S.dma("pool", (lambda e, g=g, c8=c8, s=s: e.indirect_dma_start(
                            out=GC[g][:, c8 * 16:(c8 + 1) * 16, :].rearrange("p a b -> p (a b)"), out_offset=None, in_=cc_d,
                            in_offset=bass.IndirectOffsetOnAxis(ap=IDXC[:, s, c8:c8 + 1], axis=0))),
                            "g%d" % g, ["IDXC"], ["GC%d" % g])
                    for c2 in range(2):
                        S.dma("pool", (lambda e, g=g, c2=c2, s=s: e.indirect_dma_start(
                            out=GK[g][:, c2 * 64:(c2 + 1) * 64, :].rearrange("p a b -> p (a b)"), out_offset=None, in_=ckr_d,
                            in_offset=bass.IndirectOffsetOnAxis(ap=IDXK[:, s, c2:c2 + 1], axis=0))),
                            "g%d" % g, ["IDXK"], ["GK%d" % g])
                    qa = QABS[:, s, :, :].rearrange("p h q -> p (h q)")
                    qr = QR0[:, s, :, :].rearrange("p h q -> p (h q)")
                    def stT(ub):
                        b2 = ub % 2
                        for u4 in range(4):
                            u = ub * 4 + u4
                            TR(bankb[:, u4 * 128:(u4 + 1) * 128], GC[g][:, u, :], CBm("identb"), ["GC%d" % g], ["BB"])
                        for u4 in range(4):
                            u = ub * 4 + u4
                            TR(bankb[0:32, 512 + u4 * 128:512 + (u4 + 1) * 128], GK[g][:, u, :], CBm("identb"), ["GK%d" % g], ["BB"])
                        CP("act", CTS[b2][:], bankb[:, 0:512], ["BB"], ["CTS%d" % b2])
                        CP("dve", KTS[b2][:], bankb[0:32, 512:1024], ["BB"], ["KTS%d" % b2])

                    def stM(ub):
                        b2 = ub % 2
                        nd, ndr = banks[4 + b2], "B%d" % (4 + b2)
                        for u4 in range(4):
                            cT = CTS[b2][:, u4 * 128:(u4 + 1) * 128]
                            kk = 2 * (u4 % 2)
                            sq_, sqr = SQU2[u4 % 2], "SQU%d" % (u4 % 2)
                            for hf in range(2):
                                MM(banks[kk + hf][:, :], cT, WUK[:, hf * 512:hf * 512 + 512], True, True,
                                   ["CTS%d" % b2], ["B%d" % (kk + hf)])
                                ACT(sq_[:, hf * 512:hf * 512 + 512], banks[kk + hf][:, :], AF.Square, ["B%d" % (kk + hf)], [sqr])
                            RED(SSQ2[b2][:, u4, :], sq_[:].rearrange("p (h d) -> p h d", d=64), ALU.add, [sqr], ["SSQ%d" % b2])
                            MM(nd[:, u4 * 64:u4 * 64 + 64], cT, qa, True, True, ["CTS%d" % b2, "QABS"], [ndr], skip_group_check=True)
                            MM(nd[:, 256 + u4 * 64:256 + u4 * 64 + 64], KTS[b2][:, u4 * 128:(u4 + 1) * 128], qr, True, True,
                               ["KTS%d" % b2, "QR0"], [ndr], skip_group_check=True)

                    def stE(ub):
                        b2 = ub % 2
                        nd, ndr = banks[4 + b2], "B%d" % (4 + b2)
                        sq = SSQ2[b2]
                        sr = "SSQ%d" % b2
                        TSC("dve", sq[:], sq[:], 1.0 / 64, None, ALU.mult, None, [sr], [sr])
                        ACT(sq[:], sq[:], AF.Sqrt, [sr], [sr], bias=EPS, scale=1.0)
                        RECIP(sq[:], sq[:], [sr], [sr])
                        sv = SS_[:, ub * 4:ub * 4 + 4, :]
                        TT("dve", sv.rearrange("p u (h q) -> p u h q", q=4),
                           nd[:, 0:256].rearrange("p (u h q) -> p u h q", u=4, q=4),
                           sq[:].unsqueeze(3).to_broadcast([128, 4, 16, 4]), ALU.mult, [ndr, sr], ["SSEQ"])
                        TT("dve", sv, sv, nd[:, 256:512].rearrange("p (u c) -> p u c", u=4), ALU.add, [ndr, "SSEQ"], ["SSEQ"])

                    for ub in range(32):
                        stT(ub)
                        if ub >= 1:
                            stM(ub - 1)
                        if ub >= 2:
                            stE(ub - 2)
                    stM(31)
                    stE(30)
                    stE(31)
                    nd, ndr = bank(4, 6)
                    MM(nd[0:64, 0:64], CSB[:], qa, True, True, ["CBF", "QABS"], [ndr], skip_group_check=True)
                    MM(nd[0:64, 256:320], KRSB[:], qr, True, True, ["KRBF", "QR0"], [ndr], skip_group_check=True)
                    svn = SS_[0:64, 128, :]
                    TT("dve", svn.rearrange("p (h q) -> p h q", q=4), nd[0:64, 0:64].rearrange("p (h q) -> p h q", q=4),
                       RSTN[:].unsqueeze(2).to_broadcast([64, 16, 4]), ALU.mult, [ndr, "RSTN"], ["SSEQ"])
                    TT("dve", svn, svn, nd[0:64, 256:320], ALU.add, [ndr, "SSEQ"], ["SSEQ"])
                    TT("dve", svn, svn, MNEW[:, s, :], ALU.add, ["SSEQ", "MNEW"], ["SSEQ"])
                    RED(MX[:], SS_[:].rearrange("p u c -> p c u"), ALU.max, ["SSEQ"], ["MX"])
                    pm, prm = bank(4, 6)
                    TR(pm[0:64, 0:128], MX[:], CFm("identf"), ["MX"], [prm])
                    RED(MXC[:], pm[0:64, 0:128], ALU.max, [prm], ["MXC"])
                    TSC("dve", DG[:], CFm("identf", 64)[:, 0:64], MXC[:, 0:1], None, ALU.mult, None, ["MXC"], ["DG"])
                    pb, pbr = bank(4, 6)
                    MM(pb[:, 0:64], CFm("onesf", 64), DG[:], True, True, ["DG"], [pbr])
                    CP("dve", MB[:], pb[:, 0:64], [pbr], ["MB"])
                    TT("dve", SS_[:], SS_[:], MB[:].unsqueeze(1).to_broadcast([128, 129, 64]), ALU.subtract, ["SSEQ", "MB"], ["SSEQ"])
                    ACT(PP[:], SS_[:], AF.Exp, ["SSEQ"], ["PSEQ"], scale=float(SM_SCALE))
                    pacc, paccr = banks[6], "B6"
                    plb, plbr = banks[0], "B0"
                    for u in range(128):
                        MM(pacc[:, 0:64], GC[g][:, u, :], PP[:, u, :], u == 0, False, ["GC%d" % g, "PSEQ"], [paccr])
                    MM(pacc[:, 0:64], CNAT[:], PP[0:64, 128, :], False, True, ["CNAT", "PSEQ"], [paccr])
                    for u in range(128):
                        MM(plb[:, 0:64], CBm("onesb"), PP[:, u, :], u == 0, False, ["PSEQ"], [plbr])
                    MM(plb[:, 0:64], CBm("onesb", 64), PP[0:64, 128, :], False, True, ["PSEQ"], [plbr])
                    RECIP(RLB[:], plb[:, 0:64], [plbr], ["RLB"])
                    TT("dve", OLT[:, s, :, :].rearrange("p h q -> p (h q)"), pacc[:, 0:64], RLB[:], ALU.mult, [paccr, "RLB"], ["OLT"])
                for kq in range(KC):
                    po, por = bank()
                    for hh in range(2):
                        h = 2 * kq + hh
                        MM(po[hh * 64:hh * 64 + 64, 0:NSC], WUV[:, h * 64:h * 64 + 64], OLT[:, :, h, :], True, True, ["OLT"], [por])
                    CP("act", ATS[:, kq, :], po[:, 0:NSC], [por], ["ATS"])

            with contextlib.ExitStack() as SL:
                S.barrier()
                L = layer_bufs(SL, NSC)
                for kq in range(KC):
                    wt, wr = wload(wo_d[kq], 1024)
                    for mo in range(KC):
                        po, por = bank()
                        MM(po[:, 0:NSC], wt[:, mo * 128:mo * 128 + 128], ATS[:, kq, :], True, True, [wr, "ATS"], [por])
                        TT("dve", HS[:, mo, :], po[:, 0:NSC], HS[:, mo, :], ALU.add, [por, "HS"], ["HS"])
                UPG = sb("S_UPG", [128, NF, NS, 6], F32, SL)
                UPU = sb("S_UPU", [128, NF, NS, 6], F32, SL)
                FSTG = sb("FSTG", [128, 44, NS, 2], F32, SL)
                DMA("sp", FSTG[:], sff_d[1], "ld0", (), ["FSTG"])
                CP("act", UPG[:, :, :, 0:2], FSTG[:, 0:NF], ["FSTG"], ["UPG"])
                CP("act", UPU[:, :, :, 0:2], FSTG[:, NF:2 * NF], ["FSTG"], ["UPU"])
                conv_ffn(L, 1, HS, "HS", 0, NSC, NS, UPG, UPU, False, ffs_o[1], FSTG, True)
                DMA("sp", ys_o, HS[:], "o_ys", ["HS"], [])

        with contextlib.ExitStack() as PR:
            S.barrier()
            HT = [sb("H%d" % t, [128, KC, TW], F32, PR) for t in range(NT)]
            QLP = sb("QLP", [128, 2, NT, NQ], BF16, PR)
            with contextlib.ExitStack() as PA:
                S.barrier()
                L = layer_bufs(PA, TW)
                UP_ = sb("UP", [128, KC, 1, TW], F32, PA)
                UPG = sb("P_UPG", [128, 2, 1, TW], F32, PA)
                UPU = sb("P_UPU", [128, 2, 1, TW], F32, PA)
                ROPK = sb("ROPK", [32, 2, TS_], F32, PA)
                CPF = sb("CPF", [128, TS_], F32, PA)
                CPB = sb("CPB", [128, TS_], BF16, PA)
                KPF = sb("KPF", [32, TS_], F32, PA)
                KPB = sb("KPB", [32, TS_], BF16, PA)
                TSTGP = sb("TSTGP", [128, KC, 1, 30], F32, PA)
                for t in range(NT):
                    Hh, Hr = HT[t], "H%d" % t
                    DMA("sp", Hh[:], xp_d[t], "ld0", (), [Hr])
                    last = (t == NT - 1)
                    conv_module(L, Hh, Hr, TW, 1, TW, UP_, t == 0, cap_o if last else None, TSTGP)
                    conv_ffn(L, 0, Hh, Hr, 30, TW, 1, UPG, UPU, t == 0, ffp_o[0] if last else None, "o_ffp", False)
                    DMA("sp", ROPK[:, 0, :], cosk_d[:, t, :], "ld0", (), ["ROPE"])
                    DMA("sp", ROPK[:, 1, :], sink_d[:, t, :], "ld0", (), ["ROPE"])
                    shared_kv(L, Hh, Hr, HALO, TS_, ROPK[:, 0, :], ROPK[:, 1, :], CPF[:], KPF[:], CPB[:], KPB[:])
                    DMA("sp", cp_o[t], CPF[:], "o_cp", ["CF32"], [])
                    DMA("sp", krp_o[t], KPF[:], "o_cp", ["KRF32"], [])
                    DMA("sp", xch_in.ap()[0:128, t * TS_:(t + 1) * TS_], CPB[:], "xin", ["CBF"], ["XIN"])
                    DMA("sp", xch_in.ap()[128:160, t * TS_:(t + 1) * TS_], KPB[:], "xin", ["KRBF"], ["XIN"])
                    qlat(L, Hh, Hr, 32, NQ, QLP[:, :, t, :])
            S.op("pool", lambda e: e.collective_compute("AllGather", ALU.bypass, replica_groups=[[0, 1, 2, 3], [4, 5, 6, 7]],
                                                        ins=[xch_in.ap()], outs=[xch_out.ap()]), ["XIN"], ["XOUT"])

            with contextlib.ExitStack() as PB:
                S.barrier()
                NKP = NKC * 128
                CTA = sb("CTA", [128, NKP], BF16, PB)
                KT = sb("KT", [96, NKP], BF16, PB)
                VH = sb("VH", [128, NKC, 65], BF16, PB)
                AOP = sb("AOP", [128, NT, 3, 128], BF16, PB)
                ATT = sb("ATT", [128, NQ], BF16, PB)
                RQP = sb("RQP", [96, 2, NT, NQ], F32, PB)
                THR = sb("THR", [128, NT * NKC], F32, PB)
                PTb = [sb("PTb%d" % i, [128, NQ], BF16, PB) for i in range(4)]
                QT2 = [sb("QT%d" % i, [96, NQ], BF16, PB) for i in range(2)]
                KSQ = [sb("KSQ%d" % i, [64, 512], BF16, PB) for i in range(2)]
                KRS_ = [sb("KRS%d" % i, [64, 512], F32, PB) for i in range(2)]
                RL = sb("RL", [128, 1], F32, PB)
                M01 = sb("M01", [128, NQ], BF16, PB)
                Q = {}
                Q["QSQ"] = sb("QP_SQ", [96, NQ], BF16, PB)
                Q["QRS"] = sb("QP_RS", [96, NQ], F32, PB)
                Q["QX"] = sb("QP_X", [96, NQ], F32, PB)
                Q["QXB"] = sb("QP_XB", [96, NQ], BF16, PB)
                DMA("sp", RQP[:, 0], cosq_d, "ld0", (), ["ROPQ"])
                DMA("sp", RQP[:, 1], sinq_d, "ld0", (), ["ROPQ"])
                DMA("sp", THR[:], thr_d, "ld0", (), ["THR"])
                MEMSET("dve", CTA[:, NPOS:NKP], 0.0, ["CTA"])
                MEMSET("dve", KT[:, NPOS:NKP], 0.0, ["KT"])
                MEMSET("dve", VH[:, :, 64:65], 1.0, ["VH"])
                for gt in range(24):
                    r, t = gt % 4, gt // 4
                    DMA("sp", CTA[:, gt * TS_:(gt + 1) * TS_], xch_out.ap()[r * 160:r * 160 + 128, t * TS_:(t + 1) * TS_],
                        "ld1", ["XOUT"], ["CTA"])
                    DMA("sp", KT[64:96, gt * TS_:(gt + 1) * TS_], xch_out.ap()[r * 160 + 128:r * 160 + 160, t * TS_:(t + 1) * TS_],
                        "ld1", ["XOUT"], ["KT"])
                o_gk = VOFF["gk64"][0]
                qsub = [(0, 128), (128, 128), (256, NQ - 256)]
                pctr = 0
                for h in range(16):
                    def ka(kt):
                        n = 512 if kt < 16 else NKP - 16 * 512
                        cs = kt * 512
                        pk, prk = bank(0, 4)
                        MM(pk[0:64, 0:n], WUK[:, h * 64:h * 64 + 64], CTA[:, cs:cs + n], True, True, ["CTA"], [prk])
                        ACT(KSQ[kt % 2][:, 0:n], pk[0:64, 0:n], AF.Square, [prk], ["KSQ%d" % (kt % 2)])
                        return (kt, n, cs, pk, prk)

                    def kb(kt, n, cs, pk, prk):
                        pm, prm = bank(0, 4)
                        kr_ = KRS_[kt % 2]
                        krr = "KRS%d" % (kt % 2)
                        MM(pm[0:64, 0:n], CBm("o64", 64), KSQ[kt % 2][:, 0:n], True, True, ["KSQ%d" % (kt % 2)], [prm])
                        rstd_from_ms(pm[0:64, 0:n], kr_[:, 0:n], prm, krr)
                        STT("dve", KT[0:64, cs:cs + n], pk[0:64, 0:n], VEC[0:64, o_gk:o_gk + 1], kr_[:, 0:n], ALU.mult, ALU.mult,
                            [prk, krr], ["KT"])

                    kpend = []
                    for kt in range(17):
                        kpend.append(ka(kt))
                        if len(kpend) > 1:
                            kb(*kpend.pop(0))
                    while kpend:
                        kb(*kpend.pop(0))
                    for vb in range(9):
                        nchk = 8 if vb < 8 else 1
                        pv, prv = bank(0, 4)
                        for ci in range(nchk):
                            kc = vb * 8 + ci
                            MM(pv[:, ci * 64:ci * 64 + 64], CTA[:, kc * 128:kc * 128 + 128], WUV[:, h * 64:h * 64 + 64], True, True,
                               ["CTA"], [prv], skip_group_check=True)
                        CP("act", VH[:, vb * 8:vb * 8 + nchk, 0:64], pv[:, 0:nchk * 64].rearrange("p (c d) -> p c d", d=64), [prv], ["VH"])
                    for t in range(NT):
                        QTt, QTr = QT2[t % 2], "QT%d" % (t % 2)
                        qhead(Q, h, QLP[:, :, t, :], NQ, RQP[:, 0, t, :], RQP[:, 1, t, :], QTt[:], QTr, "ROPQ")
                        nkc = min(NKC, -(-((4 * t + 4) * TS_) // 128))
                        first_masked = max(0, (4 * t * TS_ - 2 - 127 + 127) // 128)
                        def s_stage(kc, pctr):
                            kn = 128 if kc < NKC - 1 else NPOS - 128 * (NKC - 1)
                            pst, pstr = bank(0, 3)
                            MM(pst[0:kn, 0:NQ], KT[:, kc * 128:kc * 128 + kn], QTt[:], True, True, ["KT", QTr], [pstr])
                            pb_ = PTb[pctr % 4]
                            pbr = "PTb%d" % (pctr % 4)
                            ACT(pb_[0:kn, :], pst[0:kn, 0:NQ], AF.Exp, [pstr], [pbr], scale=float(SM_SCALE))
                            if kc >= first_masked:
                                col = t * NKC + kc
                                TSC("pool", M01[0:kn, :], CFm("iota")[0:kn, :], THR[0:kn, col:col + 1], None, ALU.is_ge, None, ["THR"], ["M01"])
                                TT("pool", pb_[0:kn, :], pb_[0:kn, :], M01[0:kn, :], ALU.mult, [pbr, "M01"], [pbr])
                            return (kc, kn, pb_, pbr)

                        def pv_stage(kc, kn, pb_, pbr):
                            for qi, (q0, qn) in enumerate(qsub):
                                MM(banks[3 + qi][0:qn, 0:65], pb_[0:kn, q0:q0 + qn], VH[0:kn, kc, :], kc == 0, kc == nkc - 1,
                                   [pbr, "VH"], ["B%d" % (3 + qi)])

                        pend = []
                        for kc in range(nkc):
                            pend.append(s_stage(kc, pctr))
                            pctr += 1
                            if len(pend) > 2:
                                pv_stage(*pend.pop(0))
                        while pend:
                            pv_stage(*pend.pop(0))
                        for qi, (q0, qn) in enumerate(qsub):
                            br = "B%d" % (3 + qi)
                            TSC("dve", RL[0:qn, :], banks[3 + qi][0:qn, 64:65], 1e-30, None, ALU.max, None, [br], ["RL"])
                            RECIP(RL[0:qn, :], RL[0:qn, :], ["RL"], ["RL"])
                            TSC("dve", AOP[0:qn, t, qi, (h % 2) * 64:(h % 2) * 64 + 64], banks[3 + qi][0:qn, 0:64], RL[0:qn, 0:1], None,
                                ALU.mult, None, [br, "RL"], ["AOP"])
                    if h % 2 == 1:
                        kq = h // 2
                        wt, wr = wload(wo_d[kq], 1024)
                        for t in range(NT):
                            for qi, (q0, qn) in enumerate(qsub):
                                TR(bankb[:, q0:q0 + qn], AOP[0:qn, t, qi, :], CBm("identb")[0:qn, 0:qn], ["AOP"], ["BB"])
                            CP("act", ATT[:], bankb[:, 0:NQ], ["BB"], ["ATT"])
                            for mo in range(KC):
                                po, por = bank(0, 3)
                                MM(po[:, 0:NQ], wt[:, mo * 128:mo * 128 + 128], ATT[:], True, True, [wr, "ATT"], [por])
                                TT("dve", HT[t][:, mo, 32:TW], po[:, 0:NQ], HT[t][:, mo, 32:TW], ALU.add, [por, "H%d" % t], ["H%d" % t])

            with contextlib.ExitStack() as PC:
                S.barrier()
                L = layer_bufs(PC, TW)
                UPG = sb("P_UPG", [128, 2, 1, TW], F32, PC)
                UPU = sb("P_UPU", [128, 2, 1, TW], F32, PC)
                for t in range(NT):
                    Hh, Hr = HT[t], "H%d" % t
                    last = (t == NT - 1)
                    conv_ffn(L, 1, Hh, Hr, 32, TW, 1, UPG, UPU, t == 0, ffp_o[1] if last else None, "o_ffp", False)
                    DMA("sp", yp_o[t], Hh[:, :, HALO:TW], "o_yp", [Hr], [])
        S.emit()
    return nc


def kernel(x_prompt, x_sample, state_conv_a, state_ffn_conv, cache_kv_latent, cache_k_rope, page_table,
           meta_tokens, norm_mix, norm_ffn,
           a_w_pw1, a_b_pw1, a_w_dw, a_b_dw, a_ln_g, a_ln_b, a_w_pw2, a_b_pw2,
           ffn_w_up, ffn_w_dw, ffn_w_down,
           kv_norm, mla_w_dkv, mla_lat_norm, mla_w_kr, mla_knorm_rope, mla_w_uk, mla_w_uv, mla_knorm_nope,
           mla_w_dq, mla_q_lat_norm, mla_w_uq, mla_qnorm_nope, mla_qnorm_rope, mla_w_o):
    f32 = np.float32
    A = lambda a: np.asarray(a)
    x_prompt, x_sample = A(x_prompt), A(x_sample)
    n_phys = int(A(cache_kv_latent).shape[0])
    cache_c = np.ascontiguousarray(A(cache_kv_latent), f32).reshape(n_phys * 8, 2048)
    cache_kr = np.ascontiguousarray(A(cache_k_rope), f32).reshape(n_phys * 2, 2048)
    page_table = A(page_table).astype(np.int32)

    vec = np.zeros((128, NVEC), f32)
    def putv(name, m):
        o, w = VOFF[name]
        m = np.asarray(m, f32)
        vec[:m.shape[0], o:o + w] = m.reshape(m.shape[0], w)
    putv("nm0", _fm(A(norm_mix)[0], 8)); putv("nm1", _fm(A(norm_mix)[1], 8))
    putv("nf0", _fm(A(norm_ffn)[0], 8)); putv("nf1", _fm(A(norm_ffn)[1], 8))
    putv("kvn", _fm(A(kv_norm), 8))
    putv("b1", _fm(A(a_b_pw1)[0], 16)); putv("bdw", _fm(A(a_b_dw)[0], 8))
    putv("lng", _fm(A(a_ln_g)[0], 8)); putv("lnb", _fm(A(a_ln_b)[0], 8)); putv("b2", _fm(A(a_b_pw2)[0], 8))
    putv("wdwa", np.ascontiguousarray(A(a_w_dw)[0].reshape(31, 8, 128).transpose(2, 1, 0)).reshape(128, 8 * 31))
    for l in range(2):
        putv("wdwf%d" % l, np.ascontiguousarray(A(ffn_w_dw)[l].reshape(3, 44, 128).transpose(2, 1, 0)).reshape(128, 44 * 3))
    putv("latn", A(mla_lat_norm).reshape(128, 1))
    putv("knr", A(mla_knorm_rope).reshape(32, 1))
    putv("qln", _fm(A(mla_q_lat_norm)[0], 2))
    putv("g96", np.concatenate([A(mla_qnorm_nope)[0], A(mla_qnorm_rope)[0]]).reshape(96, 1))
    putv("gk64", A(mla_knorm_nope).reshape(64, 1))
    cb = _consts_bf()
    cf = _consts_f()

    W1 = A(a_w_pw1)[0].astype(f32)
    w1r = W1.reshape(8, 128, 2, 8, 128)
    w1l = np.ascontiguousarray(w1r.transpose(3, 1, 0, 2, 4)).reshape(8, 128, 2048)
    W2 = A(a_w_pw2)[0].astype(f32).reshape(8, 128, 4, 2, 128)
    w2l = np.ascontiguousarray(W2.transpose(2, 1, 3, 0, 4)).reshape(4, 128, 2048)
    WU = A(ffn_w_up).astype(f32).reshape(2, 8, 128, 2, NF, 128)
    wupl = np.ascontiguousarray(WU.transpose(0, 4, 2, 1, 3, 5)).reshape(2, NF, 128, 2048)
    WD = A(ffn_w_down).astype(f32).reshape(2, 2, 11, 128, 8, 128)
    wdnl = np.ascontiguousarray(WD.transpose(0, 4, 1, 3, 2, 5)).reshape(2, 16, 128, 1408)
    wol = np.ascontiguousarray(A(mla_w_o)[0].astype(f32)).reshape(8, 128, 1024)
    wdq = np.ascontiguousarray(A(mla_w_dq)[0].astype(f32).reshape(8, 128, 256).transpose(1, 0, 2)).reshape(128, 2048)
    wuq = np.ascontiguousarray(A(mla_w_uq)[0].astype(f32).reshape(2, 128, 1536))
    wdkv = np.ascontiguousarray(A(mla_w_dkv).astype(f32).reshape(8, 128, 128).transpose(1, 0, 2)).reshape(128, 1024)
    wkr = np.ascontiguousarray(A(mla_w_kr).astype(f32).reshape(8, 128, 32).transpose(1, 0, 2)).reshape(128, 256)
    wuk = np.ascontiguousarray(A(mla_w_uk).astype(f32).reshape(128, 1024))
    wuv = np.ascontiguousarray(A(mla_w_uv).astype(f32).reshape(128, 1024))
    wukT = np.ascontiguousarray(A(mla_w_uk).astype(f32).transpose(2, 1, 0)).reshape(64, 2048)

    iota = np.arange(TW)
    masknew = np.full((64, NS, 64), -2000.0, f32)
    for s in range(NS):
        for k in range(4):
            for q in range(4):
                if k <= q:
                    masknew[s * 4 + k, s, np.arange(16) * 4 + q] = 0.0
    pos_s = PAST + (np.arange(NSC) % 4)
    cks, sks = _rope_tab(pos_s)
    cosqs = np.ones((96, NSC), f32); sinqs = np.zeros((96, NSC), f32)
    cosqs[64:], sinqs[64:] = cks, sks

    hp_all = np.concatenate([np.broadcast_to(A(meta_tokens).astype(f32)[None], (2, NMETA, D)), x_prompt.astype(f32)], axis=1)
    shared = dict(vec=vec, cb=cb, cf=cf, w1l=w1l, w2l=w2l, wupl=wupl, wdnl=wdnl, wol=wol, wdq=wdq, wuq=wuq, wdkv=wdkv,
                  wkr=wkr, wuk=wuk, wuv=wuv, wukT=wukT, cache_c=cache_c, cache_kr=cache_kr, masknew=masknew,
                  cosqs=cosqs, sinqs=sinqs, cosks=cks, sinks=sks)
    in_maps = []
    for c in range(8):
        b, j = c // 4, c % 4
        xp = np.zeros((NT, 128, KC, TW), f32)
        cosq = np.ones((96, NT, NQ), f32); sinq = np.zeros((96, NT, NQ), f32)
        cosk = np.zeros((32, NT, TS_), f32); sink = np.zeros((32, NT, TS_), f32)
        thr = np.zeros((128, NT * NKC), f32)
        for t in range(NT):
            g = 4 * t + j
            p0 = g * TS_ - HALO
            win = np.zeros((TW, D), f32)
            lo = max(0, p0)
            win[lo - p0:] = hp_all[b, lo:p0 + TW]
            xp[t] = win.T.reshape(KC, 128, TW).transpose(1, 0, 2)
            posq = p0 + 32 + np.arange(NQ)
            cq, sq_ = _rope_tab(posq)
            cosq[64:, t], sinq[64:, t] = cq, sq_
            cosk[:, t], sink[:, t] = cq[:, 2:], sq_[:, 2:]
            thr[:, t * NKC:(t + 1) * NKC] = (128 * np.arange(NKC) - (p0 + 32))[None, :]
        sl = slice(NS * c, NS * c + NS)
        xs = np.ascontiguousarray(x_sample[sl].astype(f32).reshape(NSC, D).T.reshape(KC, 128, NSC).transpose(1, 0, 2))
        sca = np.ascontiguousarray(A(state_conv_a)[0, sl].astype(f32).transpose(2, 0, 1).reshape(KC, 128, NS, 30).transpose(1, 0, 2, 3))
        sff = np.ascontiguousarray(A(state_ffn_conv)[:, sl].astype(f32).transpose(0, 3, 1, 2).reshape(2, 44, 128, NS, 2).transpose(0, 2, 1, 3, 4))
        cmask = np.ones((128, HALO), f32)
        if j == 0:
            cmask[:] = 0.0
        m = dict(shared)
        m.update(xp=xp, xs=xs, sca=sca, sff=sff, ptT=np.ascontiguousarray(page_table[sl].T), colmask=cmask,
                 cosq=cosq, sinq=sinq, cosk=cosk, sink=sink, thr=thr)
        in_maps.append(m)

    nc = build_program(n_phys)
    res = run_bass_kernel_spmd(nc, in_maps, core_ids=list(range(8)))
    R = res.results

    y_prompt = np.zeros((2, NPOS, D), f32)
    kvp = np.zeros((2, NPOS, 128), f32)
    krp = np.zeros((2, NPOS, 32), f32)
    y_sample = np.zeros((128, 4, D), f32)
    cas = np.zeros((1, 128, 30, D), f32)
    ffs = np.zeros((2, 128, 2, 2 * DFF), f32)
    kvs = np.zeros((128, 4, 128), f32)
    krs = np.zeros((128, 4, 32), f32)
    cap = np.zeros((1, 2, 30, D), f32)
    ffp = np.zeros((2, 2, 2, 2 * DFF), f32)
    for c in range(8):
        b, j = c // 4, c % 4
        r = R[c]
        for t in range(NT):
            g = 4 * t + j
            y_prompt[b, g * TS_:(g + 1) * TS_] = np.asarray(r["yp"][t]).transpose(1, 0, 2).reshape(D, TS_).T
            kvp[b, g * TS_:(g + 1) * TS_] = np.asarray(r["cp"][t]).T
            krp[b, g * TS_:(g + 1) * TS_] = np.asarray(r["krp"][t]).T
        sl = slice(NS * c, NS * c + NS)
        y_sample[sl] = np.asarray(r["ys"]).transpose(1, 0, 2).reshape(D, NS, 4).transpose(1, 2, 0)
        cas[0, sl] = np.asarray(r["cas"]).transpose(1, 0, 2, 3).reshape(D, NS, 30).transpose(1, 2, 0)
        ffs[:, sl] = np.asarray(r["ffs"]).transpose(0, 2, 1, 3, 4).reshape(2, 2 * DFF, NS, 2).transpose(0, 2, 3, 1)
        kvs[sl] = np.asarray(r["cs"]).T.reshape(NS, 4, 128)
        krs[sl] = np.asarray(r["krs"]).T.reshape(NS, 4, 32)
        if j == 3:
            cap[0, b] = np.asarray(r["cap"]).transpose(1, 0, 2).reshape(D, 30).T
            ffp[:, b] = np.asarray(r["ffp"]).transpose(0, 2, 1, 3).reshape(2, 2 * DFF, 2).transpose(0, 2, 1)
    return (np.ascontiguousarray(y_prompt[:, NMETA:]), y_sample, cap, cas, ffp, ffs, kvp, krp, kvs, krs)
```

```python
import contextlib
import numpy as np
import concourse.bass as bass
import concourse.mybir as mybir
from concourse.bass_utils import run_bass_kernel_spmd

F32 = mybir.dt.float32
BF16 = mybir.dt.bfloat16
I32 = mybir.dt.int32
ALU = mybir.AluOpType
AF = mybir.ActivationFunctionType
AX = mybir.AxisListType

D = 1024
KC = 8
SEQ = 8192
NMETA = 16
NPOS = SEQ + NMETA
TS_ = 342
HALO = 34
TW = TS_ + HALO
NT = 6
NQ = TW - 32
NS = 16
NSC = 64
DFF = 2816
NF = 22
NPAGE = 128
PAST = 16384
EPS = 1e-6
SM_SCALE = 1.0 / np.sqrt(96.0)
NKC = 65
EPOCH = 12000
SAME_ENGINE_SYNC = True


class Sched:
    ENG = ("pe", "act", "dve", "pool", "sp")

    def __init__(self, nc):
        self.nc = nc
        self.ops = {e: [] for e in self.ENG}
        self.count = {e: 0 for e in self.ENG}
        self.waited = {e: {} for e in self.ENG}
        self.last_write = {}
        self.readers = {}
        self.lane_count = {}
        self.sems = {}
        self.all_sem_keys = []

    def _need(self, eng, tok):
        key, val = tok
        if key[0] == eng and (not SAME_ENGINE_SYNC or eng == "pe"):
            return
        w = self.waited[eng]
        if key[0] in self.ENG:
            cur = w.get(key[0], (-1, 0))
            if (key[1], val) <= cur:
                return
            w[key[0]] = (key[1], val)
        else:
            if w.get(key, 0) >= val:
                return
            w[key] = val
        self.ops[eng].append(("wait", key, val))

    def _deps(self, eng, reads, writes):
        toks = []
        for r in reads:
            t = self.last_write.get(r)
            if t:
                toks.append(t)
            if r[0] == "B" and (r[1:].isdigit() or r == "BB"):
                toks.extend(x for x in self.readers.get(r, ()) if x[0][0] != eng)
        for r in writes:
            t = self.last_write.get(r)
            if t:
                toks.append(t)
            toks.extend(self.readers.get(r, ()))
        for t in toks:
            self._need(eng, t)

    def _commit(self, tok, reads, writes):
        for r in writes:
            self.last_write[r] = tok
            self.readers[r] = []
        for r in reads:
            if r in writes:
                continue
            self.readers.setdefault(r, []).append(tok)

    def op(self, eng, fn, reads=(), writes=()):
        self._deps(eng, reads, writes)
        self.count[eng] += 1
        idx = self.count[eng]
        key = (eng, (idx - 1) // EPOCH)
        val = (idx - 1) % EPOCH + 1
        if key not in self.sems:
            self.sems[key] = None
            self.all_sem_keys.append(key)
        self.ops[eng].append(("op", fn, key, 1))
        self._commit((key, val), reads, writes)

    def dma(self, q, fn, lane, reads=(), writes=()):
        self._deps(q, reads, writes)
        self.rr = getattr(self, "rr", {})
        self.rr[q] = self.rr.get(q, 0) + 1
        lane = "%s%d" % (q, self.rr[q] % 10)
        key = ("lane", lane)
        if self.lane_count.get(lane, 0) > 0:
            self._need(q, (key, self.lane_count[lane]))
        if key not in self.sems:
            self.sems[key] = None
            self.all_sem_keys.append(key)
        self.lane_count[lane] = self.lane_count.get(lane, 0) + 16
        self.ops[q].append(("op", fn, key, 16))
        self._commit((key, self.lane_count[lane]), reads, writes)

    def barrier(self):
        toks = []
        for e in self.ENG:
            idx = self.count[e]
            if idx > 0:
                toks.append(((e, (idx - 1) // EPOCH), (idx - 1) % EPOCH + 1))
        for lane, cnt in self.lane_count.items():
            toks.append((("lane", lane), cnt))
        for e in self.ENG:
            for t in toks:
                self._need(e, t)

    def all_wait(self, res):
        t = self.last_write.get(res)
        if t:
            for e in self.ENG:
                self._need(e, t)

    def emit(self):
        nc = self.nc
        with contextlib.ExitStack() as st:
            for i, key in enumerate(self.all_sem_keys):
                self.sems[key] = st.enter_context(nc.semaphore("s%d" % i))
            for key in self.all_sem_keys:
                if key[0] == "lane":
                    self._need("sp", (key, self.lane_count[key[1]]))
            blk = st.enter_context(nc.Block())

            def run(engname):
                def body(e):
                    for item in self.ops[engname]:
                        if item[0] == "wait":
                            e.wait_ge(self.sems[item[1]], item[2])
                        else:
                            item[1](e).then_inc(self.sems[item[2]], item[3])
                return body

            blk.tensor(run("pe"))
            blk.scalar(run("act"))
            blk.vector(run("dve"))
            blk.gpsimd(run("pool"))
            blk.sync(run("sp"))


VEC_LAYOUT = [("nm0", 8), ("nm1", 8), ("nf0", 8), ("nf1", 8), ("kvn", 8), ("b1", 16), ("bdw", 8), ("lng", 8),
              ("lnb", 8), ("b2", 8), ("wdwa", 8 * 31), ("wdwf0", 44 * 3), ("wdwf1", 44 * 3), ("latn", 1),
              ("knr", 1), ("qln", 2), ("g96", 1), ("gk64", 1)]
VOFF = {}
_o = 0
for _n, _w in VEC_LAYOUT:
    VOFF[_n] = (_o, _w)
    _o += _w
NVEC = _o

CB_LAYOUT = [("o1024", 128), ("o256", 128), ("o128", 128), ("blk96", 96), ("o64", 64), ("p96", 96), ("identb", 128),
             ("onesb", 128)]
CBOFF = {}
_o = 0
for _n, _w in CB_LAYOUT:
    CBOFF[_n] = (_o, _w)
    _o += _w
NCB = _o


def _fm(v, nchunk):
    return np.ascontiguousarray(np.asarray(v, np.float32).reshape(nchunk, 128).T)


def _consts_bf():
    c = np.zeros((128, NCB), np.float32)
    def put(name, m):
        o, w = CBOFF[name]
        c[:m.shape[0], o:o + m.shape[1]] = m
    put("o1024", np.full((128, 128), 1.0 / 1024, np.float32))
    put("o256", np.full((128, 128), 1.0 / 256, np.float32))
    put("o128", np.full((128, 128), 1.0 / 128, np.float32))
    blk = np.zeros((96, 96), np.float32)
    blk[:64, :64] = 1.0 / 64
    blk[64:, 64:] = 1.0 / 32
    put("blk96", blk)
    put("o64", np.full((64, 64), 1.0 / 64, np.float32))
    p96 = np.zeros((96, 96), np.float32)
    for m in range(16):
        p96[64 + m + 16, 64 + m] = -1.0
        p96[64 + m, 64 + m + 16] = 1.0
    put("p96", p96)
    put("identb", np.eye(128, dtype=np.float32))
    put("onesb", np.ones((128, 128), np.float32))
    return c


CF_LAYOUT = [("identf", 128), ("p32", 32), ("o32", 32), ("onesf", 128), ("iota", NQ)]
CFOFF = {}
_o = 0
for _n, _w in CF_LAYOUT:
    CFOFF[_n] = (_o, _w)
    _o += _w
NCF = _o


def _consts_f():
    c = np.zeros((128, NCF), np.float32)
    def put(name, m):
        o, w = CFOFF[name]
        c[:m.shape[0], o:o + m.shape[1]] = m
    put("identf", np.eye(128, dtype=np.float32))
    p32 = np.zeros((32, 32), np.float32)
    for m in range(16):
        p32[m + 16, m] = -1.0
        p32[m, m + 16] = 1.0
    put("p32", p32)
    put("o32", np.full((32, 32), 1.0 / 32, np.float32))
    put("onesf", np.ones((128, 128), np.float32))
    put("iota", (np.arange(NQ)[None, :] - np.arange(128)[:, None]).astype(np.float32))
    return c


def _rope_tab(pos):
    inv = (10000.0 ** (-np.arange(16, dtype=np.float32) / 16)).astype(np.float32)
    ang = pos.astype(np.float32)[None, :] * inv[:, None]
    cos = np.cos(ang).astype(np.float32)
    sin = np.sin(ang).astype(np.float32)
    return np.concatenate([cos, cos], 0), np.concatenate([sin, sin], 0)


def build_program(n_phys):
    nc = bass.Bass("TRN2", target_bir_lowering=False)
    S = Sched(nc)

    def din(name, shape, dt=F32):
        return nc.dram_tensor(name, list(shape), dt, kind="ExternalInput").ap()

    def dout(name, shape, dt=F32):
        return nc.dram_tensor(name, list(shape), dt, kind="ExternalOutput").ap()

    xp_d = din("xp", [NT, 128, KC, TW])
    xs_d = din("xs", [128, KC, NSC])
    sca_d = din("sca", [128, KC, NS, 30])
    sff_d = din("sff", [2, 128, 44, NS, 2])
    cc_d = din("cache_c", [n_phys * 8, 2048])
    ckr_d = din("cache_kr", [n_phys * 2, 2048])
    pt_d = din("ptT", [128, NS], I32)
    cmask_d = din("colmask", [128, HALO])
    cosq_d = din("cosq", [96, NT, NQ])
    sinq_d = din("sinq", [96, NT, NQ])
    cosk_d = din("cosk", [32, NT, TS_])
    sink_d = din("sink", [32, NT, TS_])
    cosqs_d = din("cosqs", [96, NSC])
    sinqs_d = din("sinqs", [96, NSC])
    cosks_d = din("cosks", [32, NSC])
    sinks_d = din("sinks", [32, NSC])
    thr_d = din("thr", [128, NT * NKC])
    mnew_d = din("masknew", [64, NS, 64])
    vec_d = din("vec", [128, NVEC])
    cb_d = din("cb", [128, NCB])
    cf_d = din("cf", [128, NCF])
    w1_d = din("w1l", [8, 128, 2048])
    w2_d = din("w2l", [4, 128, 2048])
    wup_d = din("wupl", [2, NF, 128, 2048])
    wdn_d = din("wdnl", [2, 16, 128, 1408])
    wo_d = din("wol", [8, 128, 1024])
    wdq_d = din("wdq", [128, 2048])
    wuq_d = din("wuq", [2, 128, 1536])
    wdkv_d = din("wdkv", [128, 1024])
    wkr_d = din("wkr", [128, 256])
    wuk_d = din("wuk", [128, 1024])
    wuv_d = din("wuv", [128, 1024])
    wukT_d = din("wukT", [64, 2048])

    yp_o = dout("yp", [NT, 128, KC, TS_])
    ys_o = dout("ys", [128, KC, NSC])
    cap_o = dout("cap", [128, KC, 30])
    cas_o = dout("cas", [128, KC, NS, 30])
    ffp_o = dout("ffp", [2, 128, 44, 2])
    ffs_o = dout("ffs", [2, 128, 44, NS, 2])
    cp_o = dout("cp", [NT, 128, TS_])
    krp_o = dout("krp", [NT, 32, TS_])
    cs_o = dout("cs", [128, NSC])
    krs_o = dout("krs", [32, NSC])

    xch_in = nc.dram_tensor("xch_in", [160, NT * TS_], BF16)
    xch_out = nc.dram_tensor("xch_out", [640, NT * TS_], BF16)

    def MM(out, lhsT, rhs, start, stop, R, W, **kw):
        S.op("pe", lambda e: e.matmul(out, lhsT=lhsT, rhs=rhs, start=start, stop=stop, **kw), R, W)

    def TR(out, in_, ident, R, W):
        S.op("pe", lambda e: e.transpose(out=out, in_=in_, identity=ident), R, W)

    def ACT(out, in_, func, R, W, bias=None, scale=None):
        kw = {}
        if bias is not None:
            kw["bias"] = bias
        if scale is not None:
            kw["scale"] = scale
        S.op("act", lambda e: e.activation(out=out, in_=in_, func=func, **kw), R, W)

    def TSC(eng, out, in0, s1, s2, op0, op1, R, W):
        if op1 is None:
            S.op(eng, lambda e: e.tensor_scalar(out=out, in0=in0, scalar1=s1, scalar2=None, op0=op0), R, W)
        else:
            S.op(eng, lambda e: e.tensor_scalar(out=out, in0=in0, scalar1=s1, scalar2=s2, op0=op0, op1=op1), R, W)

    def STT(eng, out, in0, scalar, in1, op0, op1, R, W):
        S.op(eng, lambda e: e.scalar_tensor_tensor(out=out, in0=in0, scalar=scalar, in1=in1, op0=op0, op1=op1), R, W)

    def TT(eng, out, in0, in1, op, R, W):
        S.op(eng, lambda e: e.tensor_tensor(out=out, in0=in0, in1=in1, op=op), R, W)

    def CP(eng, out, in_, R, W):
        if eng == "act":
            S.op("act", lambda e: e.activation(out=out, in_=in_, func=AF.Copy), R, W)
        else:
            S.op(eng, lambda e: e.tensor_copy(out=out, in_=in_), R, W)

    def RECIP(out, in_, R, W):
        S.op("dve", lambda e: e.reciprocal(out=out, in_=in_), R, W)

    def MEMSET(eng, ap, val, W):
        S.op(eng, lambda e: e.memset(ap, val), (), W)

    def RED(out, in_, op, R, W):
        S.op("dve", lambda e: e.tensor_reduce(out=out, in_=in_, axis=AX.X, op=op), R, W)

    def DMA(q, out, in_, lane, R, W):
        S.dma(q, lambda e: e.dma_start(out=out, in_=in_), lane, R, W)

    with contextlib.ExitStack() as G:
        nctr = [0]

        def sb(name, shape, dt, st=G):
            nctr[0] += 1
            return st.enter_context(nc.sbuf_tensor("%s_%d" % (name, nctr[0]), list(shape), dt))

        banks = [G.enter_context(nc.psum_tensor("B%d" % i, [128, 512], F32)) for i in range(7)]
        bankb = G.enter_context(nc.psum_tensor("BB", [128, 1024], BF16))
        bctr = [0]

        def bank(lo=0, hi=7):
            i = lo + bctr[0] % (hi - lo)
            bctr[0] += 1
            return banks[i], "B%d" % i

        VEC = sb("VEC", [128, NVEC], F32)
        CB = sb("CB", [128, NCB], BF16)
        CF = sb("CF", [128, NCF], F32)
        WDQ = sb("WDQ", [128, 2048], BF16)
        WUQ = sb("WUQ", [128, 2, 1536], BF16)
        WDKV = sb("WDKV", [128, 1024], BF16)
        WKR = sb("WKR", [128, 256], BF16)
        WUK = sb("WUK", [128, 1024], BF16)
        WUV = sb("WUV", [128, 1024], BF16)
        DMA("sp", VEC[:], vec_d, "c0", (), ["VEC"])
        DMA("sp", CF[:], cf_d, "c0", (), ["CF"])
        DMA("pool", CB[:], cb_d, "c1", (), ["CB"])
        DMA("pool", WDQ[:], wdq_d, "c1", (), ["WDQ"])
        DMA("pool", WUQ[:, 0, :], wuq_d[0], "c1", (), ["WUQ"])
        DMA("pool", WUQ[:, 1, :], wuq_d[1], "c1", (), ["WUQ"])
        DMA("pool", WDKV[:], wdkv_d, "c1", (), ["WDKV"])
        DMA("pool", WKR[:], wkr_d, "c1", (), ["WKR"])
        DMA("pool", WUK[:], wuk_d, "c1", (), ["WUK"])
        DMA("pool", WUV[:], wuv_d, "c1", (), ["WUV"])
        for r in ("VEC", "CF", "CB", "WDQ", "WUQ", "WDKV", "WKR", "WUK", "WUV"):
            S.all_wait(r)

        def V(name, a=None, b=None):
            o, w = VOFF[name]
            if a is None:
                return VEC[:, o:o + w]
            return VEC[:, o + a:o + b]

        def CBm(name, rows=128):
            o, w = CBOFF[name]
            return CB[0:rows, o:o + w]

        def CFm(name, rows=128):
            o, w = CFOFF[name]
            return CF[0:rows, o:o + w]

        NSLOT = 4
        WS = [sb("WS%d" % i, [128, 2048], BF16) for i in range(NSLOT)]
        wctr = [0]

        def wload(src, ne):
            i = wctr[0] % NSLOT
            wctr[0] += 1
            DMA("pool", WS[i][:, 0:ne], src, "w%d" % i, (), ["WS%d" % i])
            return WS[i], "WS%d" % i

        def rstd_from_ms(ps_ap, rs_ap, Rps, Wrs):
            ACT(rs_ap, ps_ap, AF.Sqrt, [Rps], [Wrs], bias=EPS, scale=1.0)
            RECIP(rs_ap, rs_ap, [Wrs], [Wrs])

        def rmsnorm(L, Hh, Hres, c0, n, gain):
            ACT(L["SQ"][:, :, 0:n], Hh[:, :, c0:c0 + n], AF.Square, [Hres], ["SQ"])
            ps, pr = bank()
            for k in range(KC):
                MM(ps[:, 0:n], CBm("o1024"), L["SQ"][:, k, 0:n], k == 0, k == KC - 1, ["SQ"], [pr])
            rstd_from_ms(ps[:, 0:n], L["RS"][:, 0:n], pr, "RS")
            for k in range(KC):
                STT("dve", L["XN"][:, k, 0:n], Hh[:, k, c0:c0 + n], gain[:, k:k + 1], L["RS"][:, 0:n],
                    ALU.mult, ALU.mult, [Hres, "RS"], ["XN"])

        def conv_module(L, Hh, Hres, N, Sq, Lw, U, mask_first, tail_out, tail_lane):
            Lo = Lw - 30
            n = Sq * Lo
            newc = N // Sq
            rmsnorm(L, Hh, Hres, 0, N, V("nm0"))
            for m in range(KC):
                wt, wr = wload(w1_d[m], 2048)
                pa, pra = bank()
                pg, prg = bank()
                for k in range(KC):
                    MM(pa[:, 0:N], wt[:, k * 256:k * 256 + 128], L["XN"][:, k, 0:N], k == 0, k == KC - 1, [wr, "XN"], [pra])
                for k in range(KC):
                    MM(pg[:, 0:N], wt[:, k * 256 + 128:k * 256 + 256], L["XN"][:, k, 0:N], k == 0, k == KC - 1, [wr, "XN"], [prg])
                ACT(L["SG"][:, 0:N], pg[:, 0:N], AF.Sigmoid, [prg], ["SG"], bias=V("b1", 8 + m, 9 + m))
                uo = U[:, m, :, Lw - newc:Lw]
                STT("dve", uo, pa[:, 0:N].rearrange("p (s l) -> p s l", s=Sq), V("b1", m, m + 1),
                    L["SG"][:, 0:N].rearrange("p (s l) -> p s l", s=Sq), ALU.add, ALU.mult, [pra, "SG"], ["U"])
                if mask_first:
                    TT("dve", U[:, m, 0, 0:HALO], U[:, m, 0, 0:HALO], L["CMASK"][:, 0:HALO], ALU.mult, ["U", "CMASK"], ["U"])
                yb = L["YB"][:, m, 0:n].rearrange("p (s l) -> p s l", s=Sq)
                o, _ = VOFF["wdwa"]
                TSC("dve", yb, U[:, m, :, 0:Lo], VEC[:, o + m * 31:o + m * 31 + 1], V("bdw", m, m + 1), ALU.mult, ALU.add,
                    ["U"], ["YB"])
                yb2 = L["YB2"][:, 0:n].rearrange("p (s l) -> p s l", s=Sq)
                TSC("dve", yb2, U[:, m, :, 1:1 + Lo], VEC[:, o + m * 31 + 1:o + m * 31 + 2], None, ALU.mult, None, ["U"], ["YB2"])
                for kk in range(2, 31):
                    if kk % 2 == 0:
                        STT("dve", yb, U[:, m, :, kk:kk + Lo], VEC[:, o + m * 31 + kk:o + m * 31 + kk + 1], yb,
                            ALU.mult, ALU.add, ["U", "YB"], ["YB"])
                    else:
                        STT("dve", yb2, U[:, m, :, kk:kk + Lo], VEC[:, o + m * 31 + kk:o + m * 31 + kk + 1], yb2,
                            ALU.mult, ALU.add, ["U", "YB2"], ["YB2"])
                TT("dve", yb, yb, yb2, ALU.add, ["YB", "YB2"], ["YB"])
            if tail_out is not None:
                for m in range(KC):
                    CP("act", tail_lane[:, m], U[:, m, :, Lw - 30:Lw], ["U"], ["TSTG"])
                DMA("sp", tail_out, tail_lane[:] if Sq > 1 else tail_lane[:, :, 0, :], "x", ["TSTG"], [])
            ACT(L["SQ"][:, :, 0:n], L["YB"][:, :, 0:n], AF.Square, ["YB"], ["SQ"])
            CP("act", L["XN"][:, :, 0:n], L["YB"][:, :, 0:n], ["YB"], ["XN"])
            pm, prm = bank()
            pq, prq = bank()
            for k in range(KC):
                MM(pm[:, 0:n], CBm("o1024"), L["XN"][:, k, 0:n], k == 0, k == KC - 1, ["XN"], [prm])
            for k in range(KC):
                MM(pq[:, 0:n], CBm("o1024"), L["SQ"][:, k, 0:n], k == 0, k == KC - 1, ["SQ"], [prq])
            CP("dve", L["MEAN"][:, 0:n], pm[:, 0:n], [prm], ["MEAN"])
            TT("dve", L["RS"][:, 0:n], L["MEAN"][:, 0:n], L["MEAN"][:, 0:n], ALU.mult, ["MEAN"], ["RS"])
            TT("dve", L["RS"][:, 0:n], pq[:, 0:n], L["RS"][:, 0:n], ALU.subtract, [prq, "RS"], ["RS"])
            TSC("dve", L["RS"][:, 0:n], L["RS"][:, 0:n], 0.0, None, ALU.max, None, ["RS"], ["RS"])
            ACT(L["RS"][:, 0:n], L["RS"][:, 0:n], AF.Sqrt, ["RS"], ["RS"], bias=EPS, scale=1.0)
            RECIP(L["RS"][:, 0:n], L["RS"][:, 0:n], ["RS"], ["RS"])
            for k in range(KC):
                TT("dve", L["YB"][:, k, 0:n], L["YB"][:, k, 0:n], L["MEAN"][:, 0:n], ALU.subtract, ["YB", "MEAN"], ["YB"])
                TT("dve", L["YB"][:, k, 0:n], L["YB"][:, k, 0:n], L["RS"][:, 0:n], ALU.mult, ["YB", "RS"], ["YB"])
                ACT(L["XN"][:, k, 0:n], L["YB"][:, k, 0:n], AF.Silu, ["YB"], ["XN"], bias=V("lnb", k, k + 1), scale=V("lng", k, k + 1))
            c0 = N - n
            for i in range(4):
                wt, wr = wload(w2_d[i], 2048)
                for mo2 in range(2):
                    mo = 2 * i + mo2
                    po, pro = bank()
                    for k in range(KC):
                        MM(po[:, 0:n], wt[:, mo2 * 1024 + k * 128:mo2 * 1024 + k * 128 + 128], L["XN"][:, k, 0:n],
                           k == 0, k == KC - 1, [wr, "XN"], [pro])
                    STT("dve", Hh[:, mo, c0:N], po[:, 0:n], V("b2", mo, mo + 1), Hh[:, mo, c0:N], ALU.add, ALU.add,
                        [pro, Hres], [Hres])

        def conv_ffn(L, l, Hh, Hres, c1, N, Sq, UPG, UPU, mask_first, tail_out, tail_lane, pre_loaded):
            n1 = N - c1
            if pre_loaded:
                Lq = n1 // Sq
                nout = n1
            else:
                Lq = n1 - 2
                nout = Lq
            rmsnorm(L, Hh, Hres, c1, n1, V("nf%d" % l))
            wo_, _ = VOFF["wdwf%d" % l]
            for f in range(NF):
                wt, wr = wload(wup_d[l, f], 2048)
                pg, prg = bank()
                pu, pru = bank()
                for k in range(KC):
                    MM(pg[:, 0:n1], wt[:, k * 256:k * 256 + 128], L["XN"][:, k, 0:n1], k == 0, k == KC - 1, [wr, "XN"], [prg])
                for k in range(KC):
                    MM(pu[:, 0:n1], wt[:, k * 256 + 128:k * 256 + 256], L["XN"][:, k, 0:n1], k == 0, k == KC - 1, [wr, "XN"], [pru])
                chains = []
                for (ps_, pr_, UP, ci, cres0, cv) in ((pg, prg, UPG, f, "UPG", "CG"), (pu, pru, UPU, NF + f, "UPU", "CU")):
                    fx = f if pre_loaded else f % 2
                    cres = cres0 if pre_loaded else "%s%d" % (cres0, fx)
                    if pre_loaded:
                        CP("act", UP[:, fx, :, 2:2 + Lq], ps_[:, 0:n1].rearrange("p (s l) -> p s l", s=Sq), [pr_], [cres])
                    else:
                        CP("act", UP[:, fx, 0, 0:n1], ps_[:, 0:n1], [pr_], [cres])
                        if mask_first:
                            TT("dve", UP[:, fx, 0, 0:HALO - c1], UP[:, fx, 0, 0:HALO - c1], L["CMASK"][:, c1:HALO], ALU.mult, [cres, "CMASK"], [cres])
                        if tail_out is not None:
                            CP("dve", L["TAILB"][:, ci, :], UP[:, fx, 0, Lq:Lq + 2], [cres], ["TAILB"])
                    cvv = L[cv][:, 0:nout].rearrange("p (s l) -> p s l", s=Sq)
                    chains.append((UP, fx, cres, cv, cvv, wo_ + ci * 3))
                for tap in range(3):
                    for (UP, fx, cres, cv, cvv, wb) in chains:
                        if tap == 0:
                            TSC("dve", cvv, UP[:, fx, :, 0:Lq], VEC[:, wb:wb + 1], None, ALU.mult, None, [cres], [cv])
                        else:
                            STT("dve", cvv, UP[:, fx, :, tap:tap + Lq], VEC[:, wb + tap:wb + tap + 1], cvv, ALU.mult, ALU.add, [cres, cv], [cv])
                ACT(L["CG"][:, 0:nout], L["CG"][:, 0:nout], AF.Silu, ["CG"], ["CG"])
                TT("dve", L["AV"][:, f, 0:nout], L["CG"][:, 0:nout], L["CU"][:, 0:nout], ALU.mult, ["CG", "CU"], ["AV"])
            if tail_out is not None:
                if pre_loaded:
                    for (UP, cres, half) in ((UPG, "UPG", 0), (UPU, "UPU", 1)):
                        CP("act", tail_lane[:, half * NF:(half + 1) * NF], UP[:, :, :, Lq:Lq + 2], [cres], ["FSTG"])
                    DMA("sp", tail_out, tail_lane[:], "x", ["FSTG"], [])
                else:
                    DMA("sp", tail_out, L["TAILB"][:], tail_lane, ["TAILB"], [])
            co = N - nout
            for mo in range(KC):
                po, pro = bank()
                for half in range(2):
                    wt, wr = wload(wdn_d[l, mo * 2 + half], 1408)
                    for fi in range(11):
                        f = half * 11 + fi
                        MM(po[:, 0:nout], wt[:, fi * 128:fi * 128 + 128], L["AV"][:, f, 0:nout], f == 0, f == NF - 1,
                           [wr, "AV"], [pro])
                TT("dve", Hh[:, mo, co:N], po[:, 0:nout], Hh[:, mo, co:N], ALU.add, [pro, Hres], [Hres])

        def shared_kv(L, Hh, Hres, c0, n, cosk, sink, CF32, KRF32, CBF, KRBF):
            rmsnorm(L, Hh, Hres, c0, n, V("kvn"))
            pc, prc = bank()
            pk, prk = bank()
            for k in range(KC):
                MM(pc[:, 0:n], WDKV[:, k * 128:k * 128 + 128], L["XN"][:, k, 0:n], k == 0, k == KC - 1, ["XN"], [prc])
            for k in range(KC):
                MM(pk[0:32, 0:n], WKR[:, k * 32:k * 32 + 32], L["XN"][:, k, 0:n], k == 0, k == KC - 1, ["XN"], [prk])
            ACT(L["SQ"][:, 0, 0:n], pc[:, 0:n], AF.Square, [prc], ["SQ"])
            pm, prm = bank()
            MM(pm[:, 0:n], CBm("o128"), L["SQ"][:, 0, 0:n], True, True, ["SQ"], [prm])
            rstd_from_ms(pm[:, 0:n], L["RS"][:, 0:n], prm, "RS")
            STT("dve", CF32, pc[:, 0:n], V("latn"), L["RS"][:, 0:n], ALU.mult, ALU.mult, [prc, "RS"], ["CF32"])
            CP("act", CBF, CF32, ["CF32"], ["CBF"])
            ACT(L["T32"][0:32, 0:n], pk[0:32, 0:n], AF.Square, [prk], ["T32"])
            pm2, prm2 = bank()
            MM(pm2[0:32, 0:n], CFm("o32", 32), L["T32"][0:32, 0:n], True, True, ["T32"], [prm2])
            rstd_from_ms(pm2[0:32, 0:n], L["RS"][0:32, 0:n], prm2, "RS")
            STT("dve", L["T32"][0:32, 0:n], pk[0:32, 0:n], VEC[0:32, VOFF["knr"][0]:VOFF["knr"][0] + 1], L["RS"][0:32, 0:n],
                ALU.mult, ALU.mult, [prk, "RS"], ["T32"])
            pr_, prr = bank()
            MM(pr_[0:32, 0:n], CFm("p32", 32), L["T32"][0:32, 0:n], True, True, ["T32"], [prr])
            TT("dve", L["T32B"][0:32, 0:n], pr_[0:32, 0:n], sink, ALU.mult, [prr, "ROPE"], ["T32B"])
            TT("dve", L["T32"][0:32, 0:n], L["T32"][0:32, 0:n], cosk, ALU.mult, ["T32", "ROPE"], ["T32"])
            TT("dve", KRF32, L["T32"][0:32, 0:n], L["T32B"][0:32, 0:n], ALU.add, ["T32", "T32B"], ["KRF32"])
            CP("act", KRBF, KRF32, ["KRF32"], ["KRBF"])

        def qlat(L, Hh, Hres, c0, n, QL):
            rmsnorm(L, Hh, Hres, c0, n, V("nm1"))
            p0, pr0 = bank()
            p1, pr1 = bank()
            for cc, (pp, prr) in enumerate(((p0, pr0), (p1, pr1))):
                for k in range(KC):
                    MM(pp[:, 0:n], WDQ[:, k * 256 + cc * 128:k * 256 + cc * 128 + 128], L["XN"][:, k, 0:n],
                       k == 0, k == KC - 1, ["XN"], [prr])
            ACT(L["SQ"][:, 0, 0:n], p0[:, 0:n], AF.Square, [pr0], ["SQ"])
            ACT(L["SQ"][:, 1, 0:n], p1[:, 0:n], AF.Square, [pr1], ["SQ"])
            pm, prm = bank()
            MM(pm[:, 0:n], CBm("o256"), L["SQ"][:, 0, 0:n], True, False, ["SQ"], [prm])
            MM(pm[:, 0:n], CBm("o256"), L["SQ"][:, 1, 0:n], False, True, ["SQ"], [prm])
            rstd_from_ms(pm[:, 0:n], L["RS"][:, 0:n], prm, "RS")
            STT("dve", QL[:, 0, :], p0[:, 0:n], V("qln", 0, 1), L["RS"][:, 0:n], ALU.mult, ALU.mult, [pr0, "RS"], ["QL"])
            STT("dve", QL[:, 1, :], p1[:, 0:n], V("qln", 1, 2), L["RS"][:, 0:n], ALU.mult, ALU.mult, [pr1, "RS"], ["QL"])

        def qhead_gen(Q, h, QLv, n, cosq, sinq, QT, QTres, ropres):
            pq, prq = banks[6], "B6"
            for cc in range(2):
                MM(pq[0:96, 0:n], WUQ[:, cc, h * 96:h * 96 + 96], QLv[:, cc, :], cc == 0, cc == 1, ["QL"], [prq])
            ACT(Q["QSQ"][0:96, 0:n], pq[0:96, 0:n], AF.Square, [prq], ["QSQ"])
            yield
            pm, prm = bank(0, 3)
            MM(pm[0:96, 0:n], CBm("blk96", 96), Q["QSQ"][0:96, 0:n], True, True, ["QSQ"], [prm])
            rstd_from_ms(pm[0:96, 0:n], Q["QRS"][0:96, 0:n], prm, "QRS")
            STT("dve", Q["QX"][0:96, 0:n], pq[0:96, 0:n], VEC[0:96, VOFF["g96"][0]:VOFF["g96"][0] + 1], Q["QRS"][0:96, 0:n],
                ALU.mult, ALU.mult, [prq, "QRS"], ["QX"])
            CP("act", Q["QXB"][0:96, 0:n], Q["QX"][0:96, 0:n], ["QX"], ["QXB"])
            yield
            pr_, prr = bank(0, 3)
            MM(pr_[0:96, 0:n], CBm("p96", 96), Q["QXB"][0:96, 0:n], True, True, ["QXB"], [prr])
            TT("dve", Q["QRS"][0:96, 0:n], pr_[0:96, 0:n], sinq, ALU.mult, [prr, ropres, "QRS"], ["QRS"])
            TT("dve", Q["QX"][0:96, 0:n], Q["QX"][0:96, 0:n], cosq, ALU.mult, ["QX", ropres], ["QX"])
            TT("dve", QT, Q["QX"][0:96, 0:n], Q["QRS"][0:96, 0:n], ALU.add, ["QX", "QRS"], [QTres])
            yield

        def qhead(*a):
            for _ in qhead_gen(*a):
                pass

        def layer_bufs(st, ncol):
            L = {}
            L["XN"] = sb("L_XN", [128, KC, ncol], BF16, st)
            L["SQ"] = sb("L_SQ", [128, KC, ncol], BF16, st)
            L["RS"] = sb("L_RS", [128, ncol], F32, st)
            L["MEAN"] = sb("L_MEAN", [128, ncol], F32, st)
            L["SG"] = sb("L_SG", [128, ncol], F32, st)
            L["YB2"] = sb("L_YB2", [128, ncol], F32, st)
            L["YB"] = sb("L_YB", [128, KC, ncol], F32, st)
            L["CG"] = sb("L_CG", [128, ncol], F32, st)
            L["CU"] = sb("L_CU", [128, ncol], F32, st)
            L["AV"] = sb("L_AV", [128, NF, ncol], BF16, st)
            L["T32"] = sb("L_T32", [32, ncol], F32, st)
            L["T32B"] = sb("L_T32B", [32, ncol], F32, st)
            L["TAILB"] = sb("L_TAILB", [128, 44, 2], F32, st)
            L["CMASK"] = sb("L_CMASK", [128, HALO], F32, st)
            DMA("sp", L["CMASK"][:], cmask_d, "c0", (), ["CMASK"])
            return L

        with contextlib.ExitStack() as SP:
            HS = sb("HS", [128, KC, NSC], F32, SP)
            DMA("sp", HS[:], xs_d, "ld0", (), ["HS"])
            QLS = sb("QLS", [128, 2, NSC], BF16, SP)
            CSF = sb("CSF", [128, NSC], F32, SP)
            CSB = sb("CSB", [128, NSC], BF16, SP)
            KRSF = sb("KRSF", [32, NSC], F32, SP)
            KRSB = sb("KRSB", [32, NSC], BF16, SP)
            ATS = sb("ATS", [128, KC, NSC], BF16, SP)
            with contextlib.ExitStack() as SL:
                S.barrier()
                L = layer_bufs(SL, NSC)
                US = sb("US", [128, KC, NS, 34], F32, SL)
                TSTG = sb("TSTG", [128, KC, NS, 30], F32, SL)
                FSTG = sb("FSTG", [128, 44, NS, 2], F32, SL)
                DMA("sp", TSTG[:], sca_d, "ld0", (), ["TSTG"])
                for m in range(KC):
                    CP("act", US[:, m, :, 0:30], TSTG[:, m], ["TSTG"], ["U"])
                conv_module(L, HS, "HS", NSC, NS, 34, US, False, cas_o, TSTG)
                UPG = sb("S_UPG", [128, NF, NS, 6], F32, SL)
                UPU = sb("S_UPU", [128, NF, NS, 6], F32, SL)
                DMA("sp", FSTG[:], sff_d[0], "ld0", (), ["FSTG"])
                CP("act", UPG[:, :, :, 0:2], FSTG[:, 0:NF], ["FSTG"], ["UPG"])
                CP("act", UPU[:, :, :, 0:2], FSTG[:, NF:2 * NF], ["FSTG"], ["UPU"])
                conv_ffn(L, 0, HS, "HS", 0, NSC, NS, UPG, UPU, False, ffs_o[0], FSTG, True)
                ROPS = sb("ROPS", [32, 2, NSC], F32, SL)
                DMA("sp", ROPS[:, 0, :], cosks_d, "ld0", (), ["ROPE"])
                DMA("sp", ROPS[:, 1, :], sinks_d, "ld0", (), ["ROPE"])
                shared_kv(L, HS, "HS", 0, NSC, ROPS[:, 0, :], ROPS[:, 1, :], CSF[:], KRSF[:], CSB[:], KRSB[:])
                DMA("sp", cs_o, CSF[:], "o_cs", ["CF32"], [])
                DMA("sp", krs_o, KRSF[:], "o_cs", ["KRF32"], [])
                qlat(L, HS, "HS", 0, NSC, QLS)

            with contextlib.ExitStack() as SA:
                S.barrier()
                Q = {}
                Q["QSQ"] = sb("Q_SQ", [96, NSC], BF16, SA)
                Q["QRS"] = sb("Q_RS", [96, NSC], F32, SA)
                Q["QX"] = sb("Q_X", [96, NSC], F32, SA)
                Q["QXB"] = sb("Q_XB", [96, NSC], BF16, SA)
                RQ = sb("RQ", [96, 2, NSC], F32, SA)
                DMA("sp", RQ[:, 0, :], cosqs_d, "ld0", (), ["ROPQ"])
                DMA("sp", RQ[:, 1, :], sinqs_d, "ld0", (), ["ROPQ"])
                WUKT = sb("WUKT", [64, 2048], BF16, SA)
                DMA("pool", WUKT[:], wukT_d, "c1", (), ["WUKT"])
                QTS = sb("QTS", [96, NSC], BF16, SA)
                QG = sb("QG", [64, NSC], BF16, SA)
                QABS = sb("QABS", [128, NS, 16, 4], BF16, SA)
                QRT = sb("QRT", [96, NS, 16, 4], BF16, SA)
                QR0 = sb("QR0", [32, NS, 16, 4], BF16, SA)
                o_gk = VOFF["gk64"][0]
                for h in range(16):
                    qhead(Q, h, QLS, NSC, RQ[:, 0, :], RQ[:, 1, :], QTS[:], "QTS", "ROPQ")
                    CP("act", QRT[64:96, :, h, :], QTS[64:96, :].rearrange("p (s q) -> p s q", q=4), ["QTS"], ["QRT"])
                    TSC("dve", QG[:], QTS[0:64, :], VEC[0:64, o_gk:o_gk + 1], None, ALU.mult, None, ["QTS"], ["QG"])
                    pa, pra = bank()
                    MM(pa[:, 0:NSC], WUKT[:, h * 128:h * 128 + 128], QG[:], True, True, ["QG", "WUKT"], [pra])
                    CP("act", QABS[:, :, h, :], pa[:, 0:NSC].rearrange("p (s q) -> p s q", q=4), [pra], ["QABS"])
                DMA("sp", QR0[:], QRT[64:96], "ld1", ["QRT"], ["QR0"])

                CNAT = sb("CNAT", [64, 128], BF16, SA)
                pt_, ptr_ = bank()
                TR(pt_[0:64, 0:128], CSF[:], CFm("identf"), ["CF32"], [ptr_])
                CP("act", CNAT[:], pt_[0:64, 0:128], [ptr_], ["CNAT"])
                RSTN = sb("RSTN", [64, 16], F32, SA)
                SQU = sb("SQU", [128, 1024], BF16, SA)
                KU = [banks[0], banks[1]]
                for hf in range(2):
                    MM(KU[hf][0:64, :], CSB[:], WUK[:, hf * 512:hf * 512 + 512], True, True, ["CBF"], ["B%d" % hf])
                    ACT(SQU[0:64, hf * 512:hf * 512 + 512], KU[hf][0:64, :], AF.Square, ["B%d" % hf], ["SQU"])
                RED(RSTN[:], SQU[0:64, :].rearrange("p (h d) -> p h d", d=64), ALU.add, ["SQU"], ["RSTN"])
                TSC("dve", RSTN[:], RSTN[:], 1.0 / 64, None, ALU.mult, None, ["RSTN"], ["RSTN"])
                ACT(RSTN[:], RSTN[:], AF.Sqrt, ["RSTN"], ["RSTN"], bias=EPS, scale=1.0)
                RECIP(RSTN[:], RSTN[:], ["RSTN"], ["RSTN"])
                MNEW = sb("MNEW", [64, NS, 64], F32, SA)
                DMA("sp", MNEW[:], mnew_d, "ld1", (), ["MNEW"])

                PT = sb("PT", [128, NS], I32, SA)
                DMA("sp", PT[:], pt_d, "ld1", (), ["PT"])
                IDXC = sb("IDXC", [128, NS, 8], I32, SA)
                IDXK = sb("IDXK", [128, NS, 2], I32, SA)
                for c8 in range(8):
                    TSC("dve", IDXC[:, :, c8], PT[:], 8, c8, ALU.mult, ALU.add, ["PT"], ["IDXC"])
                for c2 in range(2):
                    TSC("dve", IDXK[:, :, c2], PT[:], 2, c2, ALU.mult, ALU.add, ["PT"], ["IDXK"])

                GC = [sb("GC%d" % i, [128, 128, 128], BF16, SA) for i in range(2)]
                GK = [sb("GK%d" % i, [128, 128, 32], BF16, SA) for i in range(2)]
                SS_ = sb("SSEQ", [128, 129, 64], F32, SA)
                PP = sb("PSEQ", [128, 129, 64], BF16, SA)
                SSQ2 = [sb("SSQ%d" % i, [128, 4, 16], F32, SA) for i in range(2)]
                SQU2 = [sb("SQU2_%d" % i, [128, 1024], BF16, SA) for i in range(2)]
                CTS = [sb("CTS%d" % i, [128, 512], BF16, SA) for i in range(2)]
                KTS = [sb("KTS%d" % i, [32, 512], BF16, SA) for i in range(2)]
                MX = sb("MX", [128, 64], F32, SA)
                MXC = sb("MXC", [64, 1], F32, SA)
                DG = sb("DG", [64, 64], F32, SA)
                MB = sb("MB", [128, 64], F32, SA)
                RLB = sb("RLB", [128, 64], F32, SA)
                OLT = sb("OLT", [128, NS, 16, 4], BF16, SA)
                MEMSET("dve", SS_[:, 128, :], -2000.0, ["SSEQ"])

                for s in range(NS):
                    g = s % 2
                    for c8 in range(8):
                        S.dma("pool", (lambda e, g=g, c8=c8, s=s: e.indirect_dma_start(
                            out=GC[g][:, c8 * 16:(c8 + 1) * 16, :].rearrange("p a b -> p (a b)"), out_offset=None, in_=cc_d,
                            in_offset=bass.IndirectOffsetOnAxis(ap=IDXC[:, s, c8:c8 + 1], axis=0))),
                            "g%d" % g, ["IDXC"], ["GC%d" % g])
                    for c2 in range(2):
                        S.dma("pool", (lambda e, g=g, c2=c2, s=s: e.indirect_dma_start(
                            out=GK[g][:, c2 * 64:(c2 + 1) * 64, :].rearrange("p a b -> p (a b)"), out_offset=None, in_=ckr_d,
                            in_offset=bass.IndirectOffsetOnAxis(ap=IDXK[:, s, c2:c2 + 1], axis=0))),
                            "g%d" % g, ["IDXK"], ["GK%d" % g])
                    qa = QABS[:, s, :, :].rearrange("p h q -> p (h q)")
                    qr = QR0[:, s, :, :].rearrange("p h q -> p (h q)")
                    def stT(ub):
                        b2 = ub % 2
                        for u4 in range(4):
                            u = ub * 4 + u4
                            TR(bankb[:, u4 * 128:(u4 + 1) * 128], GC[g][:, u, :], CBm("identb"), ["GC%d" % g], ["BB"])
                        for u4 in range(4):
                            u = ub * 4 + u4
                            TR(bankb[0:32, 512 + u4 * 128:512 + (u4 + 1) * 128], GK[g][:, u, :], CBm("identb"), ["GK%d" % g], ["BB"])
                        CP("act", CTS[b2][:], bankb[:, 0:512], ["BB"], ["CTS%d" % b2])
                        CP("dve", KTS[b2][:], bankb[0:32, 512:1024], ["BB"], ["KTS%d" % b2])

                    def stM(ub):
                        b2 = ub % 2
                        nd, ndr = banks[4 + b2], "B%d" % (4 + b2)
                        for u4 in range(4):
                            cT = CTS[b2][:, u4 * 128:(u4 + 1) * 128]
                            kk = 2 * (u4 % 2)
                            sq_, sqr = SQU2[u4 % 2], "SQU%d" % (u4 % 2)
                            for hf in range(2):
                                MM(banks[kk + hf][:, :], cT, WUK[:, hf * 512:hf * 512 + 512], True, True,
                                   ["CTS%d" % b2], ["B%d" % (kk + hf)])
                                ACT(sq_[:, hf * 512:hf * 512 + 512], banks[kk + hf][:, :], AF.Square, ["B%d" % (kk + hf)], [sqr])
                            RED(SSQ2[b2][:, u4, :], sq_[:].rearrange("p (h d) -> p h d", d=64), ALU.add, [sqr], ["SSQ%d" % b2])
                            MM(nd[:, u4 * 64:u4 * 64 + 64], cT, qa, True, True, ["CTS%d" % b2, "QABS"], [ndr], skip_group_check=True)
                            MM(nd[:, 256 + u4 * 64:256 + u4 * 64 + 64], KTS[b2][:, u4 * 128:(u4 + 1) * 128], qr, True, True,
                               ["KTS%d" % b2, "QR0"], [ndr], skip_group_check=True)

                    def stE(ub):
                        b2 = ub % 2
                        nd, ndr = banks[4 + b2], "B%d" % (4 + b2)
                        sq = SSQ2[b2]
                        sr = "SSQ%d" % b2
                        ACT(sq[:], sq[:], AF.Sqrt, [sr], [sr], bias=EPS, scale=1.0 / 64)
                        RECIP(sq[:], sq[:], [sr], [sr])
                        sv = SS_[:, ub * 4:ub * 4 + 4, :]
                        TT("dve", sv.rearrange("p u (h q) -> p u h q", q=4),
                           nd[:, 0:256].rearrange("p (u h q) -> p u h q", u=4, q=4),
                           sq[:].unsqueeze(3).to_broadcast([128, 4, 16, 4]), ALU.mult, [ndr, sr], ["SSEQ"])
                        TT("dve", sv, sv, nd[:, 256:512].rearrange("p (u c) -> p u c", u=4), ALU.add, [ndr, "SSEQ"], ["SSEQ"])

                    for ub in range(32):
                        stT(ub)
                        if ub >= 1:
                            stM(ub - 1)
                        if ub >= 2:
                            stE(ub - 2)
                    stM(31)
                    stE(30)
                    stE(31)
                    nd, ndr = bank(4, 6)
                    MM(nd[0:64, 0:64], CSB[:], qa, True, True, ["CBF", "QABS"], [ndr], skip_group_check=True)
                    MM(nd[0:64, 256:320], KRSB[:], qr, True, True, ["KRBF", "QR0"], [ndr], skip_group_check=True)
                    svn = SS_[0:64, 128, :]
                    TT("dve", svn.rearrange("p (h q) -> p h q", q=4), nd[0:64, 0:64].rearrange("p (h q) -> p h q", q=4),
                       RSTN[:].unsqueeze(2).to_broadcast([64, 16, 4]), ALU.mult, [ndr, "RSTN"], ["SSEQ"])
                    TT("dve", svn, svn, nd[0:64, 256:320], ALU.add, [ndr, "SSEQ"], ["SSEQ"])
                    TT("dve", svn, svn, MNEW[:, s, :], ALU.add, ["SSEQ", "MNEW"], ["SSEQ"])
                    RED(MX[:], SS_[:].rearrange("p u c -> p c u"), ALU.max, ["SSEQ"], ["MX"])
                    pm, prm = bank(4, 6)
                    TR(pm[0:64, 0:128], MX[:], CFm("identf"), ["MX"], [prm])
                    RED(MXC[:], pm[0:64, 0:128], ALU.max, [prm], ["MXC"])
                    TSC("dve", DG[:], CFm("identf", 64)[:, 0:64], MXC[:, 0:1], None, ALU.mult, None, ["MXC"], ["DG"])
                    pb, pbr = bank(4, 6)
                    MM(pb[:, 0:64], CFm("onesf", 64), DG[:], True, True, ["DG"], [pbr])
                    CP("dve", MB[:], pb[:, 0:64], [pbr], ["MB"])
                    TT("dve", SS_[:], SS_[:], MB[:].unsqueeze(1).to_broadcast([128, 129, 64]), ALU.subtract, ["SSEQ", "MB"], ["SSEQ"])
                    ACT(PP[:], SS_[:], AF.Exp, ["SSEQ"], ["PSEQ"], scale=float(SM_SCALE))
                    pacc, paccr = banks[6], "B6"
                    plb, plbr = banks[0], "B0"
                    for u in range(128):
                        MM(pacc[:, 0:64], GC[g][:, u, :], PP[:, u, :], u == 0, False, ["GC%d" % g, "PSEQ"], [paccr])
                    MM(pacc[:, 0:64], CNAT[:], PP[0:64, 128, :], False, True, ["CNAT", "PSEQ"], [paccr])
                    for u in range(128):
                        MM(plb[:, 0:64], CBm("onesb"), PP[:, u, :], u == 0, False, ["PSEQ"], [plbr])
                    MM(plb[:, 0:64], CBm("onesb", 64), PP[0:64, 128, :], False, True, ["PSEQ"], [plbr])
                    RECIP(RLB[:], plb[:, 0:64], [plbr], ["RLB"])
                    TT("dve", OLT[:, s, :, :].rearrange("p h q -> p (h q)"), pacc[:, 0:64], RLB[:], ALU.mult, [paccr, "RLB"], ["OLT"])
                for kq in range(KC):
                    po, por = bank()
                    for hh in range(2):
                        h = 2 * kq + hh
                        MM(po[hh * 64:hh * 64 + 64, 0:NSC], WUV[:, h * 64:h * 64 + 64], OLT[:, :, h, :], True, True, ["OLT"], [por])
                    CP("act", ATS[:, kq, :], po[:, 0:NSC], [por], ["ATS"])

            with contextlib.ExitStack() as SL:
                S.barrier()
                L = layer_bufs(SL, NSC)
                for kq in range(KC):
                    wt, wr = wload(wo_d[kq], 1024)
                    for mo in range(KC):
                        po, por = bank()
                        MM(po[:, 0:NSC], wt[:, mo * 128:mo * 128 + 128], ATS[:, kq, :], True, True, [wr, "ATS"], [por])
                        TT("dve", HS[:, mo, :], po[:, 0:NSC], HS[:, mo, :], ALU.add, [por, "HS"], ["HS"])
                UPG = sb("S_UPG", [128, NF, NS, 6], F32, SL)
                UPU = sb("S_UPU", [128, NF, NS, 6], F32, SL)
                FSTG = sb("FSTG", [128, 44, NS, 2], F32, SL)
                DMA("sp", FSTG[:], sff_d[1], "ld0", (), ["FSTG"])
                CP("act", UPG[:, :, :, 0:2], FSTG[:, 0:NF], ["FSTG"], ["UPG"])
                CP("act", UPU[:, :, :, 0:2], FSTG[:, NF:2 * NF], ["FSTG"], ["UPU"])
                conv_ffn(L, 1, HS, "HS", 0, NSC, NS, UPG, UPU, False, ffs_o[1], FSTG, True)
                DMA("sp", ys_o, HS[:], "o_ys", ["HS"], [])

        with contextlib.ExitStack() as PR:
            S.barrier()
            HT = [sb("H%d" % t, [128, KC, TW], F32, PR) for t in range(NT)]
            QLP = sb("QLP", [128, 2, NT, NQ], BF16, PR)
            with contextlib.ExitStack() as PA:
                S.barrier()
                L = layer_bufs(PA, TW)
                UP_ = sb("UP", [128, KC, 1, TW], F32, PA)
                UPG = sb("P_UPG", [128, 2, 1, TW], F32, PA)
                UPU = sb("P_UPU", [128, 2, 1, TW], F32, PA)
                ROPK = sb("ROPK", [32, 2, TS_], F32, PA)
                CPF = sb("CPF", [128, TS_], F32, PA)
                CPB = sb("CPB", [128, TS_], BF16, PA)
                KPF = sb("KPF", [32, TS_], F32, PA)
                KPB = sb("KPB", [32, TS_], BF16, PA)
                TSTGP = sb("TSTGP", [128, KC, 1, 30], F32, PA)
                for t in range(NT):
                    Hh, Hr = HT[t], "H%d" % t
                    DMA("sp", Hh[:], xp_d[t], "ld0", (), [Hr])
                    last = (t == NT - 1)
                    conv_module(L, Hh, Hr, TW, 1, TW, UP_, t == 0, cap_o if last else None, TSTGP)
                    conv_ffn(L, 0, Hh, Hr, 30, TW, 1, UPG, UPU, t == 0, ffp_o[0] if last else None, "o_ffp", False)
                    DMA("sp", ROPK[:, 0, :], cosk_d[:, t, :], "ld0", (), ["ROPE"])
                    DMA("sp", ROPK[:, 1, :], sink_d[:, t, :], "ld0", (), ["ROPE"])
                    shared_kv(L, Hh, Hr, HALO, TS_, ROPK[:, 0, :], ROPK[:, 1, :], CPF[:], KPF[:], CPB[:], KPB[:])
                    DMA("sp", cp_o[t], CPF[:], "o_cp", ["CF32"], [])
                    DMA("sp", krp_o[t], KPF[:], "o_cp", ["KRF32"], [])
                    DMA("sp", xch_in.ap()[0:128, t * TS_:(t + 1) * TS_], CPB[:], "xin", ["CBF"], ["XIN"])
                    DMA("sp", xch_in.ap()[128:160, t * TS_:(t + 1) * TS_], KPB[:], "xin", ["KRBF"], ["XIN"])
                    qlat(L, Hh, Hr, 32, NQ, QLP[:, :, t, :])
            S.op("pool", lambda e: e.collective_compute("AllGather", ALU.bypass, replica_groups=[[0, 1, 2, 3], [4, 5, 6, 7]],
                                                        ins=[xch_in.ap()], outs=[xch_out.ap()]), ["XIN"], ["XOUT"])

            with contextlib.ExitStack() as PB:
                S.barrier()
                NKP = NKC * 128
                CTA = sb("CTA", [128, NKP], BF16, PB)
                KT = sb("KT", [96, NKP], BF16, PB)
                VH = sb("VH", [128, NKC, 65], BF16, PB)
                AOP = sb("AOP", [128, NT, 3, 128], BF16, PB)
                ATT = sb("ATT", [128, NQ], BF16, PB)
                RQP = sb("RQP", [96, 2, NT, NQ], F32, PB)
                THR = sb("THR", [128, NT * NKC], F32, PB)
                PTb = [sb("PTb%d" % i, [128, NQ], BF16, PB) for i in range(4)]
                QT2 = [sb("QT%d" % i, [96, NQ], BF16, PB) for i in range(2)]
                KSQ = [sb("KSQ%d" % i, [64, 512], BF16, PB) for i in range(2)]
                KRS_ = [sb("KRS%d" % i, [64, 512], F32, PB) for i in range(2)]
                RL = sb("RL", [128, 1], F32, PB)
                Q = {}
                Q["QSQ"] = sb("QP_SQ", [96, NQ], BF16, PB)
                Q["QRS"] = sb("QP_RS", [96, NQ], F32, PB)
                Q["QX"] = sb("QP_X", [96, NQ], F32, PB)
                Q["QXB"] = sb("QP_XB", [96, NQ], BF16, PB)
                DMA("sp", RQP[:, 0], cosq_d, "ld0", (), ["ROPQ"])
                DMA("sp", RQP[:, 1], sinq_d, "ld0", (), ["ROPQ"])
                DMA("sp", THR[:], thr_d, "ld0", (), ["THR"])
                MEMSET("dve", CTA[:, NPOS:NKP], 0.0, ["CTA"])
                MEMSET("dve", KT[:, NPOS:NKP], 0.0, ["KT"])
                MEMSET("dve", VH[:, :, 64:65], 1.0, ["VH"])
                for gt in range(24):
                    r, t = gt % 4, gt // 4
                    DMA("sp", CTA[:, gt * TS_:(gt + 1) * TS_], xch_out.ap()[r * 160:r * 160 + 128, t * TS_:(t + 1) * TS_],
                        "ld1", ["XOUT"], ["CTA"])
                    DMA("sp", KT[64:96, gt * TS_:(gt + 1) * TS_], xch_out.ap()[r * 160 + 128:r * 160 + 160, t * TS_:(t + 1) * TS_],
                        "ld1", ["XOUT"], ["KT"])
                o_gk = VOFF["gk64"][0]
                qsub = [(0, 128), (128, 128), (256, NQ - 256)]
                pctr = 0
                qgen = [None]
                for h in range(16):
                    def ka(kt):
                        n = 512 if kt < 16 else NKP - 16 * 512
                        cs = kt * 512
                        pk, prk = bank(0, 4)
                        MM(pk[0:64, 0:n], WUK[:, h * 64:h * 64 + 64], CTA[:, cs:cs + n], True, True, ["CTA"], [prk])
                        ACT(KSQ[kt % 2][:, 0:n], pk[0:64, 0:n], AF.Square, [prk], ["KSQ%d" % (kt % 2)])
                        return (kt, n, cs, pk, prk)

                    def kb(kt, n, cs, pk, prk):
                        pm, prm = bank(0, 4)
                        kr_ = KRS_[kt % 2]
                        krr = "KRS%d" % (kt % 2)
                        MM(pm[0:64, 0:n], CBm("o64", 64), KSQ[kt % 2][:, 0:n], True, True, ["KSQ%d" % (kt % 2)], [prm])
                        rstd_from_ms(pm[0:64, 0:n], kr_[:, 0:n], prm, krr)
                        STT("dve", KT[0:64, cs:cs + n], pk[0:64, 0:n], VEC[0:64, o_gk:o_gk + 1], kr_[:, 0:n], ALU.mult, ALU.mult,
                            [prk, krr], ["KT"])

                    kpend = []
                    for kt in range(17):
                        kpend.append(ka(kt))
                        if len(kpend) > 1:
                            kb(*kpend.pop(0))
                    while kpend:
                        kb(*kpend.pop(0))
                    for vb in range(9):
                        nchk = 8 if vb < 8 else 1
                        pv, prv = bank(0, 4)
                        for ci in range(nchk):
                            kc = vb * 8 + ci
                            MM(pv[:, ci * 64:ci * 64 + 64], CTA[:, kc * 128:kc * 128 + 128], WUV[:, h * 64:h * 64 + 64], True, True,
                               ["CTA"], [prv], skip_group_check=True)
                        CP("act", VH[:, vb * 8:vb * 8 + nchk, 0:64], pv[:, 0:nchk * 64].rearrange("p (c d) -> p c d", d=64), [prv], ["VH"])
                    for t in range(NT):
                        qi_ = h * NT + t
                        QTt, QTr = QT2[qi_ % 2], "QT%d" % (qi_ % 2)
                        if qgen[0] is None:
                            qhead(Q, h, QLP[:, :, t, :], NQ, RQP[:, 0, t, :], RQP[:, 1, t, :], QTt[:], QTr, "ROPQ")
                        else:
                            for _ in qgen[0]:
                                pass
                        nh, nt_ = (h, t + 1) if t + 1 < NT else (h + 1, 0)
                        if nh < 16:
                            nq = qi_ + 1
                            qgen[0] = qhead_gen(Q, nh, QLP[:, :, nt_, :], NQ, RQP[:, 0, nt_, :], RQP[:, 1, nt_, :],
                                                QT2[nq % 2][:], "QT%d" % (nq % 2), "ROPQ")
                        else:
                            qgen[0] = iter(())
                        nkc = min(NKC, -(-((4 * t + 4) * TS_) // 128))
                        first_masked = max(0, (4 * t * TS_ - 2 - 127 + 127) // 128)
                        def s_stage(kc, pctr):
                            kn = 128 if kc < NKC - 1 else NPOS - 128 * (NKC - 1)
                            pst, pstr = bank(0, 3)
                            MM(pst[0:kn, 0:NQ], KT[:, kc * 128:kc * 128 + kn], QTt[:], True, True, ["KT", QTr], [pstr])
                            pb_ = PTb[pctr % 4]
                            pbr = "PTb%d" % (pctr % 4)
                            ACT(pb_[0:kn, :], pst[0:kn, 0:NQ], AF.Exp, [pstr], [pbr], scale=float(SM_SCALE))
                            if kc >= first_masked:
                                col = t * NKC + kc
                                STT("dve", pb_[0:kn, :], CFm("iota")[0:kn, :], THR[0:kn, col:col + 1], pb_[0:kn, :], ALU.is_ge, ALU.mult,
                                    [pbr, "THR"], [pbr])
                            return (kc, kn, pb_, pbr)

                        def pv_stage(kc, kn, pb_, pbr):
                            for qi, (q0, qn) in enumerate(qsub):
                                MM(banks[3 + qi][0:qn, 0:65], pb_[0:kn, q0:q0 + qn], VH[0:kn, kc, :], kc == 0, kc == nkc - 1,
                                   [pbr, "VH"], ["B%d" % (3 + qi)])

                        pend = []
                        for kc in range(nkc):
                            pend.append(s_stage(kc, pctr))
                            pctr += 1
                            if kc in (2, 5, 8):
                                next(qgen[0], None)
                            if len(pend) > 2:
                                pv_stage(*pend.pop(0))
                        while pend:
                            pv_stage(*pend.pop(0))
                        for qi, (q0, qn) in enumerate(qsub):
                            br = "B%d" % (3 + qi)
                            TSC("dve", RL[0:qn, :], banks[3 + qi][0:qn, 64:65], 1e-30, None, ALU.max, None, [br], ["RL"])
                            RECIP(RL[0:qn, :], RL[0:qn, :], ["RL"], ["RL"])
                            TSC("dve", AOP[0:qn, t, qi, (h % 2) * 64:(h % 2) * 64 + 64], banks[3 + qi][0:qn, 0:64], RL[0:qn, 0:1], None,
                                ALU.mult, None, [br, "RL"], ["AOP"])
                    if h % 2 == 1:
                        kq = h // 2
                        wt, wr = wload(wo_d[kq], 1024)
                        for t in range(NT):
                            for qi, (q0, qn) in enumerate(qsub):
                                TR(bankb[:, q0:q0 + qn], AOP[0:qn, t, qi, :], CBm("identb")[0:qn, 0:qn], ["AOP"], ["BB"])
                            CP("act", ATT[:], bankb[:, 0:NQ], ["BB"], ["ATT"])
                            for mo in range(KC):
                                po, por = bank(0, 3)
                                MM(po[:, 0:NQ], wt[:, mo * 128:mo * 128 + 128], ATT[:], True, True, [wr, "ATT"], [por])
                                TT("dve", HT[t][:, mo, 32:TW], po[:, 0:NQ], HT[t][:, mo, 32:TW], ALU.add, [por, "H%d" % t], ["H%d" % t])

            with contextlib.ExitStack() as PC:
                S.barrier()
                L = layer_bufs(PC, TW)
                UPG = sb("P_UPG", [128, 2, 1, TW], F32, PC)
                UPU = sb("P_UPU", [128, 2, 1, TW], F32, PC)
                for t in range(NT):
                    Hh, Hr = HT[t], "H%d" % t
                    last = (t == NT - 1)
                    conv_ffn(L, 1, Hh, Hr, 32, TW, 1, UPG, UPU, t == 0, ffp_o[1] if last else None, "o_ffp", False)
                    DMA("sp", yp_o[t], Hh[:, :, HALO:TW], "o_yp", [Hr], [])
        S.emit()
    return nc


def kernel(x_prompt, x_sample, state_conv_a, state_ffn_conv, cache_kv_latent, cache_k_rope, page_table,
           meta_tokens, norm_mix, norm_ffn,
           a_w_pw1, a_b_pw1, a_w_dw, a_b_dw, a_ln_g, a_ln_b, a_w_pw2, a_b_pw2,
           ffn_w_up, ffn_w_dw, ffn_w_down,
           kv_norm, mla_w_dkv, mla_lat_norm, mla_w_kr, mla_knorm_rope, mla_w_uk, mla_w_uv, mla_knorm_nope,
           mla_w_dq, mla_q_lat_norm, mla_w_uq, mla_qnorm_nope, mla_qnorm_rope, mla_w_o):
    f32 = np.float32
    A = lambda a: np.asarray(a)
    x_prompt, x_sample = A(x_prompt), A(x_sample)
    n_phys = int(A(cache_kv_latent).shape[0])
    cache_c = np.ascontiguousarray(A(cache_kv_latent), f32).reshape(n_phys * 8, 2048)
    cache_kr = np.ascontiguousarray(A(cache_k_rope), f32).reshape(n_phys * 2, 2048)
    page_table = A(page_table).astype(np.int32)

    vec = np.zeros((128, NVEC), f32)
    def putv(name, m):
        o, w = VOFF[name]
        m = np.asarray(m, f32)
        vec[:m.shape[0], o:o + w] = m.reshape(m.shape[0], w)
    putv("nm0", _fm(A(norm_mix)[0], 8)); putv("nm1", _fm(A(norm_mix)[1], 8))
    putv("nf0", _fm(A(norm_ffn)[0], 8)); putv("nf1", _fm(A(norm_ffn)[1], 8))
    putv("kvn", _fm(A(kv_norm), 8))
    putv("b1", _fm(A(a_b_pw1)[0], 16)); putv("bdw", _fm(A(a_b_dw)[0], 8))
    putv("lng", _fm(A(a_ln_g)[0], 8)); putv("lnb", _fm(A(a_ln_b)[0], 8)); putv("b2", _fm(A(a_b_pw2)[0], 8))
    putv("wdwa", np.ascontiguousarray(A(a_w_dw)[0].reshape(31, 8, 128).transpose(2, 1, 0)).reshape(128, 8 * 31))
    for l in range(2):
        putv("wdwf%d" % l, np.ascontiguousarray(A(ffn_w_dw)[l].reshape(3, 44, 128).transpose(2, 1, 0)).reshape(128, 44 * 3))
    putv("latn", A(mla_lat_norm).reshape(128, 1))
    putv("knr", A(mla_knorm_rope).reshape(32, 1))
    putv("qln", _fm(A(mla_q_lat_norm)[0], 2))
    putv("g96", np.concatenate([A(mla_qnorm_nope)[0], A(mla_qnorm_rope)[0]]).reshape(96, 1))
    putv("gk64", A(mla_knorm_nope).reshape(64, 1))
    cb = _consts_bf()
    cf = _consts_f()

    W1 = A(a_w_pw1)[0].astype(f32)
    w1r = W1.reshape(8, 128, 2, 8, 128)
    w1l = np.ascontiguousarray(w1r.transpose(3, 1, 0, 2, 4)).reshape(8, 128, 2048)
    W2 = A(a_w_pw2)[0].astype(f32).reshape(8, 128, 4, 2, 128)
    w2l = np.ascontiguousarray(W2.transpose(2, 1, 3, 0, 4)).reshape(4, 128, 2048)
    WU = A(ffn_w_up).astype(f32).reshape(2, 8, 128, 2, NF, 128)
    wupl = np.ascontiguousarray(WU.transpose(0, 4, 2, 1, 3, 5)).reshape(2, NF, 128, 2048)
    WD = A(ffn_w_down).astype(f32).reshape(2, 2, 11, 128, 8, 128)
    wdnl = np.ascontiguousarray(WD.transpose(0, 4, 1, 3, 2, 5)).reshape(2, 16, 128, 1408)
    wol = np.ascontiguousarray(A(mla_w_o)[0].astype(f32)).reshape(8, 128, 1024)
    wdq = np.ascontiguousarray(A(mla_w_dq)[0].astype(f32).reshape(8, 128, 256).transpose(1, 0, 2)).reshape(128, 2048)
    wuq = np.ascontiguousarray(A(mla_w_uq)[0].astype(f32).reshape(2, 128, 1536))
    wdkv = np.ascontiguousarray(A(mla_w_dkv).astype(f32).reshape(8, 128, 128).transpose(1, 0, 2)).reshape(128, 1024)
    wkr = np.ascontiguousarray(A(mla_w_kr).astype(f32).reshape(8, 128, 32).transpose(1, 0, 2)).reshape(128, 256)
    wuk = np.ascontiguousarray(A(mla_w_uk).astype(f32).reshape(128, 1024))
    wuv = np.ascontiguousarray(A(mla_w_uv).astype(f32).reshape(128, 1024))
    wukT = np.ascontiguousarray(A(mla_w_uk).astype(f32).transpose(2, 1, 0)).reshape(64, 2048)

    iota = np.arange(TW)
    masknew = np.full((64, NS, 64), -2000.0, f32)
    for s in range(NS):
        for k in range(4):
            for q in range(4):
                if k <= q:
                    masknew[s * 4 + k, s, np.arange(16) * 4 + q] = 0.0
    pos_s = PAST + (np.arange(NSC) % 4)
    cks, sks = _rope_tab(pos_s)
    cosqs = np.ones((96, NSC), f32); sinqs = np.zeros((96, NSC), f32)
    cosqs[64:], sinqs[64:] = cks, sks

    hp_all = np.concatenate([np.broadcast_to(A(meta_tokens).astype(f32)[None], (2, NMETA, D)), x_prompt.astype(f32)], axis=1)
    shared = dict(vec=vec, cb=cb, cf=cf, w1l=w1l, w2l=w2l, wupl=wupl, wdnl=wdnl, wol=wol, wdq=wdq, wuq=wuq, wdkv=wdkv,
                  wkr=wkr, wuk=wuk, wuv=wuv, wukT=wukT, cache_c=cache_c, cache_kr=cache_kr, masknew=masknew,
                  cosqs=cosqs, sinqs=sinqs, cosks=cks, sinks=sks)
    in_maps = []
    for c in range(8):
        b, j = c // 4, c % 4
        xp = np.zeros((NT, 128, KC, TW), f32)
        cosq = np.ones((96, NT, NQ), f32); sinq = np.zeros((96, NT, NQ), f32)
        cosk = np.zeros((32, NT, TS_), f32); sink = np.zeros((32, NT, TS_), f32)
        thr = np.zeros((128, NT * NKC), f32)
        for t in range(NT):
            g = 4 * t + j
            p0 = g * TS_ - HALO
            win = np.zeros((TW, D), f32)
            lo = max(0, p0)
            win[lo - p0:] = hp_all[b, lo:p0 + TW]
            xp[t] = win.T.reshape(KC, 128, TW).transpose(1, 0, 2)
            posq = p0 + 32 + np.arange(NQ)
            cq, sq_ = _rope_tab(posq)
            cosq[64:, t], sinq[64:, t] = cq, sq_
            cosk[:, t], sink[:, t] = cq[:, 2:], sq_[:, 2:]
            thr[:, t * NKC:(t + 1) * NKC] = (128 * np.arange(NKC) - (p0 + 32))[None, :]
        sl = slice(NS * c, NS * c + NS)
        xs = np.ascontiguousarray(x_sample[sl].astype(f32).reshape(NSC, D).T.reshape(KC, 128, NSC).transpose(1, 0, 2))
        sca = np.ascontiguousarray(A(state_conv_a)[0, sl].astype(f32).transpose(2, 0, 1).reshape(KC, 128, NS, 30).transpose(1, 0, 2, 3))
        sff = np.ascontiguousarray(A(state_ffn_conv)[:, sl].astype(f32).transpose(0, 3, 1, 2).reshape(2, 44, 128, NS, 2).transpose(0, 2, 1, 3, 4))
        cmask = np.ones((128, HALO), f32)
        if j == 0:
            cmask[:] = 0.0
        m = dict(shared)
        m.update(xp=xp, xs=xs, sca=sca, sff=sff, ptT=np.ascontiguousarray(page_table[sl].T), colmask=cmask,
                 cosq=cosq, sinq=sinq, cosk=cosk, sink=sink, thr=thr)
        in_maps.append(m)

    nc = build_program(n_phys)
    res = run_bass_kernel_spmd(nc, in_maps, core_ids=list(range(8)))
    R = res.results

    y_prompt = np.zeros((2, NPOS, D), f32)
    kvp = np.zeros((2, NPOS, 128), f32)
    krp = np.zeros((2, NPOS, 32), f32)
    y_sample = np.zeros((128, 4, D), f32)
    cas = np.zeros((1, 128, 30, D), f32)
    ffs = np.zeros((2, 128, 2, 2 * DFF), f32)
    kvs = np.zeros((128, 4, 128), f32)
    krs = np.zeros((128, 4, 32), f32)
    cap = np.zeros((1, 2, 30, D), f32)
    ffp = np.zeros((2, 2, 2, 2 * DFF), f32)
    for c in range(8):
        b, j = c // 4, c % 4
        r = R[c]
        for t in range(NT):
            g = 4 * t + j
            y_prompt[b, g * TS_:(g + 1) * TS_] = np.asarray(r["yp"][t]).transpose(1, 0, 2).reshape(D, TS_).T
            kvp[b, g * TS_:(g + 1) * TS_] = np.asarray(r["cp"][t]).T
            krp[b, g * TS_:(g + 1) * TS_] = np.asarray(r["krp"][t]).T
        sl = slice(NS * c, NS * c + NS)
        y_sample[sl] = np.asarray(r["ys"]).transpose(1, 0, 2).reshape(D, NS, 4).transpose(1, 2, 0)
        cas[0, sl] = np.asarray(r["cas"]).transpose(1, 0, 2, 3).reshape(D, NS, 30).transpose(1, 2, 0)
        ffs[:, sl] = np.asarray(r["ffs"]).transpose(0, 2, 1, 3, 4).reshape(2, 2 * DFF, NS, 2).transpose(0, 2, 3, 1)
        kvs[sl] = np.asarray(r["cs"]).T.reshape(NS, 4, 128)
        krs[sl] = np.asarray(r["krs"]).T.reshape(NS, 4, 32)
        if j == 3:
            cap[0, b] = np.asarray(r["cap"]).transpose(1, 0, 2).reshape(D, 30).T
            ffp[:, b] = np.asarray(r["ffp"]).transpose(0, 2, 1, 3).reshape(2, 2 * DFF, 2).transpose(0, 2, 1)
    return (np.ascontiguousarray(y_prompt[:, NMETA:]), y_sample, cap, cas, ffp, ffs, kvp, krp, kvs, krs)
```

```python
import contextlib
import numpy as np
import concourse.bass as bass
import concourse.mybir as mybir
from concourse.bass_utils import run_bass_kernel_spmd

F32 = mybir.dt.float32
BF16 = mybir.dt.bfloat16
I32 = mybir.dt.int32
ALU = mybir.AluOpType
AF = mybir.ActivationFunctionType
AX = mybir.AxisListType

D = 1024
KC = 8
SEQ = 8192
NMETA = 16
NPOS = SEQ + NMETA
TS_ = 342
HALO = 34
TW = TS_ + HALO
NT = 6
NQ = TW - 32
NS = 16
NSC = 64
DFF = 2816
NF = 22
NPAGE = 128
PAST = 16384
EPS = 1e-6
SM_SCALE = 1.0 / np.sqrt(96.0)
NKC = 65
EPOCH = 12000
SAME_ENGINE_SYNC = True


class Sched:
    ENG = ("pe", "act", "dve", "pool", "sp")

    def __init__(self, nc):
        self.nc = nc
        self.ops = {e: [] for e in self.ENG}
        self.count = {e: 0 for e in self.ENG}
        self.waited = {e: {} for e in self.ENG}
        self.last_write = {}
        self.readers = {}
        self.lane_count = {}
        self.sems = {}
        self.all_sem_keys = []

    def _need(self, eng, tok):
        key, val = tok
        if key[0] == eng and (not SAME_ENGINE_SYNC or eng == "pe"):
            return
        w = self.waited[eng]
        if key[0] in self.ENG:
            cur = w.get(key[0], (-1, 0))
            if (key[1], val) <= cur:
                return
            w[key[0]] = (key[1], val)
        else:
            if w.get(key, 0) >= val:
                return
            w[key] = val
        self.ops[eng].append(("wait", key, val))

    def _deps(self, eng, reads, writes):
        toks = []
        for r in reads:
            t = self.last_write.get(r)
            if t:
                toks.append(t)
            if r[0] == "B" and (r[1:].isdigit() or r == "BB"):
                toks.extend(x for x in self.readers.get(r, ()) if x[0][0] != eng)
        for r in writes:
            t = self.last_write.get(r)
            if t:
                toks.append(t)
            toks.extend(self.readers.get(r, ()))
        for t in toks:
            self._need(eng, t)

    def _commit(self, tok, reads, writes):
        for r in writes:
            self.last_write[r] = tok
            self.readers[r] = []
        for r in reads:
            if r in writes:
                continue
            self.readers.setdefault(r, []).append(tok)

    def op(self, eng, fn, reads=(), writes=()):
        self._deps(eng, reads, writes)
        self.count[eng] += 1
        idx = self.count[eng]
        key = (eng, (idx - 1) // EPOCH)
        val = (idx - 1) % EPOCH + 1
        if key not in self.sems:
            self.sems[key] = None
            self.all_sem_keys.append(key)
        self.ops[eng].append(("op", fn, key, 1))
        self._commit((key, val), reads, writes)

    def dma(self, q, fn, lane, reads=(), writes=()):
        self._deps(q, reads, writes)
        self.rr = getattr(self, "rr", {})
        self.rr[q] = self.rr.get(q, 0) + 1
        lane = "%s%d" % (q, self.rr[q] % 10)
        key = ("lane", lane)
        if self.lane_count.get(lane, 0) > 0:
            self._need(q, (key, self.lane_count[lane]))
        if key not in self.sems:
            self.sems[key] = None
            self.all_sem_keys.append(key)
        self.lane_count[lane] = self.lane_count.get(lane, 0) + 16
        self.ops[q].append(("op", fn, key, 16))
        self._commit((key, self.lane_count[lane]), reads, writes)

    def barrier(self):
        toks = []
        for e in self.ENG:
            idx = self.count[e]
            if idx > 0:
                toks.append(((e, (idx - 1) // EPOCH), (idx - 1) % EPOCH + 1))
        for lane, cnt in self.lane_count.items():
            toks.append((("lane", lane), cnt))
        for e in self.ENG:
            for t in toks:
                self._need(e, t)

    def all_wait(self, res):
        t = self.last_write.get(res)
        if t:
            for e in self.ENG:
                self._need(e, t)

    def emit(self):
        nc = self.nc
        with contextlib.ExitStack() as st:
            for i, key in enumerate(self.all_sem_keys):
                self.sems[key] = st.enter_context(nc.semaphore("s%d" % i))
            for key in self.all_sem_keys:
                if key[0] == "lane":
                    self._need("sp", (key, self.lane_count[key[1]]))
            blk = st.enter_context(nc.Block())

            def run(engname):
                def body(e):
                    for item in self.ops[engname]:
                        if item[0] == "wait":
                            e.wait_ge(self.sems[item[1]], item[2])
                        else:
                            item[1](e).then_inc(self.sems[item[2]], item[3])
                return body

            blk.tensor(run("pe"))
            blk.scalar(run("act"))
            blk.vector(run("dve"))
            blk.gpsimd(run("pool"))
            blk.sync(run("sp"))


VEC_LAYOUT = [("nm0", 8), ("nm1", 8), ("nf0", 8), ("nf1", 8), ("kvn", 8), ("b1", 16), ("bdw", 8), ("lng", 8),
              ("lnb", 8), ("b2", 8), ("wdwa", 8 * 31), ("wdwf0", 44 * 3), ("wdwf1", 44 * 3), ("latn", 1),
              ("knr", 1), ("qln", 2), ("g96", 1), ("gk64", 1)]
VOFF = {}
_o = 0
for _n, _w in VEC_LAYOUT:
    VOFF[_n] = (_o, _w)
    _o += _w
NVEC = _o

CB_LAYOUT = [("o1024", 128), ("o256", 128), ("o128", 128), ("blk96", 96), ("o64", 64), ("p96", 96), ("identb", 128),
             ("onesb", 128)]
CBOFF = {}
_o = 0
for _n, _w in CB_LAYOUT:
    CBOFF[_n] = (_o, _w)
    _o += _w
NCB = _o


def _fm(v, nchunk):
    return np.ascontiguousarray(np.asarray(v, np.float32).reshape(nchunk, 128).T)


def _consts_bf():
    c = np.zeros((128, NCB), np.float32)
    def put(name, m):
        o, w = CBOFF[name]
        c[:m.shape[0], o:o + m.shape[1]] = m
    put("o1024", np.full((128, 128), 1.0 / 1024, np.float32))
    put("o256", np.full((128, 128), 1.0 / 256, np.float32))
    put("o128", np.full((128, 128), 1.0 / 128, np.float32))
    blk = np.zeros((96, 96), np.float32)
    blk[:64, :64] = 1.0 / 64
    blk[64:, 64:] = 1.0 / 32
    put("blk96", blk)
    put("o64", np.full((64, 64), 1.0 / 64, np.float32))
    p96 = np.zeros((96, 96), np.float32)
    for m in range(16):
        p96[64 + m + 16, 64 + m] = -1.0
        p96[64 + m, 64 + m + 16] = 1.0
    put("p96", p96)
    put("identb", np.eye(128, dtype=np.float32))
    put("onesb", np.ones((128, 128), np.float32))
    return c


CF_LAYOUT = [("identf", 128), ("p32", 32), ("o32", 32), ("onesf", 128), ("iota", NQ)]
CFOFF = {}
_o = 0
for _n, _w in CF_LAYOUT:
    CFOFF[_n] = (_o, _w)
    _o += _w
NCF = _o


def _consts_f():
    c = np.zeros((128, NCF), np.float32)
    def put(name, m):
        o, w = CFOFF[name]
        c[:m.shape[0], o:o + m.shape[1]] = m
    put("identf", np.eye(128, dtype=np.float32))
    p32 = np.zeros((32, 32), np.float32)
    for m in range(16):
        p32[m + 16, m] = -1.0
        p32[m, m + 16] = 1.0
    put("p32", p32)
    put("o32", np.full((32, 32), 1.0 / 32, np.float32))
    put("onesf", np.ones((128, 128), np.float32))
    put("iota", (np.arange(NQ)[None, :] - np.arange(128)[:, None]).astype(np.float32))
    return c


def _rope_tab(pos):
    inv = (10000.0 ** (-np.arange(16, dtype=np.float32) / 16)).astype(np.float32)
    ang = pos.astype(np.float32)[None, :] * inv[:, None]
    cos = np.cos(ang).astype(np.float32)
    sin = np.sin(ang).astype(np.float32)
    return np.concatenate([cos, cos], 0), np.concatenate([sin, sin], 0)


def build_program(n_phys):
    nc = bass.Bass("TRN2", target_bir_lowering=False)
    S = Sched(nc)

    def din(name, shape, dt=F32):
        return nc.dram_tensor(name, list(shape), dt, kind="ExternalInput").ap()

    def dout(name, shape, dt=F32):
        return nc.dram_tensor(name, list(shape), dt, kind="ExternalOutput").ap()

    xp_d = din("xp", [NT, 128, KC, TW])
    xs_d = din("xs", [128, KC, NSC])
    sca_d = din("sca", [128, KC, NS, 30])
    sff_d = din("sff", [2, 128, 44, NS, 2])
    cc_d = din("cache_c", [n_phys * 8, 2048])
    ckr_d = din("cache_kr", [n_phys * 2, 2048])
    pt_d = din("ptT", [128, NS], I32)
    cmask_d = din("colmask", [128, HALO])
    cosq_d = din("cosq", [96, NT, NQ])
    sinq_d = din("sinq", [96, NT, NQ])
    cosk_d = din("cosk", [32, NT, TS_])
    sink_d = din("sink", [32, NT, TS_])
    cosqs_d = din("cosqs", [96, NSC])
    sinqs_d = din("sinqs", [96, NSC])
    cosks_d = din("cosks", [32, NSC])
    sinks_d = din("sinks", [32, NSC])
    thr_d = din("thr", [128, NT * NKC])
    mnew_d = din("masknew", [64, NS, 64])
    vec_d = din("vec", [128, NVEC])
    cb_d = din("cb", [128, NCB])
    cf_d = din("cf", [128, NCF])
    w1_d = din("w1l", [8, 128, 2048])
    w2_d = din("w2l", [4, 128, 2048])
    wup_d = din("wupl", [2, NF, 128, 2048])
    wdn_d = din("wdnl", [2, 16, 128, 1408])
    wo_d = din("wol", [8, 128, 1024])
    wdq_d = din("wdq", [128, 2048])
    wuq_d = din("wuq", [2, 128, 1536])
    wdkv_d = din("wdkv", [128, 1024])
    wkr_d = din("wkr", [128, 256])
    wuk_d = din("wuk", [128, 1024])
    wuv_d = din("wuv", [128, 1024])
    wukT_d = din("wukT", [64, 2048])

    yp_o = dout("yp", [NT, 128, KC, TS_])
    ys_o = dout("ys", [128, KC, NSC])
    cap_o = dout("cap", [128, KC, 30])
    cas_o = dout("cas", [128, KC, NS, 30])
    ffp_o = dout("ffp", [2, 128, 44, 2])
    ffs_o = dout("ffs", [2, 128, 44, NS, 2])
    cp_o = dout("cp", [NT, 128, TS_])
    krp_o = dout("krp", [NT, 32, TS_])
    cs_o = dout("cs", [128, NSC])
    krs_o = dout("krs", [32, NSC])

    xch_in = nc.dram_tensor("xch_in", [160, NT * TS_], BF16)
    xch_out = nc.dram_tensor("xch_out", [640, NT * TS_], BF16)

    def MM(out, lhsT, rhs, start, stop, R, W, **kw):
        S.op("pe", lambda e: e.matmul(out, lhsT=lhsT, rhs=rhs, start=start, stop=stop, **kw), R, W)

    def TR(out, in_, ident, R, W):
        S.op("pe", lambda e: e.transpose(out=out, in_=in_, identity=ident), R, W)

    def ACT(out, in_, func, R, W, bias=None, scale=None):
        kw = {}
        if bias is not None:
            kw["bias"] = bias
        if scale is not None:
            kw["scale"] = scale
        S.op("act", lambda e: e.activation(out=out, in_=in_, func=func, **kw), R, W)

    def TSC(eng, out, in0, s1, s2, op0, op1, R, W):
        if op1 is None:
            S.op(eng, lambda e: e.tensor_scalar(out=out, in0=in0, scalar1=s1, scalar2=None, op0=op0), R, W)
        else:
            S.op(eng, lambda e: e.tensor_scalar(out=out, in0=in0, scalar1=s1, scalar2=s2, op0=op0, op1=op1), R, W)

    def STT(eng, out, in0, scalar, in1, op0, op1, R, W):
        S.op(eng, lambda e: e.scalar_tensor_tensor(out=out, in0=in0, scalar=scalar, in1=in1, op0=op0, op1=op1), R, W)

    def TT(eng, out, in0, in1, op, R, W):
        S.op(eng, lambda e: e.tensor_tensor(out=out, in0=in0, in1=in1, op=op), R, W)

    def CP(eng, out, in_, R, W):
        if eng == "act":
            S.op("act", lambda e: e.activation(out=out, in_=in_, func=AF.Copy), R, W)
        else:
            S.op(eng, lambda e: e.tensor_copy(out=out, in_=in_), R, W)

    def RECIP(out, in_, R, W):
        S.op("dve", lambda e: e.reciprocal(out=out, in_=in_), R, W)

    def MEMSET(eng, ap, val, W):
        S.op(eng, lambda e: e.memset(ap, val), (), W)

    def RED(out, in_, op, R, W):
        S.op("dve", lambda e: e.tensor_reduce(out=out, in_=in_, axis=AX.X, op=op), R, W)

    def DMA(q, out, in_, lane, R, W):
        S.dma(q, lambda e: e.dma_start(out=out, in_=in_), lane, R, W)

    with contextlib.ExitStack() as G:
        nctr = [0]

        def sb(name, shape, dt, st=G):
            nctr[0] += 1
            return st.enter_context(nc.sbuf_tensor("%s_%d" % (name, nctr[0]), list(shape), dt))

        banks = [G.enter_context(nc.psum_tensor("B%d" % i, [128, 512], F32)) for i in range(7)]
        bankb = G.enter_context(nc.psum_tensor("BB", [128, 1024], BF16))
        bctr = [0]

        def bank(lo=0, hi=7):
            i = lo + bctr[0] % (hi - lo)
            bctr[0] += 1
            return banks[i], "B%d" % i

        VEC = sb("VEC", [128, NVEC], F32)
        CB = sb("CB", [128, NCB], BF16)
        CF = sb("CF", [128, NCF], F32)
        WDQ = sb("WDQ", [128, 2048], BF16)
        WUQ = sb("WUQ", [128, 2, 1536], BF16)
        WDKV = sb("WDKV", [128, 1024], BF16)
        WKR = sb("WKR", [128, 256], BF16)
        WUK = sb("WUK", [128, 1024], BF16)
        WUV = sb("WUV", [128, 1024], BF16)
        DMA("sp", VEC[:], vec_d, "c0", (), ["VEC"])
        DMA("sp", CF[:], cf_d, "c0", (), ["CF"])
        DMA("pool", CB[:], cb_d, "c1", (), ["CB"])
        DMA("pool", WDQ[:], wdq_d, "c1", (), ["WDQ"])
        DMA("pool", WUQ[:, 0, :], wuq_d[0], "c1", (), ["WUQ"])
        DMA("pool", WUQ[:, 1, :], wuq_d[1], "c1", (), ["WUQ"])
        DMA("pool", WDKV[:], wdkv_d, "c1", (), ["WDKV"])
        DMA("pool", WKR[:], wkr_d, "c1", (), ["WKR"])
        DMA("pool", WUK[:], wuk_d, "c1", (), ["WUK"])
        DMA("pool", WUV[:], wuv_d, "c1", (), ["WUV"])
        for r in ("VEC", "CF", "CB", "WDQ", "WUQ", "WDKV", "WKR", "WUK", "WUV"):
            S.all_wait(r)

        def V(name, a=None, b=None):
            o, w = VOFF[name]
            if a is None:
                return VEC[:, o:o + w]
            return VEC[:, o + a:o + b]

        def CBm(name, rows=128):
            o, w = CBOFF[name]
            return CB[0:rows, o:o + w]

        def CFm(name, rows=128):
            o, w = CFOFF[name]
            return CF[0:rows, o:o + w]

        NSLOT = 4
        WS = [sb("WS%d" % i, [128, 2048], BF16) for i in range(NSLOT)]
        wctr = [0]

        def wload(src, ne):
            i = wctr[0] % NSLOT
            wctr[0] += 1
            DMA("pool", WS[i][:, 0:ne], src, "w%d" % i, (), ["WS%d" % i])
            return WS[i], "WS%d" % i

        def rstd_from_ms(ps_ap, rs_ap, Rps, Wrs):
            ACT(rs_ap, ps_ap, AF.Sqrt, [Rps], [Wrs], bias=EPS, scale=1.0)
            RECIP(rs_ap, rs_ap, [Wrs], [Wrs])

        def rstd_lnexp(ps_ap, rs_ap, Rps, Wrs):
            ACT(rs_ap, ps_ap, AF.Ln, [Rps], [Wrs], bias=EPS, scale=1.0)
            ACT(rs_ap, rs_ap, AF.Exp, [Wrs], [Wrs], scale=-0.5)

        def rmsnorm(L, Hh, Hres, c0, n, gain):
            ACT(L["SQ"][:, :, 0:n], Hh[:, :, c0:c0 + n], AF.Square, [Hres], ["SQ"])
            ps, pr = bank()
            for k in range(KC):
                MM(ps[:, 0:n], CBm("o1024"), L["SQ"][:, k, 0:n], k == 0, k == KC - 1, ["SQ"], [pr])
            rstd_from_ms(ps[:, 0:n], L["RS"][:, 0:n], pr, "RS")
            for k in range(KC):
                STT("dve", L["XN"][:, k, 0:n], Hh[:, k, c0:c0 + n], gain[:, k:k + 1], L["RS"][:, 0:n],
                    ALU.mult, ALU.mult, [Hres, "RS"], ["XN"])

        def conv_module(L, Hh, Hres, N, Sq, Lw, U, mask_first, tail_out, tail_lane):
            Lo = Lw - 30
            n = Sq * Lo
            newc = N // Sq
            rmsnorm(L, Hh, Hres, 0, N, V("nm0"))
            for m in range(KC):
                wt, wr = wload(w1_d[m], 2048)
                pa, pra = bank()
                pg, prg = bank()
                for k in range(KC):
                    MM(pa[:, 0:N], wt[:, k * 256:k * 256 + 128], L["XN"][:, k, 0:N], k == 0, k == KC - 1, [wr, "XN"], [pra])
                for k in range(KC):
                    MM(pg[:, 0:N], wt[:, k * 256 + 128:k * 256 + 256], L["XN"][:, k, 0:N], k == 0, k == KC - 1, [wr, "XN"], [prg])
                ACT(L["SG"][:, 0:N], pg[:, 0:N], AF.Sigmoid, [prg], ["SG"], bias=V("b1", 8 + m, 9 + m))
                uo = U[:, m, :, Lw - newc:Lw]
                STT("dve", uo, pa[:, 0:N].rearrange("p (s l) -> p s l", s=Sq), V("b1", m, m + 1),
                    L["SG"][:, 0:N].rearrange("p (s l) -> p s l", s=Sq), ALU.add, ALU.mult, [pra, "SG"], ["U"])
                if mask_first:
                    TT("dve", U[:, m, 0, 0:HALO], U[:, m, 0, 0:HALO], L["CMASK"][:, 0:HALO], ALU.mult, ["U", "CMASK"], ["U"])
                yb = L["YB"][:, m, 0:n].rearrange("p (s l) -> p s l", s=Sq)
                o, _ = VOFF["wdwa"]
                TSC("dve", yb, U[:, m, :, 0:Lo], VEC[:, o + m * 31:o + m * 31 + 1], V("bdw", m, m + 1), ALU.mult, ALU.add,
                    ["U"], ["YB"])
                yb2 = L["YB2"][:, 0:n].rearrange("p (s l) -> p s l", s=Sq)
                TSC("dve", yb2, U[:, m, :, 1:1 + Lo], VEC[:, o + m * 31 + 1:o + m * 31 + 2], None, ALU.mult, None, ["U"], ["YB2"])
                for kk in range(2, 31):
                    if kk % 2 == 0:
                        STT("dve", yb, U[:, m, :, kk:kk + Lo], VEC[:, o + m * 31 + kk:o + m * 31 + kk + 1], yb,
                            ALU.mult, ALU.add, ["U", "YB"], ["YB"])
                    else:
                        STT("dve", yb2, U[:, m, :, kk:kk + Lo], VEC[:, o + m * 31 + kk:o + m * 31 + kk + 1], yb2,
                            ALU.mult, ALU.add, ["U", "YB2"], ["YB2"])
                TT("dve", yb, yb, yb2, ALU.add, ["YB", "YB2"], ["YB"])
            if tail_out is not None:
                for m in range(KC):
                    CP("act", tail_lane[:, m], U[:, m, :, Lw - 30:Lw], ["U"], ["TSTG"])
                DMA("sp", tail_out, tail_lane[:] if Sq > 1 else tail_lane[:, :, 0, :], "x", ["TSTG"], [])
            ACT(L["SQ"][:, :, 0:n], L["YB"][:, :, 0:n], AF.Square, ["YB"], ["SQ"])
            CP("act", L["XN"][:, :, 0:n], L["YB"][:, :, 0:n], ["YB"], ["XN"])
            pm, prm = bank()
            pq, prq = bank()
            for k in range(KC):
                MM(pm[:, 0:n], CBm("o1024"), L["XN"][:, k, 0:n], k == 0, k == KC - 1, ["XN"], [prm])
            for k in range(KC):
                MM(pq[:, 0:n], CBm("o1024"), L["SQ"][:, k, 0:n], k == 0, k == KC - 1, ["SQ"], [prq])
            CP("dve", L["MEAN"][:, 0:n], pm[:, 0:n], [prm], ["MEAN"])
            TT("dve", L["RS"][:, 0:n], L["MEAN"][:, 0:n], L["MEAN"][:, 0:n], ALU.mult, ["MEAN"], ["RS"])
            TT("dve", L["RS"][:, 0:n], pq[:, 0:n], L["RS"][:, 0:n], ALU.subtract, [prq, "RS"], ["RS"])
            TSC("dve", L["RS"][:, 0:n], L["RS"][:, 0:n], 0.0, None, ALU.max, None, ["RS"], ["RS"])
            ACT(L["RS"][:, 0:n], L["RS"][:, 0:n], AF.Sqrt, ["RS"], ["RS"], bias=EPS, scale=1.0)
            RECIP(L["RS"][:, 0:n], L["RS"][:, 0:n], ["RS"], ["RS"])
            for k in range(KC):
                TT("dve", L["YB"][:, k, 0:n], L["YB"][:, k, 0:n], L["MEAN"][:, 0:n], ALU.subtract, ["YB", "MEAN"], ["YB"])
                TT("dve", L["YB"][:, k, 0:n], L["YB"][:, k, 0:n], L["RS"][:, 0:n], ALU.mult, ["YB", "RS"], ["YB"])
                ACT(L["XN"][:, k, 0:n], L["YB"][:, k, 0:n], AF.Silu, ["YB"], ["XN"], bias=V("lnb", k, k + 1), scale=V("lng", k, k + 1))
            c0 = N - n
            for i in range(4):
                wt, wr = wload(w2_d[i], 2048)
                for mo2 in range(2):
                    mo = 2 * i + mo2
                    po, pro = bank()
                    for k in range(KC):
                        MM(po[:, 0:n], wt[:, mo2 * 1024 + k * 128:mo2 * 1024 + k * 128 + 128], L["XN"][:, k, 0:n],
                           k == 0, k == KC - 1, [wr, "XN"], [pro])
                    STT("dve", Hh[:, mo, c0:N], po[:, 0:n], V("b2", mo, mo + 1), Hh[:, mo, c0:N], ALU.add, ALU.add,
                        [pro, Hres], [Hres])

        def conv_ffn(L, l, Hh, Hres, c1, N, Sq, UPG, UPU, mask_first, tail_out, tail_lane, pre_loaded):
            n1 = N - c1
            if pre_loaded:
                Lq = n1 // Sq
                nout = n1
            else:
                Lq = n1 - 2
                nout = Lq
            rmsnorm(L, Hh, Hres, c1, n1, V("nf%d" % l))
            wo_, _ = VOFF["wdwf%d" % l]
            for f in range(NF):
                wt, wr = wload(wup_d[l, f], 2048)
                pg, prg = bank()
                pu, pru = bank()
                for k in range(KC):
                    MM(pg[:, 0:n1], wt[:, k * 256:k * 256 + 128], L["XN"][:, k, 0:n1], k == 0, k == KC - 1, [wr, "XN"], [prg])
                for k in range(KC):
                    MM(pu[:, 0:n1], wt[:, k * 256 + 128:k * 256 + 256], L["XN"][:, k, 0:n1], k == 0, k == KC - 1, [wr, "XN"], [pru])
                chains = []
                for (ps_, pr_, UP, ci, cres0, cv) in ((pg, prg, UPG, f, "UPG", "CG"), (pu, pru, UPU, NF + f, "UPU", "CU")):
                    fx = f if pre_loaded else f % 2
                    cres = cres0 if pre_loaded else "%s%d" % (cres0, fx)
                    if pre_loaded:
                        CP("act", UP[:, fx, :, 2:2 + Lq], ps_[:, 0:n1].rearrange("p (s l) -> p s l", s=Sq), [pr_], [cres])
                    else:
                        CP("act", UP[:, fx, 0, 0:n1], ps_[:, 0:n1], [pr_], [cres])
                        if mask_first:
                            TT("dve", UP[:, fx, 0, 0:HALO - c1], UP[:, fx, 0, 0:HALO - c1], L["CMASK"][:, c1:HALO], ALU.mult, [cres, "CMASK"], [cres])
                        if tail_out is not None:
                            CP("dve", L["TAILB"][:, ci, :], UP[:, fx, 0, Lq:Lq + 2], [cres], ["TAILB"])
                    cvv = L[cv][:, 0:nout].rearrange("p (s l) -> p s l", s=Sq)
                    chains.append((UP, fx, cres, cv, cvv, wo_ + ci * 3))
                for tap in range(3):
                    for (UP, fx, cres, cv, cvv, wb) in chains:
                        if tap == 0:
                            TSC("dve", cvv, UP[:, fx, :, 0:Lq], VEC[:, wb:wb + 1], None, ALU.mult, None, [cres], [cv])
                        else:
                            STT("dve", cvv, UP[:, fx, :, tap:tap + Lq], VEC[:, wb + tap:wb + tap + 1], cvv, ALU.mult, ALU.add, [cres, cv], [cv])
                ACT(L["CG"][:, 0:nout], L["CG"][:, 0:nout], AF.Silu, ["CG"], ["CG"])
                TT("dve", L["AV"][:, f, 0:nout], L["CG"][:, 0:nout], L["CU"][:, 0:nout], ALU.mult, ["CG", "CU"], ["AV"])
            if tail_out is not None:
                if pre_loaded:
                    for (UP, cres, half) in ((UPG, "UPG", 0), (UPU, "UPU", 1)):
                        CP("act", tail_lane[:, half * NF:(half + 1) * NF], UP[:, :, :, Lq:Lq + 2], [cres], ["FSTG"])
                    DMA("sp", tail_out, tail_lane[:], "x", ["FSTG"], [])
                else:
                    DMA("sp", tail_out, L["TAILB"][:], tail_lane, ["TAILB"], [])
            co = N - nout
            for mo in range(KC):
                po, pro = bank()
                for half in range(2):
                    wt, wr = wload(wdn_d[l, mo * 2 + half], 1408)
                    for fi in range(11):
                        f = half * 11 + fi
                        MM(po[:, 0:nout], wt[:, fi * 128:fi * 128 + 128], L["AV"][:, f, 0:nout], f == 0, f == NF - 1,
                           [wr, "AV"], [pro])
                TT("dve", Hh[:, mo, co:N], po[:, 0:nout], Hh[:, mo, co:N], ALU.add, [pro, Hres], [Hres])

        def shared_kv(L, Hh, Hres, c0, n, cosk, sink, CF32, KRF32, CBF, KRBF):
            rmsnorm(L, Hh, Hres, c0, n, V("kvn"))
            pc, prc = bank()
            pk, prk = bank()
            for k in range(KC):
                MM(pc[:, 0:n], WDKV[:, k * 128:k * 128 + 128], L["XN"][:, k, 0:n], k == 0, k == KC - 1, ["XN"], [prc])
            for k in range(KC):
                MM(pk[0:32, 0:n], WKR[:, k * 32:k * 32 + 32], L["XN"][:, k, 0:n], k == 0, k == KC - 1, ["XN"], [prk])
            ACT(L["SQ"][:, 0, 0:n], pc[:, 0:n], AF.Square, [prc], ["SQ"])
            pm, prm = bank()
            MM(pm[:, 0:n], CBm("o128"), L["SQ"][:, 0, 0:n], True, True, ["SQ"], [prm])
            rstd_from_ms(pm[:, 0:n], L["RS"][:, 0:n], prm, "RS")
            STT("dve", CF32, pc[:, 0:n], V("latn"), L["RS"][:, 0:n], ALU.mult, ALU.mult, [prc, "RS"], ["CF32"])
            CP("act", CBF, CF32, ["CF32"], ["CBF"])
            ACT(L["T32"][0:32, 0:n], pk[0:32, 0:n], AF.Square, [prk], ["T32"])
            pm2, prm2 = bank()
            MM(pm2[0:32, 0:n], CFm("o32", 32), L["T32"][0:32, 0:n], True, True, ["T32"], [prm2])
            rstd_from_ms(pm2[0:32, 0:n], L["RS"][0:32, 0:n], prm2, "RS")
            STT("dve", L["T32"][0:32, 0:n], pk[0:32, 0:n], VEC[0:32, VOFF["knr"][0]:VOFF["knr"][0] + 1], L["RS"][0:32, 0:n],
                ALU.mult, ALU.mult, [prk, "RS"], ["T32"])
            pr_, prr = bank()
            MM(pr_[0:32, 0:n], CFm("p32", 32), L["T32"][0:32, 0:n], True, True, ["T32"], [prr])
            TT("dve", L["T32B"][0:32, 0:n], pr_[0:32, 0:n], sink, ALU.mult, [prr, "ROPE"], ["T32B"])
            TT("dve", L["T32"][0:32, 0:n], L["T32"][0:32, 0:n], cosk, ALU.mult, ["T32", "ROPE"], ["T32"])
            TT("dve", KRF32, L["T32"][0:32, 0:n], L["T32B"][0:32, 0:n], ALU.add, ["T32", "T32B"], ["KRF32"])
            CP("act", KRBF, KRF32, ["KRF32"], ["KRBF"])

        def qlat(L, Hh, Hres, c0, n, QL):
            rmsnorm(L, Hh, Hres, c0, n, V("nm1"))
            p0, pr0 = bank()
            p1, pr1 = bank()
            for cc, (pp, prr) in enumerate(((p0, pr0), (p1, pr1))):
                for k in range(KC):
                    MM(pp[:, 0:n], WDQ[:, k * 256 + cc * 128:k * 256 + cc * 128 + 128], L["XN"][:, k, 0:n],
                       k == 0, k == KC - 1, ["XN"], [prr])
            ACT(L["SQ"][:, 0, 0:n], p0[:, 0:n], AF.Square, [pr0], ["SQ"])
            ACT(L["SQ"][:, 1, 0:n], p1[:, 0:n], AF.Square, [pr1], ["SQ"])
            pm, prm = bank()
            MM(pm[:, 0:n], CBm("o256"), L["SQ"][:, 0, 0:n], True, False, ["SQ"], [prm])
            MM(pm[:, 0:n], CBm("o256"), L["SQ"][:, 1, 0:n], False, True, ["SQ"], [prm])
            rstd_from_ms(pm[:, 0:n], L["RS"][:, 0:n], prm, "RS")
            STT("dve", QL[:, 0, :], p0[:, 0:n], V("qln", 0, 1), L["RS"][:, 0:n], ALU.mult, ALU.mult, [pr0, "RS"], ["QL"])
            STT("dve", QL[:, 1, :], p1[:, 0:n], V("qln", 1, 2), L["RS"][:, 0:n], ALU.mult, ALU.mult, [pr1, "RS"], ["QL"])

        def qhead_gen(Q, h, QLv, n, cosq, sinq, QT, QTres, ropres):
            pq, prq = banks[6], "B6"
            for cc in range(2):
                MM(pq[0:96, 0:n], WUQ[:, cc, h * 96:h * 96 + 96], QLv[:, cc, :], cc == 0, cc == 1, ["QL"], [prq])
            ACT(Q["QSQ"][0:96, 0:n], pq[0:96, 0:n], AF.Square, [prq], ["QSQ"])
            yield
            pm, prm = bank(0, 3)
            MM(pm[0:96, 0:n], CBm("blk96", 96), Q["QSQ"][0:96, 0:n], True, True, ["QSQ"], [prm])
            rstd_lnexp(pm[0:96, 0:n], Q["QRS"][0:96, 0:n], prm, "QRS")
            STT("dve", Q["QX"][0:96, 0:n], pq[0:96, 0:n], VEC[0:96, VOFF["g96"][0]:VOFF["g96"][0] + 1], Q["QRS"][0:96, 0:n],
                ALU.mult, ALU.mult, [prq, "QRS"], ["QX"])
            CP("act", Q["QXB"][0:96, 0:n], Q["QX"][0:96, 0:n], ["QX"], ["QXB"])
            yield
            pr_, prr = bank(0, 3)
            MM(pr_[0:96, 0:n], CBm("p96", 96), Q["QXB"][0:96, 0:n], True, True, ["QXB"], [prr])
            TT("dve", Q["QRS"][0:96, 0:n], pr_[0:96, 0:n], sinq, ALU.mult, [prr, ropres, "QRS"], ["QRS"])
            TT("dve", Q["QX"][0:96, 0:n], Q["QX"][0:96, 0:n], cosq, ALU.mult, ["QX", ropres], ["QX"])
            TT("dve", QT, Q["QX"][0:96, 0:n], Q["QRS"][0:96, 0:n], ALU.add, ["QX", "QRS"], [QTres])
            yield

        def qhead(*a):
            for _ in qhead_gen(*a):
                pass

        def layer_bufs(st, ncol):
            L = {}
            L["XN"] = sb("L_XN", [128, KC, ncol], BF16, st)
            L["SQ"] = sb("L_SQ", [128, KC, ncol], BF16, st)
            L["RS"] = sb("L_RS", [128, ncol], F32, st)
            L["MEAN"] = sb("L_MEAN", [128, ncol], F32, st)
            L["SG"] = sb("L_SG", [128, ncol], F32, st)
            L["YB2"] = sb("L_YB2", [128, ncol], F32, st)
            L["YB"] = sb("L_YB", [128, KC, ncol], F32, st)
            L["CG"] = sb("L_CG", [128, ncol], F32, st)
            L["CU"] = sb("L_CU", [128, ncol], F32, st)
            L["AV"] = sb("L_AV", [128, NF, ncol], BF16, st)
            L["T32"] = sb("L_T32", [32, ncol], F32, st)
            L["T32B"] = sb("L_T32B", [32, ncol], F32, st)
            L["TAILB"] = sb("L_TAILB", [128, 44, 2], F32, st)
            L["CMASK"] = sb("L_CMASK", [128, HALO], F32, st)
            DMA("sp", L["CMASK"][:], cmask_d, "c0", (), ["CMASK"])
            return L

        with contextlib.ExitStack() as SP:
            HS = sb("HS", [128, KC, NSC], F32, SP)
            DMA("sp", HS[:], xs_d, "ld0", (), ["HS"])
            QLS = sb("QLS", [128, 2, NSC], BF16, SP)
            CSF = sb("CSF", [128, NSC], F32, SP)
            CSB = sb("CSB", [128, NSC], BF16, SP)
            KRSF = sb("KRSF", [32, NSC], F32, SP)
            KRSB = sb("KRSB", [32, NSC], BF16, SP)
            ATS = sb("ATS", [128, KC, NSC], BF16, SP)
            with contextlib.ExitStack() as SL:
                S.barrier()
                L = layer_bufs(SL, NSC)
                US = sb("US", [128, KC, NS, 34], F32, SL)
                TSTG = sb("TSTG", [128, KC, NS, 30], F32, SL)
                FSTG = sb("FSTG", [128, 44, NS, 2], F32, SL)
                DMA("sp", TSTG[:], sca_d, "ld0", (), ["TSTG"])
                for m in range(KC):
                    CP("act", US[:, m, :, 0:30], TSTG[:, m], ["TSTG"], ["U"])
                conv_module(L, HS, "HS", NSC, NS, 34, US, False, cas_o, TSTG)
                UPG = sb("S_UPG", [128, NF, NS, 6], F32, SL)
                UPU = sb("S_UPU", [128, NF, NS, 6], F32, SL)
                DMA("sp", FSTG[:], sff_d[0], "ld0", (), ["FSTG"])
                CP("act", UPG[:, :, :, 0:2], FSTG[:, 0:NF], ["FSTG"], ["UPG"])
                CP("act", UPU[:, :, :, 0:2], FSTG[:, NF:2 * NF], ["FSTG"], ["UPU"])
                conv_ffn(L, 0, HS, "HS", 0, NSC, NS, UPG, UPU, False, ffs_o[0], FSTG, True)
                ROPS = sb("ROPS", [32, 2, NSC], F32, SL)
                DMA("sp", ROPS[:, 0, :], cosks_d, "ld0", (), ["ROPE"])
                DMA("sp", ROPS[:, 1, :], sinks_d, "ld0", (), ["ROPE"])
                shared_kv(L, HS, "HS", 0, NSC, ROPS[:, 0, :], ROPS[:, 1, :], CSF[:], KRSF[:], CSB[:], KRSB[:])
                DMA("sp", cs_o, CSF[:], "o_cs", ["CF32"], [])
                DMA("sp", krs_o, KRSF[:], "o_cs", ["KRF32"], [])
                qlat(L, HS, "HS", 0, NSC, QLS)

            with contextlib.ExitStack() as SA:
                S.barrier()
                Q = {}
                Q["QSQ"] = sb("Q_SQ", [96, NSC], BF16, SA)
                Q["QRS"] = sb("Q_RS", [96, NSC], F32, SA)
                Q["QX"] = sb("Q_X", [96, NSC], F32, SA)
                Q["QXB"] = sb("Q_XB", [96, NSC], BF16, SA)
                RQ = sb("RQ", [96, 2, NSC], F32, SA)
                DMA("sp", RQ[:, 0, :], cosqs_d, "ld0", (), ["ROPQ"])
                DMA("sp", RQ[:, 1, :], sinqs_d, "ld0", (), ["ROPQ"])
                WUKT = sb("WUKT", [64, 2048], BF16, SA)
                DMA("pool", WUKT[:], wukT_d, "c1", (), ["WUKT"])
                QTS = sb("QTS", [96, NSC], BF16, SA)
                QG = sb("QG", [64, NSC], BF16, SA)
                QABS = sb("QABS", [128, NS, 16, 4], BF16, SA)
                QRT = sb("QRT", [96, NS, 16, 4], BF16, SA)
                QR0 = sb("QR0", [32, NS, 16, 4], BF16, SA)
                o_gk = VOFF["gk64"][0]
                for h in range(16):
                    qhead(Q, h, QLS, NSC, RQ[:, 0, :], RQ[:, 1, :], QTS[:], "QTS", "ROPQ")
                    CP("act", QRT[64:96, :, h, :], QTS[64:96, :].rearrange("p (s q) -> p s q", q=4), ["QTS"], ["QRT"])
                    TSC("dve", QG[:], QTS[0:64, :], VEC[0:64, o_gk:o_gk + 1], None, ALU.mult, None, ["QTS"], ["QG"])
                    pa, pra = bank()
                    MM(pa[:, 0:NSC], WUKT[:, h * 128:h * 128 + 128], QG[:], True, True, ["QG", "WUKT"], [pra])
                    CP("act", QABS[:, :, h, :], pa[:, 0:NSC].rearrange("p (s q) -> p s q", q=4), [pra], ["QABS"])
                DMA("sp", QR0[:], QRT[64:96], "ld1", ["QRT"], ["QR0"])

                CNAT = sb("CNAT", [64, 128], BF16, SA)
                pt_, ptr_ = bank()
                TR(pt_[0:64, 0:128], CSF[:], CFm("identf"), ["CF32"], [ptr_])
                CP("act", CNAT[:], pt_[0:64, 0:128], [ptr_], ["CNAT"])
                RSTN = sb("RSTN", [64, 16], F32, SA)
                SQU = sb("SQU", [128, 1024], BF16, SA)
                KU = [banks[0], banks[1]]
                for hf in range(2):
                    MM(KU[hf][0:64, :], CSB[:], WUK[:, hf * 512:hf * 512 + 512], True, True, ["CBF"], ["B%d" % hf])
                    ACT(SQU[0:64, hf * 512:hf * 512 + 512], KU[hf][0:64, :], AF.Square, ["B%d" % hf], ["SQU"])
                RED(RSTN[:], SQU[0:64, :].rearrange("p (h d) -> p h d", d=64), ALU.add, ["SQU"], ["RSTN"])
                TSC("dve", RSTN[:], RSTN[:], 1.0 / 64, None, ALU.mult, None, ["RSTN"], ["RSTN"])
                ACT(RSTN[:], RSTN[:], AF.Sqrt, ["RSTN"], ["RSTN"], bias=EPS, scale=1.0)
                RECIP(RSTN[:], RSTN[:], ["RSTN"], ["RSTN"])
                MNEW = sb("MNEW", [64, NS, 64], F32, SA)
                DMA("sp", MNEW[:], mnew_d, "ld1", (), ["MNEW"])

                PT = sb("PT", [128, NS], I32, SA)
                DMA("sp", PT[:], pt_d, "ld1", (), ["PT"])
                IDXC = sb("IDXC", [128, NS, 8], I32, SA)
                IDXK = sb("IDXK", [128, NS, 2], I32, SA)
                for c8 in range(8):
                    TSC("dve", IDXC[:, :, c8], PT[:], 8, c8, ALU.mult, ALU.add, ["PT"], ["IDXC"])
                for c2 in range(2):
                    TSC("dve", IDXK[:, :, c2], PT[:], 2, c2, ALU.mult, ALU.add, ["PT"], ["IDXK"])

                GC = [sb("GC%d" % i, [128, 128, 128], BF16, SA) for i in range(2)]
                GK = [sb("GK%d" % i, [128, 128, 32], BF16, SA) for i in range(2)]
                SS_ = sb("SSEQ", [128, 129, 64], F32, SA)
                PP = sb("PSEQ", [128, 129, 64], BF16, SA)
                SSQ2 = [sb("SSQ%d" % i, [128, 4, 16], F32, SA) for i in range(2)]
                SQU2 = [sb("SQU2_%d" % i, [128, 1024], BF16, SA) for i in range(2)]
                CTS = [sb("CTS%d" % i, [128, 512], BF16, SA) for i in range(2)]
                KTS = [sb("KTS%d" % i, [32, 512], BF16, SA) for i in range(2)]
                MX = sb("MX", [128, 64], F32, SA)
                MXC = sb("MXC", [64, 1], F32, SA)
                DG = sb("DG", [64, 64], F32, SA)
                MB = sb("MB", [128, 64], F32, SA)
                RLB = sb("RLB", [128, 64], F32, SA)
                OLT = sb("OLT", [128, NS, 16, 4], BF16, SA)
                MEMSET("dve", SS_[:, 128, :], -2000.0, ["SSEQ"])

                for s in range(NS):
                    g = s % 2
                    for c8 in range(8):
                        S.dma("pool", (lambda e, g=g, c8=c8, s=s: e.indirect_dma_start(
                            out=GC[g][:, c8 * 16:(c8 + 1) * 16, :].rearrange("p a b -> p (a b)"), out_offset=None, in_=cc_d,
                            in_offset=bass.IndirectOffsetOnAxis(ap=IDXC[:, s, c8:c8 + 1], axis=0))),
                            "g%d" % g, ["IDXC"], ["GC%d" % g])
                    for c2 in range(2):
                        S.dma("pool", (lambda e, g=g, c2=c2, s=s: e.indirect_dma_start(
                            out=GK[g][:, c2 * 64:(c2 + 1) * 64, :].rearrange("p a b -> p (a b)"), out_offset=None, in_=ckr_d,
                            in_offset=bass.IndirectOffsetOnAxis(ap=IDXK[:, s, c2:c2 + 1], axis=0))),
                            "g%d" % g, ["IDXK"], ["GK%d" % g])
                    qa = QABS[:, s, :, :].rearrange("p h q -> p (h q)")
                    qr = QR0[:, s, :, :].rearrange("p h q -> p (h q)")
                    def stT(ub):
                        b2 = ub % 2
                        for u4 in range(4):
                            u = ub * 4 + u4
                            TR(bankb[:, u4 * 128:(u4 + 1) * 128], GC[g][:, u, :], CBm("identb"), ["GC%d" % g], ["BB"])
                        for u4 in range(4):
                            u = ub * 4 + u4
                            TR(bankb[0:32, 512 + u4 * 128:512 + (u4 + 1) * 128], GK[g][:, u, :], CBm("identb"), ["GK%d" % g], ["BB"])
                        CP("act", CTS[b2][:], bankb[:, 0:512], ["BB"], ["CTS%d" % b2])
                        CP("dve", KTS[b2][:], bankb[0:32, 512:1024], ["BB"], ["KTS%d" % b2])

                    def stM(ub):
                        b2 = ub % 2
                        nd, ndr = banks[4 + b2], "B%d" % (4 + b2)
                        for u4 in range(4):
                            cT = CTS[b2][:, u4 * 128:(u4 + 1) * 128]
                            kk = 2 * (u4 % 2)
                            sq_, sqr = SQU2[u4 % 2], "SQU%d" % (u4 % 2)
                            for hf in range(2):
                                MM(banks[kk + hf][:, :], cT, WUK[:, hf * 512:hf * 512 + 512], True, True,
                                   ["CTS%d" % b2], ["B%d" % (kk + hf)])
                                ACT(sq_[:, hf * 512:hf * 512 + 512], banks[kk + hf][:, :], AF.Square, ["B%d" % (kk + hf)], [sqr])
                            RED(SSQ2[b2][:, u4, :], sq_[:].rearrange("p (h d) -> p h d", d=64), ALU.add, [sqr], ["SSQ%d" % b2])
                            MM(nd[:, u4 * 64:u4 * 64 + 64], cT, qa, True, True, ["CTS%d" % b2, "QABS"], [ndr], skip_group_check=True)
                            MM(nd[:, 256 + u4 * 64:256 + u4 * 64 + 64], KTS[b2][:, u4 * 128:(u4 + 1) * 128], qr, True, True,
                               ["KTS%d" % b2, "QR0"], [ndr], skip_group_check=True)

                    def stE(ub):
                        b2 = ub % 2
                        nd, ndr = banks[4 + b2], "B%d" % (4 + b2)
                        sq = SSQ2[b2]
                        sr = "SSQ%d" % b2
                        ACT(sq[:], sq[:], AF.Sqrt, [sr], [sr], bias=EPS, scale=1.0 / 64)
                        RECIP(sq[:], sq[:], [sr], [sr])
                        sv = SS_[:, ub * 4:ub * 4 + 4, :]
                        TT("dve", sv.rearrange("p u (h q) -> p u h q", q=4),
                           nd[:, 0:256].rearrange("p (u h q) -> p u h q", u=4, q=4),
                           sq[:].unsqueeze(3).to_broadcast([128, 4, 16, 4]), ALU.mult, [ndr, sr], ["SSEQ"])
                        TT("dve", sv, sv, nd[:, 256:512].rearrange("p (u c) -> p u c", u=4), ALU.add, [ndr, "SSEQ"], ["SSEQ"])

                    for ub in range(32):
                        stT(ub)
                        if ub >= 1:
                            stM(ub - 1)
                        if ub >= 2:
                            stE(ub - 2)
                    stM(31)
                    stE(30)
                    stE(31)
                    nd, ndr = bank(4, 6)
                    MM(nd[0:64, 0:64], CSB[:], qa, True, True, ["CBF", "QABS"], [ndr], skip_group_check=True)
                    MM(nd[0:64, 256:320], KRSB[:], qr, True, True, ["KRBF", "QR0"], [ndr], skip_group_check=True)
                    svn = SS_[0:64, 128, :]
                    TT("dve", svn.rearrange("p (h q) -> p h q", q=4), nd[0:64, 0:64].rearrange("p (h q) -> p h q", q=4),
                       RSTN[:].unsqueeze(2).to_broadcast([64, 16, 4]), ALU.mult, [ndr, "RSTN"], ["SSEQ"])
                    TT("dve", svn, svn, nd[0:64, 256:320], ALU.add, [ndr, "SSEQ"], ["SSEQ"])
                    TT("dve", svn, svn, MNEW[:, s, :], ALU.add, ["SSEQ", "MNEW"], ["SSEQ"])
                    RED(MX[:], SS_[:].rearrange("p u c -> p c u"), ALU.max, ["SSEQ"], ["MX"])
                    pm, prm = bank(4, 6)
                    TR(pm[0:64, 0:128], MX[:], CFm("identf"), ["MX"], [prm])
                    RED(MXC[:], pm[0:64, 0:128], ALU.max, [prm], ["MXC"])
                    TSC("dve", DG[:], CFm("identf", 64)[:, 0:64], MXC[:, 0:1], None, ALU.mult, None, ["MXC"], ["DG"])
                    pb, pbr = bank(4, 6)
                    MM(pb[:, 0:64], CFm("onesf", 64), DG[:], True, True, ["DG"], [pbr])
                    CP("dve", MB[:], pb[:, 0:64], [pbr], ["MB"])
                    TT("dve", SS_[:], SS_[:], MB[:].unsqueeze(1).to_broadcast([128, 129, 64]), ALU.subtract, ["SSEQ", "MB"], ["SSEQ"])
                    ACT(PP[:], SS_[:], AF.Exp, ["SSEQ"], ["PSEQ"], scale=float(SM_SCALE))
                    pacc, paccr = banks[6], "B6"
                    plb, plbr = banks[0], "B0"
                    for u in range(128):
                        MM(pacc[:, 0:64], GC[g][:, u, :], PP[:, u, :], u == 0, False, ["GC%d" % g, "PSEQ"], [paccr])
                    MM(pacc[:, 0:64], CNAT[:], PP[0:64, 128, :], False, True, ["CNAT", "PSEQ"], [paccr])
                    for u in range(128):
                        MM(plb[:, 0:64], CBm("onesb"), PP[:, u, :], u == 0, False, ["PSEQ"], [plbr])
                    MM(plb[:, 0:64], CBm("onesb", 64), PP[0:64, 128, :], False, True, ["PSEQ"], [plbr])
                    RECIP(RLB[:], plb[:, 0:64], [plbr], ["RLB"])
                    TT("dve", OLT[:, s, :, :].rearrange("p h q -> p (h q)"), pacc[:, 0:64], RLB[:], ALU.mult, [paccr, "RLB"], ["OLT"])
                for kq in range(KC):
                    po, por = bank()
                    for hh in range(2):
                        h = 2 * kq + hh
                        MM(po[hh * 64:hh * 64 + 64, 0:NSC], WUV[:, h * 64:h * 64 + 64], OLT[:, :, h, :], True, True, ["OLT"], [por])
                    CP("act", ATS[:, kq, :], po[:, 0:NSC], [por], ["ATS"])

            with contextlib.ExitStack() as SL:
                S.barrier()
                L = layer_bufs(SL, NSC)
                for kq in range(KC):
                    wt, wr = wload(wo_d[kq], 1024)
                    for mo in range(KC):
                        po, por = bank()
                        MM(po[:, 0:NSC], wt[:, mo * 128:mo * 128 + 128], ATS[:, kq, :], True, True, [wr, "ATS"], [por])
                        TT("dve", HS[:, mo, :], po[:, 0:NSC], HS[:, mo, :], ALU.add, [por, "HS"], ["HS"])
                UPG = sb("S_UPG", [128, NF, NS, 6], F32, SL)
                UPU = sb("S_UPU", [128, NF, NS, 6], F32, SL)
                FSTG = sb("FSTG", [128, 44, NS, 2], F32, SL)
                DMA("sp", FSTG[:], sff_d[1], "ld0", (), ["FSTG"])
                CP("act", UPG[:, :, :, 0:2], FSTG[:, 0:NF], ["FSTG"], ["UPG"])
                CP("act", UPU[:, :, :, 0:2], FSTG[:, NF:2 * NF], ["FSTG"], ["UPU"])
                conv_ffn(L, 1, HS, "HS", 0, NSC, NS, UPG, UPU, False, ffs_o[1], FSTG, True)
                DMA("sp", ys_o, HS[:], "o_ys", ["HS"], [])

        with contextlib.ExitStack() as PR:
            S.barrier()
            HT = [sb("H%d" % t, [128, KC, TW], F32, PR) for t in range(NT)]
            QLP = sb("QLP", [128, 2, NT, NQ], BF16, PR)
            with contextlib.ExitStack() as PA:
                S.barrier()
                L = layer_bufs(PA, TW)
                UP_ = sb("UP", [128, KC, 1, TW], F32, PA)
                UPG = sb("P_UPG", [128, 2, 1, TW], F32, PA)
                UPU = sb("P_UPU", [128, 2, 1, TW], F32, PA)
                ROPK = sb("ROPK", [32, 2, TS_], F32, PA)
                CPF = sb("CPF", [128, TS_], F32, PA)
                CPB = sb("CPB", [128, TS_], BF16, PA)
                KPF = sb("KPF", [32, TS_], F32, PA)
                KPB = sb("KPB", [32, TS_], BF16, PA)
                TSTGP = sb("TSTGP", [128, KC, 1, 30], F32, PA)
                for t in range(NT):
                    Hh, Hr = HT[t], "H%d" % t
                    DMA("sp", Hh[:], xp_d[t], "ld0", (), [Hr])
                    last = (t == NT - 1)
                    conv_module(L, Hh, Hr, TW, 1, TW, UP_, t == 0, cap_o if last else None, TSTGP)
                    conv_ffn(L, 0, Hh, Hr, 30, TW, 1, UPG, UPU, t == 0, ffp_o[0] if last else None, "o_ffp", False)
                    DMA("sp", ROPK[:, 0, :], cosk_d[:, t, :], "ld0", (), ["ROPE"])
                    DMA("sp", ROPK[:, 1, :], sink_d[:, t, :], "ld0", (), ["ROPE"])
                    shared_kv(L, Hh, Hr, HALO, TS_, ROPK[:, 0, :], ROPK[:, 1, :], CPF[:], KPF[:], CPB[:], KPB[:])
                    DMA("sp", cp_o[t], CPF[:], "o_cp", ["CF32"], [])
                    DMA("sp", krp_o[t], KPF[:], "o_cp", ["KRF32"], [])
                    DMA("sp", xch_in.ap()[0:128, t * TS_:(t + 1) * TS_], CPB[:], "xin", ["CBF"], ["XIN"])
                    DMA("sp", xch_in.ap()[128:160, t * TS_:(t + 1) * TS_], KPB[:], "xin", ["KRBF"], ["XIN"])
                    qlat(L, Hh, Hr, 32, NQ, QLP[:, :, t, :])
            S.op("pool", lambda e: e.collective_compute("AllGather", ALU.bypass, replica_groups=[[0, 1, 2, 3], [4, 5, 6, 7]],
                                                        ins=[xch_in.ap()], outs=[xch_out.ap()]), ["XIN"], ["XOUT"])

            with contextlib.ExitStack() as PB:
                S.barrier()
                NKP = NKC * 128
                CTA = sb("CTA", [128, NKP], BF16, PB)
                KT = sb("KT", [96, NKP], BF16, PB)
                VH = sb("VH", [128, NKC, 65], BF16, PB)
                AOP = sb("AOP", [128, NT, 3, 128], BF16, PB)
                ATT = sb("ATT", [128, NQ], BF16, PB)
                RQP = sb("RQP", [96, 2, NT, NQ], F32, PB)
                THR = sb("THR", [128, NT * NKC], F32, PB)
                PTb = [sb("PTb%d" % i, [128, NQ], BF16, PB) for i in range(4)]
                QT2 = [sb("QT%d" % i, [96, NQ], BF16, PB) for i in range(2)]
                KSQ = [sb("KSQ%d" % i, [64, 512], BF16, PB) for i in range(2)]
                KRS_ = [sb("KRS%d" % i, [64, 512], F32, PB) for i in range(2)]
                RL = sb("RL", [128, 1], F32, PB)
                Q = {}
                Q["QSQ"] = sb("QP_SQ", [96, NQ], BF16, PB)
                Q["QRS"] = sb("QP_RS", [96, NQ], F32, PB)
                Q["QX"] = sb("QP_X", [96, NQ], F32, PB)
                Q["QXB"] = sb("QP_XB", [96, NQ], BF16, PB)
                DMA("sp", RQP[:, 0], cosq_d, "ld0", (), ["ROPQ"])
                DMA("sp", RQP[:, 1], sinq_d, "ld0", (), ["ROPQ"])
                DMA("sp", THR[:], thr_d, "ld0", (), ["THR"])
                MEMSET("dve", CTA[:, NPOS:NKP], 0.0, ["CTA"])
                MEMSET("dve", KT[:, NPOS:NKP], 0.0, ["KT"])
                MEMSET("dve", VH[:, :, 64:65], 1.0, ["VH"])
                for gt in range(24):
                    r, t = gt % 4, gt // 4
                    DMA("sp", CTA[:, gt * TS_:(gt + 1) * TS_], xch_out.ap()[r * 160:r * 160 + 128, t * TS_:(t + 1) * TS_],
                        "ld1", ["XOUT"], ["CTA"])
                    DMA("sp", KT[64:96, gt * TS_:(gt + 1) * TS_], xch_out.ap()[r * 160 + 128:r * 160 + 160, t * TS_:(t + 1) * TS_],
                        "ld1", ["XOUT"], ["KT"])
                o_gk = VOFF["gk64"][0]
                qsub = [(0, 128), (128, 128), (256, NQ - 256)]
                pctr = 0
                qgen = [None]
                for h in range(16):
                    def ka(kt):
                        n = 512 if kt < 16 else NKP - 16 * 512
                        cs = kt * 512
                        pk, prk = bank(0, 4)
                        MM(pk[0:64, 0:n], WUK[:, h * 64:h * 64 + 64], CTA[:, cs:cs + n], True, True, ["CTA"], [prk])
                        ACT(KSQ[kt % 2][:, 0:n], pk[0:64, 0:n], AF.Square, [prk], ["KSQ%d" % (kt % 2)])
                        return (kt, n, cs, pk, prk)

                    def kb(kt, n, cs, pk, prk):
                        pm, prm = bank(0, 4)
                        kr_ = KRS_[kt % 2]
                        krr = "KRS%d" % (kt % 2)
                        MM(pm[0:64, 0:n], CBm("o64", 64), KSQ[kt % 2][:, 0:n], True, True, ["KSQ%d" % (kt % 2)], [prm])
                        rstd_lnexp(pm[0:64, 0:n], kr_[:, 0:n], prm, krr)
                        STT("dve", KT[0:64, cs:cs + n], pk[0:64, 0:n], VEC[0:64, o_gk:o_gk + 1], kr_[:, 0:n], ALU.mult, ALU.mult,
                            [prk, krr], ["KT"])

                    kpend = []
                    for kt in range(17):
                        kpend.append(ka(kt))
                        if len(kpend) > 1:
                            kb(*kpend.pop(0))
                    while kpend:
                        kb(*kpend.pop(0))
                    for vb in range(9):
                        nchk = 8 if vb < 8 else 1
                        pv, prv = bank(0, 4)
                        for ci in range(nchk):
                            kc = vb * 8 + ci
                            MM(pv[:, ci * 64:ci * 64 + 64], CTA[:, kc * 128:kc * 128 + 128], WUV[:, h * 64:h * 64 + 64], True, True,
                               ["CTA"], [prv], skip_group_check=True)
                        CP("act", VH[:, vb * 8:vb * 8 + nchk, 0:64], pv[:, 0:nchk * 64].rearrange("p (c d) -> p c d", d=64), [prv], ["VH"])
                    for t in range(NT):
                        qi_ = h * NT + t
                        QTt, QTr = QT2[qi_ % 2], "QT%d" % (qi_ % 2)
                        if qgen[0] is None:
                            qhead(Q, h, QLP[:, :, t, :], NQ, RQP[:, 0, t, :], RQP[:, 1, t, :], QTt[:], QTr, "ROPQ")
                        else:
                            for _ in qgen[0]:
                                pass
                        nh, nt_ = (h, t + 1) if t + 1 < NT else (h + 1, 0)
                        if nh < 16:
                            nq = qi_ + 1
                            qgen[0] = qhead_gen(Q, nh, QLP[:, :, nt_, :], NQ, RQP[:, 0, nt_, :], RQP[:, 1, nt_, :],
                                                QT2[nq % 2][:], "QT%d" % (nq % 2), "ROPQ")
                        else:
                            qgen[0] = iter(())
                        nkc = min(NKC, -(-((4 * t + 4) * TS_) // 128))
                        first_masked = max(0, (4 * t * TS_ - 2 - 127 + 127) // 128)
                        def s_stage(kc, pctr):
                            kn = 128 if kc < NKC - 1 else NPOS - 128 * (NKC - 1)
                            pst, pstr = bank(0, 3)
                            MM(pst[0:kn, 0:NQ], KT[:, kc * 128:kc * 128 + kn], QTt[:], True, True, ["KT", QTr], [pstr])
                            pb_ = PTb[pctr % 4]
                            pbr = "PTb%d" % (pctr % 4)
                            ACT(pb_[0:kn, :], pst[0:kn, 0:NQ], AF.Exp, [pstr], [pbr], scale=float(SM_SCALE))
                            if kc >= first_masked:
                                col = t * NKC + kc
                                STT("dve", pb_[0:kn, :], CFm("iota")[0:kn, :], THR[0:kn, col:col + 1], pb_[0:kn, :], ALU.is_ge, ALU.mult,
                                    [pbr, "THR"], [pbr])
                            return (kc, kn, pb_, pbr)

                        def pv_stage(kc, kn, pb_, pbr):
                            for qi, (q0, qn) in enumerate(qsub):
                                MM(banks[3 + qi][0:qn, 0:65], pb_[0:kn, q0:q0 + qn], VH[0:kn, kc, :], kc == 0, kc == nkc - 1,
                                   [pbr, "VH"], ["B%d" % (3 + qi)])

                        pend = []
                        for kc in range(nkc):
                            pend.append(s_stage(kc, pctr))
                            pctr += 1
                            if kc in (2, 5, 8):
                                next(qgen[0], None)
                            if len(pend) > 2:
                                pv_stage(*pend.pop(0))
                        while pend:
                            pv_stage(*pend.pop(0))
                        for qi, (q0, qn) in enumerate(qsub):
                            br = "B%d" % (3 + qi)
                            TSC("dve", RL[0:qn, :], banks[3 + qi][0:qn, 64:65], 1e-30, None, ALU.max, None, [br], ["RL"])
                            RECIP(RL[0:qn, :], RL[0:qn, :], ["RL"], ["RL"])
                            TSC("dve", AOP[0:qn, t, qi, (h % 2) * 64:(h % 2) * 64 + 64], banks[3 + qi][0:qn, 0:64], RL[0:qn, 0:1], None,
                                ALU.mult, None, [br, "RL"], ["AOP"])
                    if h % 2 == 1:
                        kq = h // 2
                        wt, wr = wload(wo_d[kq], 1024)
                        for t in range(NT):
                            for qi, (q0, qn) in enumerate(qsub):
                                TR(bankb[:, q0:q0 + qn], AOP[0:qn, t, qi, :], CBm("identb")[0:qn, 0:qn], ["AOP"], ["BB"])
                            CP("act", ATT[:], bankb[:, 0:NQ], ["BB"], ["ATT"])
                            for mo in range(KC):
                                po, por = bank(0, 3)
                                MM(po[:, 0:NQ], wt[:, mo * 128:mo * 128 + 128], ATT[:], True, True, [wr, "ATT"], [por])
                                TT("dve", HT[t][:, mo, 32:TW], po[:, 0:NQ], HT[t][:, mo, 32:TW], ALU.add, [por, "H%d" % t], ["H%d" % t])

            with contextlib.ExitStack() as PC:
                S.barrier()
                L = layer_bufs(PC, TW)
                UPG = sb("P_UPG", [128, 2, 1, TW], F32, PC)
                UPU = sb("P_UPU", [128, 2, 1, TW], F32, PC)
                for t in range(NT):
                    Hh, Hr = HT[t], "H%d" % t
                    last = (t == NT - 1)
                    conv_ffn(L, 1, Hh, Hr, 32, TW, 1, UPG, UPU, t == 0, ffp_o[1] if last else None, "o_ffp", False)
                    DMA("sp", yp_o[t], Hh[:, :, HALO:TW], "o_yp", [Hr], [])
        S.emit()
    return nc


def kernel(x_prompt, x_sample, state_conv_a, state_ffn_conv, cache_kv_latent, cache_k_rope, page_table,
           meta_tokens, norm_mix, norm_ffn,
           a_w_pw1, a_b_pw1, a_w_dw, a_b_dw, a_ln_g, a_ln_b, a_w_pw2, a_b_pw2,
           ffn_w_up, ffn_w_dw, ffn_w_down,
           kv_norm, mla_w_dkv, mla_lat_norm, mla_w_kr, mla_knorm_rope, mla_w_uk, mla_w_uv, mla_knorm_nope,
           mla_w_dq, mla_q_lat_norm, mla_w_uq, mla_qnorm_nope, mla_qnorm_rope, mla_w_o):
    f32 = np.float32
    A = lambda a: np.asarray(a)
    x_prompt, x_sample = A(x_prompt), A(x_sample)
    n_phys = int(A(cache_kv_latent).shape[0])
    cache_c = np.ascontiguousarray(A(cache_kv_latent), f32).reshape(n_phys * 8, 2048)
    cache_kr = np.ascontiguousarray(A(cache_k_rope), f32).reshape(n_phys * 2, 2048)
    page_table = A(page_table).astype(np.int32)

    vec = np.zeros((128, NVEC), f32)
    def putv(name, m):
        o, w = VOFF[name]
        m = np.asarray(m, f32)
        vec[:m.shape[0], o:o + w] = m.reshape(m.shape[0], w)
    putv("nm0", _fm(A(norm_mix)[0], 8)); putv("nm1", _fm(A(norm_mix)[1], 8))
    putv("nf0", _fm(A(norm_ffn)[0], 8)); putv("nf1", _fm(A(norm_ffn)[1], 8))
    putv("kvn", _fm(A(kv_norm), 8))
    putv("b1", _fm(A(a_b_pw1)[0], 16)); putv("bdw", _fm(A(a_b_dw)[0], 8))
    putv("lng", _fm(A(a_ln_g)[0], 8)); putv("lnb", _fm(A(a_ln_b)[0], 8)); putv("b2", _fm(A(a_b_pw2)[0], 8))
    putv("wdwa", np.ascontiguousarray(A(a_w_dw)[0].reshape(31, 8, 128).transpose(2, 1, 0)).reshape(128, 8 * 31))
    for l in range(2):
        putv("wdwf%d" % l, np.ascontiguousarray(A(ffn_w_dw)[l].reshape(3, 44, 128).transpose(2, 1, 0)).reshape(128, 44 * 3))
    putv("latn", A(mla_lat_norm).reshape(128, 1))
    putv("knr", A(mla_knorm_rope).reshape(32, 1))
    putv("qln", _fm(A(mla_q_lat_norm)[0], 2))
    putv("g96", np.concatenate([A(mla_qnorm_nope)[0], A(mla_qnorm_rope)[0]]).reshape(96, 1))
    putv("gk64", A(mla_knorm_nope).reshape(64, 1))
    cb = _consts_bf()
    cf = _consts_f()

    W1 = A(a_w_pw1)[0].astype(f32)
    w1r = W1.reshape(8, 128, 2, 8, 128)
    w1l = np.ascontiguousarray(w1r.transpose(3, 1, 0, 2, 4)).reshape(8, 128, 2048)
    W2 = A(a_w_pw2)[0].astype(f32).reshape(8, 128, 4, 2, 128)
    w2l = np.ascontiguousarray(W2.transpose(2, 1, 3, 0, 4)).reshape(4, 128, 2048)
    WU = A(ffn_w_up).astype(f32).reshape(2, 8, 128, 2, NF, 128)
    wupl = np.ascontiguousarray(WU.transpose(0, 4, 2, 1, 3, 5)).reshape(2, NF, 128, 2048)
    WD = A(ffn_w_down).astype(f32).reshape(2, 2, 11, 128, 8, 128)
    wdnl = np.ascontiguousarray(WD.transpose(0, 4, 1, 3, 2, 5)).reshape(2, 16, 128, 1408)
    wol = np.ascontiguousarray(A(mla_w_o)[0].astype(f32)).reshape(8, 128, 1024)
    wdq = np.ascontiguousarray(A(mla_w_dq)[0].astype(f32).reshape(8, 128, 256).transpose(1, 0, 2)).reshape(128, 2048)
    wuq = np.ascontiguousarray(A(mla_w_uq)[0].astype(f32).reshape(2, 128, 1536))
    wdkv = np.ascontiguousarray(A(mla_w_dkv).astype(f32).reshape(8, 128, 128).transpose(1, 0, 2)).reshape(128, 1024)
    wkr = np.ascontiguousarray(A(mla_w_kr).astype(f32).reshape(8, 128, 32).transpose(1, 0, 2)).reshape(128, 256)
    wuk = np.ascontiguousarray(A(mla_w_uk).astype(f32).reshape(128, 1024))
    wuv = np.ascontiguousarray(A(mla_w_uv).astype(f32).reshape(128, 1024))
    wukT = np.ascontiguousarray(A(mla_w_uk).astype(f32).transpose(2, 1, 0)).reshape(64, 2048)

    iota = np.arange(TW)
    masknew = np.full((64, NS, 64), -2000.0, f32)
    for s in range(NS):
        for k in range(4):
            for q in range(4):
                if k <= q:
                    masknew[s * 4 + k, s, np.arange(16) * 4 + q] = 0.0
    pos_s = PAST + (np.arange(NSC) % 4)
    cks, sks = _rope_tab(pos_s)
    cosqs = np.ones((96, NSC), f32); sinqs = np.zeros((96, NSC), f32)
    cosqs[64:], sinqs[64:] = cks, sks

    hp_all = np.concatenate([np.broadcast_to(A(meta_tokens).astype(f32)[None], (2, NMETA, D)), x_prompt.astype(f32)], axis=1)
    shared = dict(vec=vec, cb=cb, cf=cf, w1l=w1l, w2l=w2l, wupl=wupl, wdnl=wdnl, wol=wol, wdq=wdq, wuq=wuq, wdkv=wdkv,
                  wkr=wkr, wuk=wuk, wuv=wuv, wukT=wukT, cache_c=cache_c, cache_kr=cache_kr, masknew=masknew,
                  cosqs=cosqs, sinqs=sinqs, cosks=cks, sinks=sks)
    in_maps = []
    for c in range(8):
        b, j = c // 4, c % 4
        xp = np.zeros((NT, 128, KC, TW), f32)
        cosq = np.ones((96, NT, NQ), f32); sinq = np.zeros((96, NT, NQ), f32)
        cosk = np.zeros((32, NT, TS_), f32); sink = np.zeros((32, NT, TS_), f32)
        thr = np.zeros((128, NT * NKC), f32)
        for t in range(NT):
            g = 4 * t + j
            p0 = g * TS_ - HALO
            win = np.zeros((TW, D), f32)
            lo = max(0, p0)
            win[lo - p0:] = hp_all[b, lo:p0 + TW]
            xp[t] = win.T.reshape(KC, 128, TW).transpose(1, 0, 2)
            posq = p0 + 32 + np.arange(NQ)
            cq, sq_ = _rope_tab(posq)
            cosq[64:, t], sinq[64:, t] = cq, sq_
            cosk[:, t], sink[:, t] = cq[:, 2:], sq_[:, 2:]
            thr[:, t * NKC:(t + 1) * NKC] = (128 * np.arange(NKC) - (p0 + 32))[None, :]
        sl = slice(NS * c, NS * c + NS)
        xs = np.ascontiguousarray(x_sample[sl].astype(f32).reshape(NSC, D).T.reshape(KC, 128, NSC).transpose(1, 0, 2))
        sca = np.ascontiguousarray(A(state_conv_a)[0, sl].astype(f32).transpose(2, 0, 1).reshape(KC, 128, NS, 30).transpose(1, 0, 2, 3))
        sff = np.ascontiguousarray(A(state_ffn_conv)[:, sl].astype(f32).transpose(0, 3, 1, 2).reshape(2, 44, 128, NS, 2).transpose(0, 2, 1, 3, 4))
        cmask = np.ones((128, HALO), f32)
        if j == 0:
            cmask[:] = 0.0
        m = dict(shared)
        m.update(xp=xp, xs=xs, sca=sca, sff=sff, ptT=np.ascontiguousarray(page_table[sl].T), colmask=cmask,
                 cosq=cosq, sinq=sinq, cosk=cosk, sink=sink, thr=thr)
        in_maps.append(m)

    nc = build_program(n_phys)
    res = run_bass_kernel_spmd(nc, in_maps, core_ids=list(range(8)))
    R = res.results

    y_prompt = np.zeros((2, NPOS, D), f32)
    kvp = np.zeros((2, NPOS, 128), f32)
    krp = np.zeros((2, NPOS, 32), f32)
    y_sample = np.zeros((128, 4, D), f32)
    cas = np.zeros((1, 128, 30, D), f32)
    ffs = np.zeros((2, 128, 2, 2 * DFF), f32)
    kvs = np.zeros((128, 4, 128), f32)
    krs = np.zeros((128, 4, 32), f32)
    cap = np.zeros((1, 2, 30, D), f32)
    ffp = np.zeros((2, 2, 2, 2 * DFF), f32)
    for c in range(8):
        b, j = c // 4, c % 4
        r = R[c]
        for t in range(NT):
            g = 4 * t + j
            y_prompt[b, g * TS_:(g + 1) * TS_] = np.asarray(r["yp"][t]).transpose(1, 0, 2).reshape(D, TS_).T
            kvp[b, g * TS_:(g + 1) * TS_] = np.asarray(r["cp"][t]).T
            krp[b, g * TS_:(g + 1) * TS_] = np.asarray(r["krp"][t]).T
        sl = slice(NS * c, NS * c + NS)
        y_sample[sl] = np.asarray(r["ys"]).transpose(1, 0, 2).reshape(D, NS, 4).transpose(1, 2, 0)
        cas[0, sl] = np.asarray(r["cas"]).transpose(1, 0, 2, 3).reshape(D, NS, 30).transpose(1, 2, 0)
        ffs[:, sl] = np.asarray(r["ffs"]).transpose(0, 2, 1, 3, 4).reshape(2, 2 * DFF, NS, 2).transpose(0, 2, 3, 1)
        kvs[sl] = np.asarray(r["cs"]).T.reshape(NS, 4, 128)
        krs[sl] = np.asarray(r["krs"]).T.reshape(NS, 4, 32)
        if j == 3:
            cap[0, b] = np.asarray(r["cap"]).transpose(1, 0, 2).reshape(D, 30).T
            ffp[:, b] = np.asarray(r["ffp"]).transpose(0, 2, 1, 3).reshape(2, 2 * DFF, 2).transpose(0, 2, 1)
    return (np.ascontiguousarray(y_prompt[:, NMETA:]), y_sample, cap, cas, ffp, ffs, kvp, krp, kvs, krs)
```

```python
import contextlib
import numpy as np
import concourse.bass as bass
import concourse.mybir as mybir
from concourse.bass_utils import run_bass_kernel_spmd

F32 = mybir.dt.float32
BF16 = mybir.dt.bfloat16
I32 = mybir.dt.int32
ALU = mybir.AluOpType
AF = mybir.ActivationFunctionType
AX = mybir.AxisListType

D = 1024
KC = 8
SEQ = 8192
NMETA = 16
NPOS = SEQ + NMETA
TS_ = 342
HALO = 34
TW = TS_ + HALO
NT = 6
NQ = TW - 32
NS = 16
NSC = 64
DFF = 2816
NF = 22
NPAGE = 128
PAST = 16384
EPS = 1e-6
SM_SCALE = 1.0 / np.sqrt(96.0)
NKC = 65
EPOCH = 12000
SAME_ENGINE_SYNC = True


class Sched:
    ENG = ("pe", "act", "dve", "pool", "sp")

    def __init__(self, nc):
        self.nc = nc
        self.ops = {e: [] for e in self.ENG}
        self.count = {e: 0 for e in self.ENG}
        self.waited = {e: {} for e in self.ENG}
        self.last_write = {}
        self.readers = {}
        self.lane_count = {}
        self.sems = {}
        self.all_sem_keys = []

    def _need(self, eng, tok):
        key, val = tok
        if key[0] == eng and (not SAME_ENGINE_SYNC or eng == "pe"):
            return
        w = self.waited[eng]
        if key[0] in self.ENG:
            cur = w.get(key[0], (-1, 0))
            if (key[1], val) <= cur:
                return
            w[key[0]] = (key[1], val)
        else:
            if w.get(key, 0) >= val:
                return
            w[key] = val
        self.ops[eng].append(("wait", key, val))

    def _deps(self, eng, reads, writes):
        toks = []
        for r in reads:
            t = self.last_write.get(r)
            if t:
                toks.append(t)
            if r[0] == "B" and (r[1:].isdigit() or r == "BB"):
                toks.extend(x for x in self.readers.get(r, ()) if x[0][0] != eng)
        for r in writes:
            t = self.last_write.get(r)
            if t:
                toks.append(t)
            toks.extend(self.readers.get(r, ()))
        for t in toks:
            self._need(eng, t)

    def _commit(self, tok, reads, writes):
        for r in writes:
            self.last_write[r] = tok
            self.readers[r] = []
        for r in reads:
            if r in writes:
                continue
            self.readers.setdefault(r, []).append(tok)

    def op(self, eng, fn, reads=(), writes=()):
        self._deps(eng, reads, writes)
        self.count[eng] += 1
        idx = self.count[eng]
        key = (eng, (idx - 1) // EPOCH)
        val = (idx - 1) % EPOCH + 1
        if key not in self.sems:
            self.sems[key] = None
            self.all_sem_keys.append(key)
        self.ops[eng].append(("op", fn, key, 1))
        self._commit((key, val), reads, writes)

    def dma(self, q, fn, lane, reads=(), writes=()):
        self._deps(q, reads, writes)
        self.rr = getattr(self, "rr", {})
        self.rr[q] = self.rr.get(q, 0) + 1
        lane = "%s%d" % (q, self.rr[q] % 10)
        key = ("lane", lane)
        if self.lane_count.get(lane, 0) > 0:
            self._need(q, (key, self.lane_count[lane]))
        if key not in self.sems:
            self.sems[key] = None
            self.all_sem_keys.append(key)
        self.lane_count[lane] = self.lane_count.get(lane, 0) + 16
        self.ops[q].append(("op", fn, key, 16))
        self._commit((key, self.lane_count[lane]), reads, writes)

    def barrier(self):
        toks = []
        for e in self.ENG:
            idx = self.count[e]
            if idx > 0:
                toks.append(((e, (idx - 1) // EPOCH), (idx - 1) % EPOCH + 1))
        for lane, cnt in self.lane_count.items():
            toks.append((("lane", lane), cnt))
        for e in self.ENG:
            for t in toks:
                self._need(e, t)

    def all_wait(self, res):
        t = self.last_write.get(res)
        if t:
            for e in self.ENG:
                self._need(e, t)

    def emit(self):
        nc = self.nc
        with contextlib.ExitStack() as st:
            for i, key in enumerate(self.all_sem_keys):
                self.sems[key] = st.enter_context(nc.semaphore("s%d" % i))
            for key in self.all_sem_keys:
                if key[0] == "lane":
                    self._need("sp", (key, self.lane_count[key[1]]))
            blk = st.enter_context(nc.Block())

            def run(engname):
                def body(e):
                    for item in self.ops[engname]:
                        if item[0] == "wait":
                            e.wait_ge(self.sems[item[1]], item[2])
                        else:
                            item[1](e).then_inc(self.sems[item[2]], item[3])
                return body

            blk.tensor(run("pe"))
            blk.scalar(run("act"))
            blk.vector(run("dve"))
            blk.gpsimd(run("pool"))
            blk.sync(run("sp"))


VEC_LAYOUT = [("nm0", 8), ("nm1", 8), ("nf0", 8), ("nf1", 8), ("kvn", 8), ("b1", 16), ("bdw", 8), ("lng", 8),
              ("lnb", 8), ("b2", 8), ("wdwa", 8 * 31), ("wdwf0", 44 * 3), ("wdwf1", 44 * 3), ("latn", 1),
              ("knr", 1), ("qln", 2), ("g96", 1), ("gk64", 1)]
VOFF = {}
_o = 0
for _n, _w in VEC_LAYOUT:
    VOFF[_n] = (_o, _w)
    _o += _w
NVEC = _o

CB_LAYOUT = [("o1024", 128), ("o256", 128), ("o128", 128), ("blk96", 96), ("o64", 64), ("p96", 96), ("identb", 128),
             ("onesb", 128)]
CBOFF = {}
_o = 0
for _n, _w in CB_LAYOUT:
    CBOFF[_n] = (_o, _w)
    _o += _w
NCB = _o


def _fm(v, nchunk):
    return np.ascontiguousarray(np.asarray(v, np.float32).reshape(nchunk, 128).T)


def _consts_bf():
    c = np.zeros((128, NCB), np.float32)
    def put(name, m):
        o, w = CBOFF[name]
        c[:m.shape[0], o:o + m.shape[1]] = m
    put("o1024", np.full((128, 128), 1.0 / 1024, np.float32))
    put("o256", np.full((128, 128), 1.0 / 256, np.float32))
    put("o128", np.full((128, 128), 1.0 / 128, np.float32))
    blk = np.zeros((96, 96), np.float32)
    blk[:64, :64] = 1.0 / 64
    blk[64:, 64:] = 1.0 / 32
    put("blk96", blk)
    put("o64", np.full((64, 64), 1.0 / 64, np.float32))
    p96 = np.zeros((96, 96), np.float32)
    for m in range(16):
        p96[64 + m + 16, 64 + m] = -1.0
        p96[64 + m, 64 + m + 16] = 1.0
    put("p96", p96)
    put("identb", np.eye(128, dtype=np.float32))
    put("onesb", np.ones((128, 128), np.float32))
    return c


CF_LAYOUT = [("identf", 128), ("p32", 32), ("o32", 32), ("onesf", 128), ("iota", NQ)]
CFOFF = {}
_o = 0
for _n, _w in CF_LAYOUT:
    CFOFF[_n] = (_o, _w)
    _o += _w
NCF = _o


def _consts_f():
    c = np.zeros((128, NCF), np.float32)
    def put(name, m):
        o, w = CFOFF[name]
        c[:m.shape[0], o:o + m.shape[1]] = m
    put("identf", np.eye(128, dtype=np.float32))
    p32 = np.zeros((32, 32), np.float32)
    for m in range(16):
        p32[m + 16, m] = -1.0
        p32[m, m + 16] = 1.0
    put("p32", p32)
    put("o32", np.full((32, 32), 1.0 / 32, np.float32))
    put("onesf", np.ones((128, 128), np.float32))
    put("iota", (np.arange(NQ)[None, :] - np.arange(128)[:, None]).astype(np.float32))
    return c


def _rope_tab(pos):
    inv = (10000.0 ** (-np.arange(16, dtype=np.float32) / 16)).astype(np.float32)
    ang = pos.astype(np.float32)[None, :] * inv[:, None]
    cos = np.cos(ang).astype(np.float32)
    sin = np.sin(ang).astype(np.float32)
    return np.concatenate([cos, cos], 0), np.concatenate([sin, sin], 0)


def build_program(n_phys):
    nc = bass.Bass("TRN2", target_bir_lowering=False)
    S = Sched(nc)

    def din(name, shape, dt=F32):
        return nc.dram_tensor(name, list(shape), dt, kind="ExternalInput").ap()

    def dout(name, shape, dt=F32):
        return nc.dram_tensor(name, list(shape), dt, kind="ExternalOutput").ap()

    xp_d = din("xp", [NT, 128, KC, TW])
    xs_d = din("xs", [128, KC, NSC])
    sca_d = din("sca", [128, KC, NS, 30])
    sff_d = din("sff", [2, 128, 44, NS, 2])
    cc_d = din("cache_c", [n_phys * 8, 2048])
    ckr_d = din("cache_kr", [n_phys * 2, 2048])
    pt_d = din("ptT", [128, NS], I32)
    cmask_d = din("colmask", [128, HALO])
    cosq_d = din("cosq", [96, NT, NQ])
    sinq_d = din("sinq", [96, NT, NQ])
    cosk_d = din("cosk", [32, NT, TS_])
    sink_d = din("sink", [32, NT, TS_])
    cosqs_d = din("cosqs", [96, NSC])
    sinqs_d = din("sinqs", [96, NSC])
    cosks_d = din("cosks", [32, NSC])
    sinks_d = din("sinks", [32, NSC])
    thr_d = din("thr", [128, NT * NKC])
    mnew_d = din("masknew", [64, NS, 64])
    vec_d = din("vec", [128, NVEC])
    cb_d = din("cb", [128, NCB])
    cf_d = din("cf", [128, NCF])
    w1_d = din("w1l", [8, 128, 2048])
    w2_d = din("w2l", [4, 128, 2048])
    wup_d = din("wupl", [2, NF, 128, 2048])
    wdn_d = din("wdnl", [2, 16, 128, 1408])
    wo_d = din("wol", [8, 128, 1024])
    wdq_d = din("wdq", [128, 2048])
    wuq_d = din("wuq", [2, 128, 1536])
    wdkv_d = din("wdkv", [128, 1024])
    wkr_d = din("wkr", [128, 256])
    wuk_d = din("wuk", [128, 1024])
    wuv_d = din("wuv", [128, 1024])
    wukT_d = din("wukT", [64, 2048])

    yp_o = dout("yp", [NT, 128, KC, TS_])
    ys_o = dout("ys", [128, KC, NSC])
    cap_o = dout("cap", [128, KC, 30])
    cas_o = dout("cas", [128, KC, NS, 30])
    ffp_o = dout("ffp", [2, 128, 44, 2])
    ffs_o = dout("ffs", [2, 128, 44, NS, 2])
    cp_o = dout("cp", [NT, 128, TS_])
    krp_o = dout("krp", [NT, 32, TS_])
    cs_o = dout("cs", [128, NSC])
    krs_o = dout("krs", [32, NSC])

    xch_in = nc.dram_tensor("xch_in", [160, NT * TS_], BF16)
    xch_out = nc.dram_tensor("xch_out", [640, NT * TS_], BF16)

    def MM(out, lhsT, rhs, start, stop, R, W, **kw):
        S.op("pe", lambda e: e.matmul(out, lhsT=lhsT, rhs=rhs, start=start, stop=stop, **kw), R, W)

    def TR(out, in_, ident, R, W):
        S.op("pe", lambda e: e.transpose(out=out, in_=in_, identity=ident), R, W)

    def ACT(out, in_, func, R, W, bias=None, scale=None):
        kw = {}
        if bias is not None:
            kw["bias"] = bias
        if scale is not None:
            kw["scale"] = scale
        S.op("act", lambda e: e.activation(out=out, in_=in_, func=func, **kw), R, W)

    def TSC(eng, out, in0, s1, s2, op0, op1, R, W):
        if op1 is None:
            S.op(eng, lambda e: e.tensor_scalar(out=out, in0=in0, scalar1=s1, scalar2=None, op0=op0), R, W)
        else:
            S.op(eng, lambda e: e.tensor_scalar(out=out, in0=in0, scalar1=s1, scalar2=s2, op0=op0, op1=op1), R, W)

    def STT(eng, out, in0, scalar, in1, op0, op1, R, W):
        S.op(eng, lambda e: e.scalar_tensor_tensor(out=out, in0=in0, scalar=scalar, in1=in1, op0=op0, op1=op1), R, W)

    def TT(eng, out, in0, in1, op, R, W):
        S.op(eng, lambda e: e.tensor_tensor(out=out, in0=in0, in1=in1, op=op), R, W)

    def CP(eng, out, in_, R, W):
        if eng == "act":
            S.op("act", lambda e: e.activation(out=out, in_=in_, func=AF.Copy), R, W)
        else:
            S.op(eng, lambda e: e.tensor_copy(out=out, in_=in_), R, W)

    def RECIP(out, in_, R, W):
        S.op("dve", lambda e: e.reciprocal(out=out, in_=in_), R, W)

    def MEMSET(eng, ap, val, W):
        S.op(eng, lambda e: e.memset(ap, val), (), W)

    def RED(out, in_, op, R, W):
        S.op("dve", lambda e: e.tensor_reduce(out=out, in_=in_, axis=AX.X, op=op), R, W)

    def DMA(q, out, in_, lane, R, W):
        S.dma(q, lambda e: e.dma_start(out=out, in_=in_), lane, R, W)

    with contextlib.ExitStack() as G:
        nctr = [0]

        def sb(name, shape, dt, st=G):
            nctr[0] += 1
            return st.enter_context(nc.sbuf_tensor("%s_%d" % (name, nctr[0]), list(shape), dt))

        banks = [G.enter_context(nc.psum_tensor("B%d" % i, [128, 512], F32)) for i in range(7)]
        bankb = G.enter_context(nc.psum_tensor("BB", [128, 1024], BF16))
        bctr = [0]

        def bank(lo=0, hi=7):
            i = lo + bctr[0] % (hi - lo)
            bctr[0] += 1
            return banks[i], "B%d" % i

        VEC = sb("VEC", [128, NVEC], F32)
        CB = sb("CB", [128, NCB], BF16)
        CF = sb("CF", [128, NCF], F32)
        WDQ = sb("WDQ", [128, 2048], BF16)
        WUQ = sb("WUQ", [128, 2, 1536], BF16)
        WDKV = sb("WDKV", [128, 1024], BF16)
        WKR = sb("WKR", [128, 256], BF16)
        WUK = sb("WUK", [128, 1024], BF16)
        WUV = sb("WUV", [128, 1024], BF16)
        DMA("sp", VEC[:], vec_d, "c0", (), ["VEC"])
        DMA("sp", CF[:], cf_d, "c0", (), ["CF"])
        DMA("pool", CB[:], cb_d, "c1", (), ["CB"])
        DMA("pool", WDQ[:], wdq_d, "c1", (), ["WDQ"])
        DMA("pool", WUQ[:, 0, :], wuq_d[0], "c1", (), ["WUQ"])
        DMA("pool", WUQ[:, 1, :], wuq_d[1], "c1", (), ["WUQ"])
        DMA("pool", WDKV[:], wdkv_d, "c1", (), ["WDKV"])
        DMA("pool", WKR[:], wkr_d, "c1", (), ["WKR"])
        DMA("pool", WUK[:], wuk_d, "c1", (), ["WUK"])
        DMA("pool", WUV[:], wuv_d, "c1", (), ["WUV"])
        for r in ("VEC", "CF", "CB", "WDQ", "WUQ", "WDKV", "WKR", "WUK", "WUV"):
            S.all_wait(r)

        def V(name, a=None, b=None):
            o, w = VOFF[name]
            if a is None:
                return VEC[:, o:o + w]
            return VEC[:, o + a:o + b]

        def CBm(name, rows=128):
            o, w = CBOFF[name]
            return CB[0:rows, o:o + w]

        def CFm(name, rows=128):
            o, w = CFOFF[name]
            return CF[0:rows, o:o + w]

        NSLOT = 4
        WS = [sb("WS%d" % i, [128, 2048], BF16) for i in range(NSLOT)]
        wctr = [0]

        def wload(src, ne):
            i = wctr[0] % NSLOT
            wctr[0] += 1
            DMA("pool", WS[i][:, 0:ne], src, "w%d" % i, (), ["WS%d" % i])
            return WS[i], "WS%d" % i

        def rstd_from_ms(ps_ap, rs_ap, Rps, Wrs):
            ACT(rs_ap, ps_ap, AF.Sqrt, [Rps], [Wrs], bias=EPS, scale=1.0)
            RECIP(rs_ap, rs_ap, [Wrs], [Wrs])

        def rstd_lnexp(ps_ap, rs_ap, Rps, Wrs):
            ACT(rs_ap, ps_ap, AF.Ln, [Rps], [Wrs], bias=EPS, scale=1.0)
            ACT(rs_ap, rs_ap, AF.Exp, [Wrs], [Wrs], scale=-0.5)

        def rmsnorm(L, Hh, Hres, c0, n, gain):
            ACT(L["SQ"][:, :, 0:n], Hh[:, :, c0:c0 + n], AF.Square, [Hres], ["SQ"])
            ps, pr = bank()
            for k in range(KC):
                MM(ps[:, 0:n], CBm("o1024"), L["SQ"][:, k, 0:n], k == 0, k == KC - 1, ["SQ"], [pr])
            rstd_from_ms(ps[:, 0:n], L["RS"][:, 0:n], pr, "RS")
            for k in range(KC):
                STT("dve", L["XN"][:, k, 0:n], Hh[:, k, c0:c0 + n], gain[:, k:k + 1], L["RS"][:, 0:n],
                    ALU.mult, ALU.mult, [Hres, "RS"], ["XN"])

        def conv_module(L, Hh, Hres, N, Sq, Lw, U, mask_first, tail_out, tail_lane):
            Lo = Lw - 30
            n = Sq * Lo
            newc = N // Sq
            rmsnorm(L, Hh, Hres, 0, N, V("nm0"))
            for m in range(KC):
                wt, wr = wload(w1_d[m], 2048)
                pa, pra = bank()
                pg, prg = bank()
                for k in range(KC):
                    MM(pa[:, 0:N], wt[:, k * 256:k * 256 + 128], L["XN"][:, k, 0:N], k == 0, k == KC - 1, [wr, "XN"], [pra])
                for k in range(KC):
                    MM(pg[:, 0:N], wt[:, k * 256 + 128:k * 256 + 256], L["XN"][:, k, 0:N], k == 0, k == KC - 1, [wr, "XN"], [prg])
                ACT(L["SG"][:, 0:N], pg[:, 0:N], AF.Sigmoid, [prg], ["SG"], bias=V("b1", 8 + m, 9 + m))
                uo = U[:, m, :, Lw - newc:Lw]
                STT("dve", uo, pa[:, 0:N].rearrange("p (s l) -> p s l", s=Sq), V("b1", m, m + 1),
                    L["SG"][:, 0:N].rearrange("p (s l) -> p s l", s=Sq), ALU.add, ALU.mult, [pra, "SG"], ["U"])
                if mask_first:
                    TT("dve", U[:, m, 0, 0:HALO], U[:, m, 0, 0:HALO], L["CMASK"][:, 0:HALO], ALU.mult, ["U", "CMASK"], ["U"])
                yb = L["YB"][:, m, 0:n].rearrange("p (s l) -> p s l", s=Sq)
                o, _ = VOFF["wdwa"]
                ACT(yb, U[:, m, :, 0:Lo], AF.Identity, ["U"], ["YB"], bias=V("bdw", m, m + 1), scale=VEC[:, o + m * 31:o + m * 31 + 1])
                yb2 = L["YB2"][:, 0:n].rearrange("p (s l) -> p s l", s=Sq)
                ACT(yb2, U[:, m, :, 1:1 + Lo], AF.Identity, ["U"], ["YB2"], scale=VEC[:, o + m * 31 + 1:o + m * 31 + 2])
                for kk in range(2, 31):
                    if kk % 2 == 0:
                        STT("dve", yb, U[:, m, :, kk:kk + Lo], VEC[:, o + m * 31 + kk:o + m * 31 + kk + 1], yb,
                            ALU.mult, ALU.add, ["U", "YB"], ["YB"])
                    else:
                        STT("dve", yb2, U[:, m, :, kk:kk + Lo], VEC[:, o + m * 31 + kk:o + m * 31 + kk + 1], yb2,
                            ALU.mult, ALU.add, ["U", "YB2"], ["YB2"])
                TT("dve", yb, yb, yb2, ALU.add, ["YB", "YB2"], ["YB"])
            if tail_out is not None:
                for m in range(KC):
                    CP("act", tail_lane[:, m], U[:, m, :, Lw - 30:Lw], ["U"], ["TSTG"])
                DMA("sp", tail_out, tail_lane[:] if Sq > 1 else tail_lane[:, :, 0, :], "x", ["TSTG"], [])
            ACT(L["SQ"][:, :, 0:n], L["YB"][:, :, 0:n], AF.Square, ["YB"], ["SQ"])
            CP("act", L["XN"][:, :, 0:n], L["YB"][:, :, 0:n], ["YB"], ["XN"])
            pm, prm = bank()
            pq, prq = bank()
            for k in range(KC):
                MM(pm[:, 0:n], CBm("o1024"), L["XN"][:, k, 0:n], k == 0, k == KC - 1, ["XN"], [prm])
            for k in range(KC):
                MM(pq[:, 0:n], CBm("o1024"), L["SQ"][:, k, 0:n], k == 0, k == KC - 1, ["SQ"], [prq])
            CP("dve", L["MEAN"][:, 0:n], pm[:, 0:n], [prm], ["MEAN"])
            TT("dve", L["RS"][:, 0:n], L["MEAN"][:, 0:n], L["MEAN"][:, 0:n], ALU.mult, ["MEAN"], ["RS"])
            TT("dve", L["RS"][:, 0:n], pq[:, 0:n], L["RS"][:, 0:n], ALU.subtract, [prq, "RS"], ["RS"])
            TSC("dve", L["RS"][:, 0:n], L["RS"][:, 0:n], 0.0, None, ALU.max, None, ["RS"], ["RS"])
            ACT(L["RS"][:, 0:n], L["RS"][:, 0:n], AF.Sqrt, ["RS"], ["RS"], bias=EPS, scale=1.0)
            RECIP(L["RS"][:, 0:n], L["RS"][:, 0:n], ["RS"], ["RS"])
            for k in range(KC):
                TT("dve", L["YB"][:, k, 0:n], L["YB"][:, k, 0:n], L["MEAN"][:, 0:n], ALU.subtract, ["YB", "MEAN"], ["YB"])
                TT("dve", L["YB"][:, k, 0:n], L["YB"][:, k, 0:n], L["RS"][:, 0:n], ALU.mult, ["YB", "RS"], ["YB"])
                ACT(L["XN"][:, k, 0:n], L["YB"][:, k, 0:n], AF.Silu, ["YB"], ["XN"], bias=V("lnb", k, k + 1), scale=V("lng", k, k + 1))
            c0 = N - n
            for i in range(4):
                wt, wr = wload(w2_d[i], 2048)
                for mo2 in range(2):
                    mo = 2 * i + mo2
                    po, pro = bank()
                    for k in range(KC):
                        MM(po[:, 0:n], wt[:, mo2 * 1024 + k * 128:mo2 * 1024 + k * 128 + 128], L["XN"][:, k, 0:n],
                           k == 0, k == KC - 1, [wr, "XN"], [pro])
                    STT("dve", Hh[:, mo, c0:N], po[:, 0:n], V("b2", mo, mo + 1), Hh[:, mo, c0:N], ALU.add, ALU.add,
                        [pro, Hres], [Hres])

        def conv_ffn(L, l, Hh, Hres, c1, N, Sq, UPG, UPU, mask_first, tail_out, tail_lane, pre_loaded):
            n1 = N - c1
            if pre_loaded:
                Lq = n1 // Sq
                nout = n1
            else:
                Lq = n1 - 2
                nout = Lq
            rmsnorm(L, Hh, Hres, c1, n1, V("nf%d" % l))
            wo_, _ = VOFF["wdwf%d" % l]
            for f in range(NF):
                wt, wr = wload(wup_d[l, f], 2048)
                pg, prg = bank()
                pu, pru = bank()
                for k in range(KC):
                    MM(pg[:, 0:n1], wt[:, k * 256:k * 256 + 128], L["XN"][:, k, 0:n1], k == 0, k == KC - 1, [wr, "XN"], [prg])
                for k in range(KC):
                    MM(pu[:, 0:n1], wt[:, k * 256 + 128:k * 256 + 256], L["XN"][:, k, 0:n1], k == 0, k == KC - 1, [wr, "XN"], [pru])
                chains = []
                for (ps_, pr_, UP, ci, cres0, cv) in ((pg, prg, UPG, f, "UPG", "CG"), (pu, pru, UPU, NF + f, "UPU", "CU")):
                    fx = f if pre_loaded else f % 2
                    cres = cres0 if pre_loaded else "%s%d" % (cres0, fx)
                    if pre_loaded:
                        CP("act", UP[:, fx, :, 2:2 + Lq], ps_[:, 0:n1].rearrange("p (s l) -> p s l", s=Sq), [pr_], [cres])
                    else:
                        CP("act", UP[:, fx, 0, 0:n1], ps_[:, 0:n1], [pr_], [cres])
                        if mask_first:
                            TT("dve", UP[:, fx, 0, 0:HALO - c1], UP[:, fx, 0, 0:HALO - c1], L["CMASK"][:, c1:HALO], ALU.mult, [cres, "CMASK"], [cres])
                        if tail_out is not None:
                            CP("dve", L["TAILB"][:, ci, :], UP[:, fx, 0, Lq:Lq + 2], [cres], ["TAILB"])
                    cvv = L[cv][:, 0:nout].rearrange("p (s l) -> p s l", s=Sq)
                    chains.append((UP, fx, cres, cv, cvv, wo_ + ci * 3))
                for tap in range(3):
                    for (UP, fx, cres, cv, cvv, wb) in chains:
                        if tap == 0:
                            ACT(cvv, UP[:, fx, :, 0:Lq], AF.Identity, [cres], [cv], scale=VEC[:, wb:wb + 1])
                        else:
                            STT("dve", cvv, UP[:, fx, :, tap:tap + Lq], VEC[:, wb + tap:wb + tap + 1], cvv, ALU.mult, ALU.add, [cres, cv], [cv])
                ACT(L["CG"][:, 0:nout], L["CG"][:, 0:nout], AF.Silu, ["CG"], ["CG"])
                TT("dve", L["AV"][:, f, 0:nout], L["CG"][:, 0:nout], L["CU"][:, 0:nout], ALU.mult, ["CG", "CU"], ["AV"])
            if tail_out is not None:
                if pre_loaded:
                    for (UP, cres, half) in ((UPG, "UPG", 0), (UPU, "UPU", 1)):
                        CP("act", tail_lane[:, half * NF:(half + 1) * NF], UP[:, :, :, Lq:Lq + 2], [cres], ["FSTG"])
                    DMA("sp", tail_out, tail_lane[:], "x", ["FSTG"], [])
                else:
                    DMA("sp", tail_out, L["TAILB"][:], tail_lane, ["TAILB"], [])
            co = N - nout
            for mo in range(KC):
                po, pro = bank()
                for half in range(2):
                    wt, wr = wload(wdn_d[l, mo * 2 + half], 1408)
                    for fi in range(11):
                        f = half * 11 + fi
                        MM(po[:, 0:nout], wt[:, fi * 128:fi * 128 + 128], L["AV"][:, f, 0:nout], f == 0, f == NF - 1,
                           [wr, "AV"], [pro])
                TT("dve", Hh[:, mo, co:N], po[:, 0:nout], Hh[:, mo, co:N], ALU.add, [pro, Hres], [Hres])

        def shared_kv(L, Hh, Hres, c0, n, cosk, sink, CF32, KRF32, CBF, KRBF):
            rmsnorm(L, Hh, Hres, c0, n, V("kvn"))
            pc, prc = bank()
            pk, prk = bank()
            for k in range(KC):
                MM(pc[:, 0:n], WDKV[:, k * 128:k * 128 + 128], L["XN"][:, k, 0:n], k == 0, k == KC - 1, ["XN"], [prc])
            for k in range(KC):
                MM(pk[0:32, 0:n], WKR[:, k * 32:k * 32 + 32], L["XN"][:, k, 0:n], k == 0, k == KC - 1, ["XN"], [prk])
            ACT(L["SQ"][:, 0, 0:n], pc[:, 0:n], AF.Square, [prc], ["SQ"])
            pm, prm = bank()
            MM(pm[:, 0:n], CBm("o128"), L["SQ"][:, 0, 0:n], True, True, ["SQ"], [prm])
            rstd_from_ms(pm[:, 0:n], L["RS"][:, 0:n], prm, "RS")
            STT("dve", CF32, pc[:, 0:n], V("latn"), L["RS"][:, 0:n], ALU.mult, ALU.mult, [prc, "RS"], ["CF32"])
            CP("act", CBF, CF32, ["CF32"], ["CBF"])
            ACT(L["T32"][0:32, 0:n], pk[0:32, 0:n], AF.Square, [prk], ["T32"])
            pm2, prm2 = bank()
            MM(pm2[0:32, 0:n], CFm("o32", 32), L["T32"][0:32, 0:n], True, True, ["T32"], [prm2])
            rstd_from_ms(pm2[0:32, 0:n], L["RS"][0:32, 0:n], prm2, "RS")
            STT("dve", L["T32"][0:32, 0:n], pk[0:32, 0:n], VEC[0:32, VOFF["knr"][0]:VOFF["knr"][0] + 1], L["RS"][0:32, 0:n],
                ALU.mult, ALU.mult, [prk, "RS"], ["T32"])
            pr_, prr = bank()
            MM(pr_[0:32, 0:n], CFm("p32", 32), L["T32"][0:32, 0:n], True, True, ["T32"], [prr])
            TT("dve", L["T32B"][0:32, 0:n], pr_[0:32, 0:n], sink, ALU.mult, [prr, "ROPE"], ["T32B"])
            TT("dve", L["T32"][0:32, 0:n], L["T32"][0:32, 0:n], cosk, ALU.mult, ["T32", "ROPE"], ["T32"])
            TT("dve", KRF32, L["T32"][0:32, 0:n], L["T32B"][0:32, 0:n], ALU.add, ["T32", "T32B"], ["KRF32"])
            CP("act", KRBF, KRF32, ["KRF32"], ["KRBF"])

        def qlat(L, Hh, Hres, c0, n, QL):
            rmsnorm(L, Hh, Hres, c0, n, V("nm1"))
            p0, pr0 = bank()
            p1, pr1 = bank()
            for cc, (pp, prr) in enumerate(((p0, pr0), (p1, pr1))):
                for k in range(KC):
                    MM(pp[:, 0:n], WDQ[:, k * 256 + cc * 128:k * 256 + cc * 128 + 128], L["XN"][:, k, 0:n],
                       k == 0, k == KC - 1, ["XN"], [prr])
            ACT(L["SQ"][:, 0, 0:n], p0[:, 0:n], AF.Square, [pr0], ["SQ"])
            ACT(L["SQ"][:, 1, 0:n], p1[:, 0:n], AF.Square, [pr1], ["SQ"])
            pm, prm = bank()
            MM(pm[:, 0:n], CBm("o256"), L["SQ"][:, 0, 0:n], True, False, ["SQ"], [prm])
            MM(pm[:, 0:n], CBm("o256"), L["SQ"][:, 1, 0:n], False, True, ["SQ"], [prm])
            rstd_from_ms(pm[:, 0:n], L["RS"][:, 0:n], prm, "RS")
            STT("dve", QL[:, 0, :], p0[:, 0:n], V("qln", 0, 1), L["RS"][:, 0:n], ALU.mult, ALU.mult, [pr0, "RS"], ["QL"])
            STT("dve", QL[:, 1, :], p1[:, 0:n], V("qln", 1, 2), L["RS"][:, 0:n], ALU.mult, ALU.mult, [pr1, "RS"], ["QL"])

        def qhead_gen(Q, h, QLv, n, cosq, sinq, QT, QTres, ropres):
            pq, prq = banks[6], "B6"
            for cc in range(2):
                MM(pq[0:96, 0:n], WUQ[:, cc, h * 96:h * 96 + 96], QLv[:, cc, :], cc == 0, cc == 1, ["QL"], [prq])
            ACT(Q["QSQ"][0:96, 0:n], pq[0:96, 0:n], AF.Square, [prq], ["QSQ"])
            yield
            pm, prm = bank(0, 3)
            MM(pm[0:96, 0:n], CBm("blk96", 96), Q["QSQ"][0:96, 0:n], True, True, ["QSQ"], [prm])
            rstd_lnexp(pm[0:96, 0:n], Q["QRS"][0:96, 0:n], prm, "QRS")
            STT("dve", Q["QX"][0:96, 0:n], pq[0:96, 0:n], VEC[0:96, VOFF["g96"][0]:VOFF["g96"][0] + 1], Q["QRS"][0:96, 0:n],
                ALU.mult, ALU.mult, [prq, "QRS"], ["QX"])
            CP("act", Q["QXB"][0:96, 0:n], Q["QX"][0:96, 0:n], ["QX"], ["QXB"])
            yield
            pr_, prr = bank(0, 3)
            MM(pr_[0:96, 0:n], CBm("p96", 96), Q["QXB"][0:96, 0:n], True, True, ["QXB"], [prr])
            TT("dve", Q["QRS"][0:96, 0:n], pr_[0:96, 0:n], sinq, ALU.mult, [prr, ropres, "QRS"], ["QRS"])
            TT("dve", Q["QX"][0:96, 0:n], Q["QX"][0:96, 0:n], cosq, ALU.mult, ["QX", ropres], ["QX"])
            TT("dve", QT, Q["QX"][0:96, 0:n], Q["QRS"][0:96, 0:n], ALU.add, ["QX", "QRS"], [QTres])
            yield

        def qhead(*a):
            for _ in qhead_gen(*a):
                pass

        def layer_bufs(st, ncol):
            L = {}
            L["XN"] = sb("L_XN", [128, KC, ncol], BF16, st)
            L["SQ"] = sb("L_SQ", [128, KC, ncol], BF16, st)
            L["RS"] = sb("L_RS", [128, ncol], F32, st)
            L["MEAN"] = sb("L_MEAN", [128, ncol], F32, st)
            L["SG"] = sb("L_SG", [128, ncol], F32, st)
            L["YB2"] = sb("L_YB2", [128, ncol], F32, st)
            L["YB"] = sb("L_YB", [128, KC, ncol], F32, st)
            L["CG"] = sb("L_CG", [128, ncol], F32, st)
            L["CU"] = sb("L_CU", [128, ncol], F32, st)
            L["AV"] = sb("L_AV", [128, NF, ncol], BF16, st)
            L["T32"] = sb("L_T32", [32, ncol], F32, st)
            L["T32B"] = sb("L_T32B", [32, ncol], F32, st)
            L["TAILB"] = sb("L_TAILB", [128, 44, 2], F32, st)
            L["CMASK"] = sb("L_CMASK", [128, HALO], F32, st)
            DMA("sp", L["CMASK"][:], cmask_d, "c0", (), ["CMASK"])
            return L

        with contextlib.ExitStack() as SP:
            HS = sb("HS", [128, KC, NSC], F32, SP)
            DMA("sp", HS[:], xs_d, "ld0", (), ["HS"])
            QLS = sb("QLS", [128, 2, NSC], BF16, SP)
            CSF = sb("CSF", [128, NSC], F32, SP)
            CSB = sb("CSB", [128, NSC], BF16, SP)
            KRSF = sb("KRSF", [32, NSC], F32, SP)
            KRSB = sb("KRSB", [32, NSC], BF16, SP)
            ATS = sb("ATS", [128, KC, NSC], BF16, SP)
            with contextlib.ExitStack() as SL:
                S.barrier()
                L = layer_bufs(SL, NSC)
                US = sb("US", [128, KC, NS, 34], F32, SL)
                TSTG = sb("TSTG", [128, KC, NS, 30], F32, SL)
                FSTG = sb("FSTG", [128, 44, NS, 2], F32, SL)
                DMA("sp", TSTG[:], sca_d, "ld0", (), ["TSTG"])
                for m in range(KC):
                    CP("act", US[:, m, :, 0:30], TSTG[:, m], ["TSTG"], ["U"])
                conv_module(L, HS, "HS", NSC, NS, 34, US, False, cas_o, TSTG)
                UPG = sb("S_UPG", [128, NF, NS, 6], F32, SL)
                UPU = sb("S_UPU", [128, NF, NS, 6], F32, SL)
                DMA("sp", FSTG[:], sff_d[0], "ld0", (), ["FSTG"])
                CP("act", UPG[:, :, :, 0:2], FSTG[:, 0:NF], ["FSTG"], ["UPG"])
                CP("act", UPU[:, :, :, 0:2], FSTG[:, NF:2 * NF], ["FSTG"], ["UPU"])
                conv_ffn(L, 0, HS, "HS", 0, NSC, NS, UPG, UPU, False, ffs_o[0], FSTG, True)
                ROPS = sb("ROPS", [32, 2, NSC], F32, SL)
                DMA("sp", ROPS[:, 0, :], cosks_d, "ld0", (), ["ROPE"])
                DMA("sp", ROPS[:, 1, :], sinks_d, "ld0", (), ["ROPE"])
                shared_kv(L, HS, "HS", 0, NSC, ROPS[:, 0, :], ROPS[:, 1, :], CSF[:], KRSF[:], CSB[:], KRSB[:])
                DMA("sp", cs_o, CSF[:], "o_cs", ["CF32"], [])
                DMA("sp", krs_o, KRSF[:], "o_cs", ["KRF32"], [])
                qlat(L, HS, "HS", 0, NSC, QLS)

            with contextlib.ExitStack() as SA:
                S.barrier()
                Q = {}
                Q["QSQ"] = sb("Q_SQ", [96, NSC], BF16, SA)
                Q["QRS"] = sb("Q_RS", [96, NSC], F32, SA)
                Q["QX"] = sb("Q_X", [96, NSC], F32, SA)
                Q["QXB"] = sb("Q_XB", [96, NSC], BF16, SA)
                RQ = sb("RQ", [96, 2, NSC], F32, SA)
                DMA("sp", RQ[:, 0, :], cosqs_d, "ld0", (), ["ROPQ"])
                DMA("sp", RQ[:, 1, :], sinqs_d, "ld0", (), ["ROPQ"])
                WUKT = sb("WUKT", [64, 2048], BF16, SA)
                DMA("pool", WUKT[:], wukT_d, "c1", (), ["WUKT"])
                QTS = sb("QTS", [96, NSC], BF16, SA)
                QG = sb("QG", [64, NSC], BF16, SA)
                QABS = sb("QABS", [128, NS, 16, 4], BF16, SA)
                QRT = sb("QRT", [96, NS, 16, 4], BF16, SA)
                QR0 = sb("QR0", [128, NS, 16, 4], BF16, SA)
                MEMSET("dve", QR0[:], 0.0, ["QR0"])
                o_gk = VOFF["gk64"][0]
                for h in range(16):
                    qhead(Q, h, QLS, NSC, RQ[:, 0, :], RQ[:, 1, :], QTS[:], "QTS", "ROPQ")
                    CP("act", QRT[64:96, :, h, :], QTS[64:96, :].rearrange("p (s q) -> p s q", q=4), ["QTS"], ["QRT"])
                    TSC("dve", QG[:], QTS[0:64, :], VEC[0:64, o_gk:o_gk + 1], None, ALU.mult, None, ["QTS"], ["QG"])
                    pa, pra = bank()
                    MM(pa[:, 0:NSC], WUKT[:, h * 128:h * 128 + 128], QG[:], True, True, ["QG", "WUKT"], [pra])
                    CP("act", QABS[:, :, h, :], pa[:, 0:NSC].rearrange("p (s q) -> p s q", q=4), [pra], ["QABS"])
                DMA("sp", QR0[0:32], QRT[64:96], "ld1", ["QRT"], ["QR0"])

                CNAT = sb("CNAT", [64, 128], BF16, SA)
                pt_, ptr_ = bank()
                TR(pt_[0:64, 0:128], CSF[:], CFm("identf"), ["CF32"], [ptr_])
                CP("act", CNAT[:], pt_[0:64, 0:128], [ptr_], ["CNAT"])
                RSTN = sb("RSTN", [64, 16], F32, SA)
                SQU = sb("SQU", [128, 1024], BF16, SA)
                KU = [banks[0], banks[1]]
                for hf in range(2):
                    MM(KU[hf][0:64, :], CSB[:], WUK[:, hf * 512:hf * 512 + 512], True, True, ["CBF"], ["B%d" % hf])
                    ACT(SQU[0:64, hf * 512:hf * 512 + 512], KU[hf][0:64, :], AF.Square, ["B%d" % hf], ["SQU"])
                RED(RSTN[:], SQU[0:64, :].rearrange("p (h d) -> p h d", d=64), ALU.add, ["SQU"], ["RSTN"])
                TSC("dve", RSTN[:], RSTN[:], 1.0 / 64, None, ALU.mult, None, ["RSTN"], ["RSTN"])
                ACT(RSTN[:], RSTN[:], AF.Sqrt, ["RSTN"], ["RSTN"], bias=EPS, scale=1.0)
                RECIP(RSTN[:], RSTN[:], ["RSTN"], ["RSTN"])
                MNEW = sb("MNEW", [64, NS, 64], F32, SA)
                DMA("sp", MNEW[:], mnew_d, "ld1", (), ["MNEW"])

                PT = sb("PT", [128, NS], I32, SA)
                DMA("sp", PT[:], pt_d, "ld1", (), ["PT"])
                IDXC = sb("IDXC", [128, NS, 8], I32, SA)
                IDXK = sb("IDXK", [128, NS, 2], I32, SA)
                for c8 in range(8):
                    TSC("dve", IDXC[:, :, c8], PT[:], 8, c8, ALU.mult, ALU.add, ["PT"], ["IDXC"])
                for c2 in range(2):
                    TSC("dve", IDXK[:, :, c2], PT[:], 2, c2, ALU.mult, ALU.add, ["PT"], ["IDXK"])

                GC = [sb("GC%d" % i, [128, 128, 128], BF16, SA) for i in range(2)]
                GK = [sb("GK%d" % i, [128, 128, 32], BF16, SA) for i in range(2)]
                SS_ = sb("SSEQ", [128, 129, 64], F32, SA)
                PP = sb("PSEQ", [128, 129, 64], BF16, SA)
                SSQ2 = [sb("SSQ%d" % i, [128, 4, 16], F32, SA) for i in range(2)]
                SQU2 = [sb("SQU2_%d" % i, [128, 1024], BF16, SA) for i in range(2)]
                CTS = [sb("CTS%d" % i, [128, 512], BF16, SA) for i in range(2)]
                KTS = [sb("KTS%d" % i, [128, 512], BF16, SA) for i in range(2)]
                for i in range(2):
                    MEMSET("dve", KTS[i][:], 0.0, ["KTS%d" % i])
                MX = sb("MX", [128, 64], F32, SA)
                MXC = sb("MXC", [64, 1], F32, SA)
                DG = sb("DG", [64, 64], F32, SA)
                MB = sb("MB", [128, 64], F32, SA)
                RLB = sb("RLB", [128, 64], F32, SA)
                OLT = sb("OLT", [128, NS, 16, 4], BF16, SA)
                MEMSET("dve", SS_[:, 128, :], -2000.0, ["SSEQ"])

                for s in range(NS):
                    g = s % 2
                    for c8 in range(8):
                        S.dma("pool", (lambda e, g=g, c8=c8, s=s: e.indirect_dma_start(
                            out=GC[g][:, c8 * 16:(c8 + 1) * 16, :].rearrange("p a b -> p (a b)"), out_offset=None, in_=cc_d,
                            in_offset=bass.IndirectOffsetOnAxis(ap=IDXC[:, s, c8:c8 + 1], axis=0))),
                            "g%d" % g, ["IDXC"], ["GC%d" % g])
                    for c2 in range(2):
                        S.dma("pool", (lambda e, g=g, c2=c2, s=s: e.indirect_dma_start(
                            out=GK[g][:, c2 * 64:(c2 + 1) * 64, :].rearrange("p a b -> p (a b)"), out_offset=None, in_=ckr_d,
                            in_offset=bass.IndirectOffsetOnAxis(ap=IDXK[:, s, c2:c2 + 1], axis=0))),
                            "g%d" % g, ["IDXK"], ["GK%d" % g])
                    qa = QABS[:, s, :, :].rearrange("p h q -> p (h q)")
                    qr = QR0[:, s, :, :].rearrange("p h q -> p (h q)")
                    qr32 = QR0[0:32, s, :, :].rearrange("p h q -> p (h q)")
                    def stT(ub):
                        b2 = ub % 2
                        for u4 in range(4):
                            u = ub * 4 + u4
                            TR(bankb[:, u4 * 128:(u4 + 1) * 128], GC[g][:, u, :], CBm("identb"), ["GC%d" % g], ["BB"])
                        for u4 in range(4):
                            u = ub * 4 + u4
                            TR(bankb[0:32, 512 + u4 * 128:512 + (u4 + 1) * 128], GK[g][:, u, :], CBm("identb"), ["GK%d" % g], ["BB"])
                        CP("act", CTS[b2][:], bankb[:, 0:512], ["BB"], ["CTS%d" % b2])
                        CP("dve", KTS[b2][0:32, :], bankb[0:32, 512:1024], ["BB"], ["KTS%d" % b2])

                    def stM(ub):
                        b2 = ub % 2
                        nd, ndr = banks[4 + b2], "B%d" % (4 + b2)
                        for u4 in range(4):
                            cT = CTS[b2][:, u4 * 128:(u4 + 1) * 128]
                            kk = 2 * (u4 % 2)
                            sq_, sqr = SQU2[u4 % 2], "SQU%d" % (u4 % 2)
                            for hf in range(2):
                                MM(banks[kk + hf][:, :], cT, WUK[:, hf * 512:hf * 512 + 512], True, True,
                                   ["CTS%d" % b2], ["B%d" % (kk + hf)])
                                ACT(sq_[:, hf * 512:hf * 512 + 512], banks[kk + hf][:, :], AF.Square, ["B%d" % (kk + hf)], [sqr])
                            RED(SSQ2[b2][:, u4, :], sq_[:].rearrange("p (h d) -> p h d", d=64), ALU.add, [sqr], ["SSQ%d" % b2])
                            MM(nd[:, u4 * 64:u4 * 64 + 64], cT, qa, True, True, ["CTS%d" % b2, "QABS"], [ndr], skip_group_check=True)
                            MM(nd[:, 256 + u4 * 64:256 + u4 * 64 + 64], KTS[b2][:, u4 * 128:(u4 + 1) * 128], qr, True, True,
                               ["KTS%d" % b2, "QR0"], [ndr], skip_group_check=True)

                    def stE(ub):
                        b2 = ub % 2
                        nd, ndr = banks[4 + b2], "B%d" % (4 + b2)
                        sq = SSQ2[b2]
                        sr = "SSQ%d" % b2
                        ACT(sq[:], sq[:], AF.Sqrt, [sr], [sr], bias=EPS, scale=1.0 / 64)
                        RECIP(sq[:], sq[:], [sr], [sr])
                        sv = SS_[:, ub * 4:ub * 4 + 4, :]
                        TT("dve", sv.rearrange("p u (h q) -> p u h q", q=4),
                           nd[:, 0:256].rearrange("p (u h q) -> p u h q", u=4, q=4),
                           sq[:].unsqueeze(3).to_broadcast([128, 4, 16, 4]), ALU.mult, [ndr, sr], ["SSEQ"])
                        TT("dve", sv, sv, nd[:, 256:512].rearrange("p (u c) -> p u c", u=4), ALU.add, [ndr, "SSEQ"], ["SSEQ"])

                    for ub in range(32):
                        stT(ub)
                        if ub >= 1:
                            stM(ub - 1)
                        if ub >= 2:
                            stE(ub - 2)
                    stM(31)
                    stE(30)
                    stE(31)
                    nd, ndr = bank(4, 6)
                    MM(nd[0:64, 0:64], CSB[:], qa, True, True, ["CBF", "QABS"], [ndr], skip_group_check=True)
                    MM(nd[0:64, 256:320], KRSB[:], qr32, True, True, ["KRBF", "QR0"], [ndr], skip_group_check=True)
                    svn = SS_[0:64, 128, :]
                    TT("dve", svn.rearrange("p (h q) -> p h q", q=4), nd[0:64, 0:64].rearrange("p (h q) -> p h q", q=4),
                       RSTN[:].unsqueeze(2).to_broadcast([64, 16, 4]), ALU.mult, [ndr, "RSTN"], ["SSEQ"])
                    TT("dve", svn, svn, nd[0:64, 256:320], ALU.add, [ndr, "SSEQ"], ["SSEQ"])
                    TT("dve", svn, svn, MNEW[:, s, :], ALU.add, ["SSEQ", "MNEW"], ["SSEQ"])
                    RED(MX[:], SS_[:].rearrange("p u c -> p c u"), ALU.max, ["SSEQ"], ["MX"])
                    pm, prm = bank(4, 6)
                    TR(pm[0:64, 0:128], MX[:], CFm("identf"), ["MX"], [prm])
                    RED(MXC[:], pm[0:64, 0:128], ALU.max, [prm], ["MXC"])
                    TSC("dve", DG[:], CFm("identf", 64)[:, 0:64], MXC[:, 0:1], None, ALU.mult, None, ["MXC"], ["DG"])
                    pb, pbr = bank(4, 6)
                    MM(pb[:, 0:64], CFm("onesf", 64), DG[:], True, True, ["DG"], [pbr])
                    CP("dve", MB[:], pb[:, 0:64], [pbr], ["MB"])
                    TT("dve", SS_[:], SS_[:], MB[:].unsqueeze(1).to_broadcast([128, 129, 64]), ALU.subtract, ["SSEQ", "MB"], ["SSEQ"])
                    ACT(PP[:], SS_[:], AF.Exp, ["SSEQ"], ["PSEQ"], scale=float(SM_SCALE))
                    pacc, paccr = banks[6], "B6"
                    plb, plbr = banks[0], "B0"
                    for u in range(128):
                        MM(pacc[:, 0:64], GC[g][:, u, :], PP[:, u, :], u == 0, False, ["GC%d" % g, "PSEQ"], [paccr])
                    MM(pacc[:, 0:64], CNAT[:], PP[0:64, 128, :], False, True, ["CNAT", "PSEQ"], [paccr])
                    for u in range(128):
                        MM(plb[:, 0:64], CBm("onesb"), PP[:, u, :], u == 0, False, ["PSEQ"], [plbr])
                    MM(plb[:, 0:64], CBm("onesb", 64), PP[0:64, 128, :], False, True, ["PSEQ"], [plbr])
                    RECIP(RLB[:], plb[:, 0:64], [plbr], ["RLB"])
                    TT("dve", OLT[:, s, :, :].rearrange("p h q -> p (h q)"), pacc[:, 0:64], RLB[:], ALU.mult, [paccr, "RLB"], ["OLT"])
                for kq in range(KC):
                    po, por = bank()
                    for hh in range(2):
                        h = 2 * kq + hh
                        MM(po[hh * 64:hh * 64 + 64, 0:NSC], WUV[:, h * 64:h * 64 + 64], OLT[:, :, h, :], True, True, ["OLT"], [por])
                    CP("act", ATS[:, kq, :], po[:, 0:NSC], [por], ["ATS"])

            with contextlib.ExitStack() as SL:
                S.barrier()
                L = layer_bufs(SL, NSC)
                for kq in range(KC):
                    wt, wr = wload(wo_d[kq], 1024)
                    for mo in range(KC):
                        po, por = bank()
                        MM(po[:, 0:NSC], wt[:, mo * 128:mo * 128 + 128], ATS[:, kq, :], True, True, [wr, "ATS"], [por])
                        TT("dve", HS[:, mo, :], po[:, 0:NSC], HS[:, mo, :], ALU.add, [por, "HS"], ["HS"])
                UPG = sb("S_UPG", [128, NF, NS, 6], F32, SL)
                UPU = sb("S_UPU", [128, NF, NS, 6], F32, SL)
                FSTG = sb("FSTG", [128, 44, NS, 2], F32, SL)
                DMA("sp", FSTG[:], sff_d[1], "ld0", (), ["FSTG"])
                CP("act", UPG[:, :, :, 0:2], FSTG[:, 0:NF], ["FSTG"], ["UPG"])
                CP("act", UPU[:, :, :, 0:2], FSTG[:, NF:2 * NF], ["FSTG"], ["UPU"])
                conv_ffn(L, 1, HS, "HS", 0, NSC, NS, UPG, UPU, False, ffs_o[1], FSTG, True)
                DMA("sp", ys_o, HS[:], "o_ys", ["HS"], [])

        with contextlib.ExitStack() as PR:
            S.barrier()
            HT = [sb("H%d" % t, [128, KC, TW], F32, PR) for t in range(NT)]
            QLP = sb("QLP", [128, 2, NT, NQ], BF16, PR)
            with contextlib.ExitStack() as PA:
                S.barrier()
                L = layer_bufs(PA, TW)
                UP_ = sb("UP", [128, KC, 1, TW], F32, PA)
                UPG = sb("P_UPG", [128, 2, 1, TW], F32, PA)
                UPU = sb("P_UPU", [128, 2, 1, TW], F32, PA)
                ROPK = sb("ROPK", [32, 2, TS_], F32, PA)
                CPF = sb("CPF", [128, TS_], F32, PA)
                CPB = sb("CPB", [128, TS_], BF16, PA)
                KPF = sb("KPF", [32, TS_], F32, PA)
                KPB = sb("KPB", [32, TS_], BF16, PA)
                TSTGP = sb("TSTGP", [128, KC, 1, 30], F32, PA)
                for t in range(NT):
                    Hh, Hr = HT[t], "H%d" % t
                    DMA("sp", Hh[:], xp_d[t], "ld0", (), [Hr])
                    last = (t == NT - 1)
                    conv_module(L, Hh, Hr, TW, 1, TW, UP_, t == 0, cap_o if last else None, TSTGP)
                    conv_ffn(L, 0, Hh, Hr, 30, TW, 1, UPG, UPU, t == 0, ffp_o[0] if last else None, "o_ffp", False)
                    DMA("sp", ROPK[:, 0, :], cosk_d[:, t, :], "ld0", (), ["ROPE"])
                    DMA("sp", ROPK[:, 1, :], sink_d[:, t, :], "ld0", (), ["ROPE"])
                    shared_kv(L, Hh, Hr, HALO, TS_, ROPK[:, 0, :], ROPK[:, 1, :], CPF[:], KPF[:], CPB[:], KPB[:])
                    DMA("sp", cp_o[t], CPF[:], "o_cp", ["CF32"], [])
                    DMA("sp", krp_o[t], KPF[:], "o_cp", ["KRF32"], [])
                    DMA("sp", xch_in.ap()[0:128, t * TS_:(t + 1) * TS_], CPB[:], "xin", ["CBF"], ["XIN"])
                    DMA("sp", xch_in.ap()[128:160, t * TS_:(t + 1) * TS_], KPB[:], "xin", ["KRBF"], ["XIN"])
                    qlat(L, Hh, Hr, 32, NQ, QLP[:, :, t, :])
            S.op("pool", lambda e: e.collective_compute("AllGather", ALU.bypass, replica_groups=[[0, 1, 2, 3], [4, 5, 6, 7]],
                                                        ins=[xch_in.ap()], outs=[xch_out.ap()]), ["XIN"], ["XOUT"])

            with contextlib.ExitStack() as PB:
                S.barrier()
                NKP = NKC * 128
                CTA = sb("CTA", [128, NKP], BF16, PB)
                KT = sb("KT", [96, NKP], BF16, PB)
                VH = sb("VH", [128, NKC, 65], BF16, PB)
                AOP = sb("AOP", [128, NT, 3, 128], BF16, PB)
                ATT = sb("ATT", [128, NQ], BF16, PB)
                RQP = sb("RQP", [96, 2, NT, NQ], F32, PB)
                THR = sb("THR", [128, NT * NKC], F32, PB)
                PTb = [sb("PTb%d" % i, [128, NQ], BF16, PB) for i in range(4)]
                QT2 = [sb("QT%d" % i, [96, NQ], BF16, PB) for i in range(2)]
                KSQ = [sb("KSQ%d" % i, [64, 512], BF16, PB) for i in range(2)]
                KRS_ = [sb("KRS%d" % i, [64, 512], F32, PB) for i in range(2)]
                RL = sb("RL", [128, 1], F32, PB)
                Q = {}
                Q["QSQ"] = sb("QP_SQ", [96, NQ], BF16, PB)
                Q["QRS"] = sb("QP_RS", [96, NQ], F32, PB)
                Q["QX"] = sb("QP_X", [96, NQ], F32, PB)
                Q["QXB"] = sb("QP_XB", [96, NQ], BF16, PB)
                DMA("sp", RQP[:, 0], cosq_d, "ld0", (), ["ROPQ"])
                DMA("sp", RQP[:, 1], sinq_d, "ld0", (), ["ROPQ"])
                DMA("sp", THR[:], thr_d, "ld0", (), ["THR"])
                MEMSET("dve", CTA[:, NPOS:NKP], 0.0, ["CTA"])
                MEMSET("dve", KT[:, NPOS:NKP], 0.0, ["KT"])
                MEMSET("dve", VH[:, :, 64:65], 1.0, ["VH"])
                for gt in range(24):
                    r, t = gt % 4, gt // 4
                    DMA("sp", CTA[:, gt * TS_:(gt + 1) * TS_], xch_out.ap()[r * 160:r * 160 + 128, t * TS_:(t + 1) * TS_],
                        "ld1", ["XOUT"], ["CTA"])
                    DMA("sp", KT[64:96, gt * TS_:(gt + 1) * TS_], xch_out.ap()[r * 160 + 128:r * 160 + 160, t * TS_:(t + 1) * TS_],
                        "ld1", ["XOUT"], ["KT"])
                o_gk = VOFF["gk64"][0]
                qsub = [(0, 128), (128, 128), (256, NQ - 256)]
                pctr = 0
                qgen = [None]
                for h in range(16):
                    def ka(kt):
                        n = 512 if kt < 16 else NKP - 16 * 512
                        cs = kt * 512
                        pk, prk = bank(0, 4)
                        MM(pk[0:64, 0:n], WUK[:, h * 64:h * 64 + 64], CTA[:, cs:cs + n], True, True, ["CTA"], [prk])
                        ACT(KSQ[kt % 2][:, 0:n], pk[0:64, 0:n], AF.Square, [prk], ["KSQ%d" % (kt % 2)])
                        return (kt, n, cs, pk, prk)

                    def kb(kt, n, cs, pk, prk):
                        pm, prm = bank(0, 4)
                        kr_ = KRS_[kt % 2]
                        krr = "KRS%d" % (kt % 2)
                        MM(pm[0:64, 0:n], CBm("o64", 64), KSQ[kt % 2][:, 0:n], True, True, ["KSQ%d" % (kt % 2)], [prm])
                        rstd_lnexp(pm[0:64, 0:n], kr_[:, 0:n], prm, krr)
                        STT("dve", KT[0:64, cs:cs + n], pk[0:64, 0:n], VEC[0:64, o_gk:o_gk + 1], kr_[:, 0:n], ALU.mult, ALU.mult,
                            [prk, krr], ["KT"])

                    kpend = []
                    for kt in range(17):
                        kpend.append(ka(kt))
                        if len(kpend) > 1:
                            kb(*kpend.pop(0))
                    while kpend:
                        kb(*kpend.pop(0))
                    for vb in range(9):
                        nchk = 8 if vb < 8 else 1
                        pv, prv = bank(0, 4)
                        for ci in range(nchk):
                            kc = vb * 8 + ci
                            MM(pv[:, ci * 64:ci * 64 + 64], CTA[:, kc * 128:kc * 128 + 128], WUV[:, h * 64:h * 64 + 64], True, True,
                               ["CTA"], [prv], skip_group_check=True)
                        CP("act", VH[:, vb * 8:vb * 8 + nchk, 0:64], pv[:, 0:nchk * 64].rearrange("p (c d) -> p c d", d=64), [prv], ["VH"])
                    for t in range(NT):
                        qi_ = h * NT + t
                        QTt, QTr = QT2[qi_ % 2], "QT%d" % (qi_ % 2)
                        if qgen[0] is None:
                            qhead(Q, h, QLP[:, :, t, :], NQ, RQP[:, 0, t, :], RQP[:, 1, t, :], QTt[:], QTr, "ROPQ")
                        else:
                            for _ in qgen[0]:
                                pass
                        nh, nt_ = (h, t + 1) if t + 1 < NT else (h + 1, 0)
                        if nh < 16:
                            nq = qi_ + 1
                            qgen[0] = qhead_gen(Q, nh, QLP[:, :, nt_, :], NQ, RQP[:, 0, nt_, :], RQP[:, 1, nt_, :],
                                                QT2[nq % 2][:], "QT%d" % (nq % 2), "ROPQ")
                        else:
                            qgen[0] = iter(())
                        nkc = min(NKC, -(-((4 * t + 4) * TS_) // 128))
                        first_masked = max(0, (4 * t * TS_ - 2 - 127 + 127) // 128)
                        def s_stage(kc, pctr):
                            kn = 128 if kc < NKC - 1 else NPOS - 128 * (NKC - 1)
                            pst, pstr = bank(0, 3)
                            MM(pst[0:kn, 0:NQ], KT[:, kc * 128:kc * 128 + kn], QTt[:], True, True, ["KT", QTr], [pstr])
                            pb_ = PTb[pctr % 4]
                            pbr = "PTb%d" % (pctr % 4)
                            ACT(pb_[0:kn, :], pst[0:kn, 0:NQ], AF.Exp, [pstr], [pbr], scale=float(SM_SCALE))
                            if kc >= first_masked:
                                col = t * NKC + kc
                                STT("dve", pb_[0:kn, :], CFm("iota")[0:kn, :], THR[0:kn, col:col + 1], pb_[0:kn, :], ALU.is_ge, ALU.mult,
                                    [pbr, "THR"], [pbr])
                            return (kc, kn, pb_, pbr)

                        def pv_stage(kc, kn, pb_, pbr):
                            for qi, (q0, qn) in enumerate(qsub):
                                MM(banks[3 + qi][0:qn, 0:65], pb_[0:kn, q0:q0 + qn], VH[0:kn, kc, :], kc == 0, kc == nkc - 1,
                                   [pbr, "VH"], ["B%d" % (3 + qi)])

                        pend = []
                        for kc in range(nkc):
                            pend.append(s_stage(kc, pctr))
                            pctr += 1
                            if kc in (2, 5, 8):
                                next(qgen[0], None)
                            if len(pend) > 2:
                                pv_stage(*pend.pop(0))
                        while pend:
                            pv_stage(*pend.pop(0))
                        for qi, (q0, qn) in enumerate(qsub):
                            br = "B%d" % (3 + qi)
                            TSC("dve", RL[0:qn, :], banks[3 + qi][0:qn, 64:65], 1e-30, None, ALU.max, None, [br], ["RL"])
                            RECIP(RL[0:qn, :], RL[0:qn, :], ["RL"], ["RL"])
                            TSC("dve", AOP[0:qn, t, qi, (h % 2) * 64:(h % 2) * 64 + 64], banks[3 + qi][0:qn, 0:64], RL[0:qn, 0:1], None,
                                ALU.mult, None, [br, "RL"], ["AOP"])
                    if h % 2 == 1:
                        kq = h // 2
                        wt, wr = wload(wo_d[kq], 1024)
                        for t in range(NT):
                            for qi, (q0, qn) in enumerate(qsub):
                                TR(bankb[:, q0:q0 + qn], AOP[0:qn, t, qi, :], CBm("identb")[0:qn, 0:qn], ["AOP"], ["BB"])
                            CP("act", ATT[:], bankb[:, 0:NQ], ["BB"], ["ATT"])
                            for mo in range(KC):
                                po, por = bank(0, 3)
                                MM(po[:, 0:NQ], wt[:, mo * 128:mo * 128 + 128], ATT[:], True, True, [wr, "ATT"], [por])
                                TT("dve", HT[t][:, mo, 32:TW], po[:, 0:NQ], HT[t][:, mo, 32:TW], ALU.add, [por, "H%d" % t], ["H%d" % t])

            with contextlib.ExitStack() as PC:
                S.barrier()
                L = layer_bufs(PC, TW)
                UPG = sb("P_UPG", [128, 2, 1, TW], F32, PC)
                UPU = sb("P_UPU", [128, 2, 1, TW], F32, PC)
                for t in range(NT):
                    Hh, Hr = HT[t], "H%d" % t
                    last = (t == NT - 1)
                    conv_ffn(L, 1, Hh, Hr, 32, TW, 1, UPG, UPU, t == 0, ffp_o[1] if last else None, "o_ffp", False)
                    DMA("sp", yp_o[t], Hh[:, :, HALO:TW], "o_yp", [Hr], [])
        S.emit()
    return nc


def kernel(x_prompt, x_sample, state_conv_a, state_ffn_conv, cache_kv_latent, cache_k_rope, page_table,
           meta_tokens, norm_mix, norm_ffn,
           a_w_pw1, a_b_pw1, a_w_dw, a_b_dw, a_ln_g, a_ln_b, a_w_pw2, a_b_pw2,
           ffn_w_up, ffn_w_dw, ffn_w_down,
           kv_norm, mla_w_dkv, mla_lat_norm, mla_w_kr, mla_knorm_rope, mla_w_uk, mla_w_uv, mla_knorm_nope,
           mla_w_dq, mla_q_lat_norm, mla_w_uq, mla_qnorm_nope, mla_qnorm_rope, mla_w_o):
    f32 = np.float32
    A = lambda a: np.asarray(a)
    x_prompt, x_sample = A(x_prompt), A(x_sample)
    n_phys = int(A(cache_kv_latent).shape[0])
    cache_c = np.ascontiguousarray(A(cache_kv_latent), f32).reshape(n_phys * 8, 2048)
    cache_kr = np.ascontiguousarray(A(cache_k_rope), f32).reshape(n_phys * 2, 2048)
    page_table = A(page_table).astype(np.int32)

    vec = np.zeros((128, NVEC), f32)
    def putv(name, m):
        o, w = VOFF[name]
        m = np.asarray(m, f32)
        vec[:m.shape[0], o:o + w] = m.reshape(m.shape[0], w)
    putv("nm0", _fm(A(norm_mix)[0], 8)); putv("nm1", _fm(A(norm_mix)[1], 8))
    putv("nf0", _fm(A(norm_ffn)[0], 8)); putv("nf1", _fm(A(norm_ffn)[1], 8))
    putv("kvn", _fm(A(kv_norm), 8))
    putv("b1", _fm(A(a_b_pw1)[0], 16)); putv("bdw", _fm(A(a_b_dw)[0], 8))
    putv("lng", _fm(A(a_ln_g)[0], 8)); putv("lnb", _fm(A(a_ln_b)[0], 8)); putv("b2", _fm(A(a_b_pw2)[0], 8))
    putv("wdwa", np.ascontiguousarray(A(a_w_dw)[0].reshape(31, 8, 128).transpose(2, 1, 0)).reshape(128, 8 * 31))
    for l in range(2):
        putv("wdwf%d" % l, np.ascontiguousarray(A(ffn_w_dw)[l].reshape(3, 44, 128).transpose(2, 1, 0)).reshape(128, 44 * 3))
    putv("latn", A(mla_lat_norm).reshape(128, 1))
    putv("knr", A(mla_knorm_rope).reshape(32, 1))
    putv("qln", _fm(A(mla_q_lat_norm)[0], 2))
    putv("g96", np.concatenate([A(mla_qnorm_nope)[0], A(mla_qnorm_rope)[0]]).reshape(96, 1))
    putv("gk64", A(mla_knorm_nope).reshape(64, 1))
    cb = _consts_bf()
    cf = _consts_f()

    W1 = A(a_w_pw1)[0].astype(f32)
    w1r = W1.reshape(8, 128, 2, 8, 128)
    w1l = np.ascontiguousarray(w1r.transpose(3, 1, 0, 2, 4)).reshape(8, 128, 2048)
    W2 = A(a_w_pw2)[0].astype(f32).reshape(8, 128, 4, 2, 128)
    w2l = np.ascontiguousarray(W2.transpose(2, 1, 3, 0, 4)).reshape(4, 128, 2048)
    WU = A(ffn_w_up).astype(f32).reshape(2, 8, 128, 2, NF, 128)
    wupl = np.ascontiguousarray(WU.transpose(0, 4, 2, 1, 3, 5)).reshape(2, NF, 128, 2048)
    WD = A(ffn_w_down).astype(f32).reshape(2, 2, 11, 128, 8, 128)
    wdnl = np.ascontiguousarray(WD.transpose(0, 4, 1, 3, 2, 5)).reshape(2, 16, 128, 1408)
    wol = np.ascontiguousarray(A(mla_w_o)[0].astype(f32)).reshape(8, 128, 1024)
    wdq = np.ascontiguousarray(A(mla_w_dq)[0].astype(f32).reshape(8, 128, 256).transpose(1, 0, 2)).reshape(128, 2048)
    wuq = np.ascontiguousarray(A(mla_w_uq)[0].astype(f32).reshape(2, 128, 1536))
    wdkv = np.ascontiguousarray(A(mla_w_dkv).astype(f32).reshape(8, 128, 128).transpose(1, 0, 2)).reshape(128, 1024)
    wkr = np.ascontiguousarray(A(mla_w_kr).astype(f32).reshape(8, 128, 32).transpose(1, 0, 2)).reshape(128, 256)
    wuk = np.ascontiguousarray(A(mla_w_uk).astype(f32).reshape(128, 1024))
    wuv = np.ascontiguousarray(A(mla_w_uv).astype(f32).reshape(128, 1024))
    wukT = np.ascontiguousarray(A(mla_w_uk).astype(f32).transpose(2, 1, 0)).reshape(64, 2048)

    iota = np.arange(TW)
    masknew = np.full((64, NS, 64), -2000.0, f32)
    for s in range(NS):
        for k in range(4):
            for q in range(4):
                if k <= q:
                    masknew[s * 4 + k, s, np.arange(16) * 4 + q] = 0.0
    pos_s = PAST + (np.arange(NSC) % 4)
    cks, sks = _rope_tab(pos_s)
    cosqs = np.ones((96, NSC), f32); sinqs = np.zeros((96, NSC), f32)
    cosqs[64:], sinqs[64:] = cks, sks

    hp_all = np.concatenate([np.broadcast_to(A(meta_tokens).astype(f32)[None], (2, NMETA, D)), x_prompt.astype(f32)], axis=1)
    shared = dict(vec=vec, cb=cb, cf=cf, w1l=w1l, w2l=w2l, wupl=wupl, wdnl=wdnl, wol=wol, wdq=wdq, wuq=wuq, wdkv=wdkv,
                  wkr=wkr, wuk=wuk, wuv=wuv, wukT=wukT, cache_c=cache_c, cache_kr=cache_kr, masknew=masknew,
                  cosqs=cosqs, sinqs=sinqs, cosks=cks, sinks=sks)
    in_maps = []
    for c in range(8):
        b, j = c // 4, c % 4
        xp = np.zeros((NT, 128, KC, TW), f32)
        cosq = np.ones((96, NT, NQ), f32); sinq = np.zeros((96, NT, NQ), f32)
        cosk = np.zeros((32, NT, TS_), f32); sink = np.zeros((32, NT, TS_), f32)
        thr = np.zeros((128, NT * NKC), f32)
        for t in range(NT):
            g = 4 * t + j
            p0 = g * TS_ - HALO
            win = np.zeros((TW, D), f32)
            lo = max(0, p0)
            win[lo - p0:] = hp_all[b, lo:p0 + TW]
            xp[t] = win.T.reshape(KC, 128, TW).transpose(1, 0, 2)
            posq = p0 + 32 + np.arange(NQ)
            cq, sq_ = _rope_tab(posq)
            cosq[64:, t], sinq[64:, t] = cq, sq_
            cosk[:, t], sink[:, t] = cq[:, 2:], sq_[:, 2:]
            thr[:, t * NKC:(t + 1) * NKC] = (128 * np.arange(NKC) - (p0 + 32))[None, :]
        sl = slice(NS * c, NS * c + NS)
        xs = np.ascontiguousarray(x_sample[sl].astype(f32).reshape(NSC, D).T.reshape(KC, 128, NSC).transpose(1, 0, 2))
        sca = np.ascontiguousarray(A(state_conv_a)[0, sl].astype(f32).transpose(2, 0, 1).reshape(KC, 128, NS, 30).transpose(1, 0, 2, 3))
        sff = np.ascontiguousarray(A(state_ffn_conv)[:, sl].astype(f32).transpose(0, 3, 1, 2).reshape(2, 44, 128, NS, 2).transpose(0, 2, 1, 3, 4))
        cmask = np.ones((128, HALO), f32)
        if j == 0:
            cmask[:] = 0.0
        m = dict(shared)
        m.update(xp=xp, xs=xs, sca=sca, sff=sff, ptT=np.ascontiguousarray(page_table[sl].T), colmask=cmask,
                 cosq=cosq, sinq=sinq, cosk=cosk, sink=sink, thr=thr)
        in_maps.append(m)

    nc = build_program(n_phys)
    res = run_bass_kernel_spmd(nc, in_maps, core_ids=list(range(8)))
    R = res.results

    y_prompt = np.zeros((2, NPOS, D), f32)
    kvp = np.zeros((2, NPOS, 128), f32)
    krp = np.zeros((2, NPOS, 32), f32)
    y_sample = np.zeros((128, 4, D), f32)
    cas = np.zeros((1, 128, 30, D), f32)
    ffs = np.zeros((2, 128, 2, 2 * DFF), f32)
    kvs = np.zeros((128, 4, 128), f32)
    krs = np.zeros((128, 4, 32), f32)
    cap = np.zeros((1, 2, 30, D), f32)
    ffp = np.zeros((2, 2, 2, 2 * DFF), f32)
    for c in range(8):
        b, j = c // 4, c % 4
        r = R[c]
        for t in range(NT):
            g = 4 * t + j
            y_prompt[b, g * TS_:(g + 1) * TS_] = np.asarray(r["yp"][t]).transpose(1, 0, 2).reshape(D, TS_).T
            kvp[b, g * TS_:(g + 1) * TS_] = np.asarray(r["cp"][t]).T
            krp[b, g * TS_:(g + 1) * TS_] = np.asarray(r["krp"][t]).T
        sl = slice(NS * c, NS * c + NS)
        y_sample[sl] = np.asarray(r["ys"]).transpose(1, 0, 2).reshape(D, NS, 4).transpose(1, 2, 0)
        cas[0, sl] = np.asarray(r["cas"]).transpose(1, 0, 2, 3).reshape(D, NS, 30).transpose(1, 2, 0)
        ffs[:, sl] = np.asarray(r["ffs"]).transpose(0, 2, 1, 3, 4).reshape(2, 2 * DFF, NS, 2).transpose(0, 2, 3, 1)
        kvs[sl] = np.asarray(r["cs"]).T.reshape(NS, 4, 128)
        krs[sl] = np.asarray(r["krs"]).T.reshape(NS, 4, 32)
        if j == 3:
            cap[0, b] = np.asarray(r["cap"]).transpose(1, 0, 2).reshape(D, 30).T
            ffp[:, b] = np.asarray(r["ffp"]).transpose(0, 2, 1, 3).reshape(2, 2 * DFF, 2).transpose(0, 2, 1)
    return (np.ascontiguousarray(y_prompt[:, NMETA:]), y_sample, cap, cas, ffp, ffs, kvp, krp, kvs, krs)
```

```python
import contextlib
import numpy as np
import concourse.bass as bass
import concourse.mybir as mybir
from concourse.bass_utils import run_bass_kernel_spmd

F32 = mybir.dt.float32
BF16 = mybir.dt.bfloat16
I32 = mybir.dt.int32
ALU = mybir.AluOpType
AF = mybir.ActivationFunctionType
AX = mybir.AxisListType

D = 1024
KC = 8
SEQ = 8192
NMETA = 16
NPOS = SEQ + NMETA
TS_ = 342
HALO = 34
TW = TS_ + HALO
NT = 6
NQ = TW - 32
NS = 16
NSC = 64
DFF = 2816
NF = 22
NPAGE = 128
PAST = 16384
EPS = 1e-6
SM_SCALE = 1.0 / np.sqrt(96.0)
NKC = 65
EPOCH = 12000
SAME_ENGINE_SYNC = True


class Sched:
    ENG = ("pe", "act", "dve", "pool", "sp")

    def __init__(self, nc):
        self.nc = nc
        self.ops = {e: [] for e in self.ENG}
        self.count = {e: 0 for e in self.ENG}
        self.waited = {e: {} for e in self.ENG}
        self.last_write = {}
        self.readers = {}
        self.lane_count = {}
        self.sems = {}
        self.all_sem_keys = []

    def _need(self, eng, tok):
        key, val = tok
        if key[0] == eng and (not SAME_ENGINE_SYNC or eng == "pe"):
            return
        w = self.waited[eng]
        if key[0] in self.ENG:
            cur = w.get(key[0], (-1, 0))
            if (key[1], val) <= cur:
                return
            w[key[0]] = (key[1], val)
        else:
            if w.get(key, 0) >= val:
                return
            w[key] = val
        self.ops[eng].append(("wait", key, val))

    def _deps(self, eng, reads, writes):
        toks = []
        for r in reads:
            t = self.last_write.get(r)
            if t:
                toks.append(t)
            if r[0] == "B" and (r[1:].isdigit() or r == "BB"):
                toks.extend(x for x in self.readers.get(r, ()) if x[0][0] != eng)
        for r in writes:
            t = self.last_write.get(r)
            if t:
                toks.append(t)
            toks.extend(self.readers.get(r, ()))
        for t in toks:
            self._need(eng, t)

    def _commit(self, tok, reads, writes):
        for r in writes:
            self.last_write[r] = tok
            self.readers[r] = []
        for r in reads:
            if r in writes:
                continue
            self.readers.setdefault(r, []).append(tok)

    def op(self, eng, fn, reads=(), writes=()):
        self._deps(eng, reads, writes)
        self.count[eng] += 1
        idx = self.count[eng]
        key = (eng, (idx - 1) // EPOCH)
        val = (idx - 1) % EPOCH + 1
        if key not in self.sems:
            self.sems[key] = None
            self.all_sem_keys.append(key)
        self.ops[eng].append(("op", fn, key, 1))
        self._commit((key, val), reads, writes)

    def dma(self, q, fn, lane, reads=(), writes=()):
        self._deps(q, reads, writes)
        self.rr = getattr(self, "rr", {})
        self.rr[q] = self.rr.get(q, 0) + 1
        lane = "%s%d" % (q, self.rr[q] % 10)
        key = ("lane", lane)
        if self.lane_count.get(lane, 0) > 0:
            self._need(q, (key, self.lane_count[lane]))
        if key not in self.sems:
            self.sems[key] = None
            self.all_sem_keys.append(key)
        self.lane_count[lane] = self.lane_count.get(lane, 0) + 16
        self.ops[q].append(("op", fn, key, 16))
        self._commit((key, self.lane_count[lane]), reads, writes)

    def barrier(self):
        toks = []
        for e in self.ENG:
            idx = self.count[e]
            if idx > 0:
                toks.append(((e, (idx - 1) // EPOCH), (idx - 1) % EPOCH + 1))
        for lane, cnt in self.lane_count.items():
            toks.append((("lane", lane), cnt))
        for e in self.ENG:
            for t in toks:
                self._need(e, t)

    def all_wait(self, res):
        t = self.last_write.get(res)
        if t:
            for e in self.ENG:
                self._need(e, t)

    def emit(self):
        nc = self.nc
        with contextlib.ExitStack() as st:
            for i, key in enumerate(self.all_sem_keys):
                self.sems[key] = st.enter_context(nc.semaphore("s%d" % i))
            for key in self.all_sem_keys:
                if key[0] == "lane":
                    self._need("sp", (key, self.lane_count[key[1]]))
            blk = st.enter_context(nc.Block())

            def run(engname):
                def body(e):
                    for item in self.ops[engname]:
                        if item[0] == "wait":
                            e.wait_ge(self.sems[item[1]], item[2])
                        else:
                            item[1](e).then_inc(self.sems[item[2]], item[3])
                return body

            blk.tensor(run("pe"))
            blk.scalar(run("act"))
            blk.vector(run("dve"))
            blk.gpsimd(run("pool"))
            blk.sync(run("sp"))


VEC_LAYOUT = [("nm0", 8), ("nm1", 8), ("nf0", 8), ("nf1", 8), ("kvn", 8), ("b1", 16), ("bdw", 8), ("lng", 8),
              ("lnb", 8), ("b2", 8), ("wdwa", 8 * 31), ("wdwf0", 44 * 3), ("wdwf1", 44 * 3), ("latn", 1),
              ("knr", 1), ("qln", 2), ("g96", 1), ("gk64", 1)]
VOFF = {}
_o = 0
for _n, _w in VEC_LAYOUT:
    VOFF[_n] = (_o, _w)
    _o += _w
NVEC = _o

CB_LAYOUT = [("o1024", 128), ("o256", 128), ("o128", 128), ("blk96", 96), ("o64", 64), ("p96", 96), ("identb", 128),
             ("onesb", 128)]
CBOFF = {}
_o = 0
for _n, _w in CB_LAYOUT:
    CBOFF[_n] = (_o, _w)
    _o += _w
NCB = _o


def _fm(v, nchunk):
    return np.ascontiguousarray(np.asarray(v, np.float32).reshape(nchunk, 128).T)


def _consts_bf():
    c = np.zeros((128, NCB), np.float32)
    def put(name, m):
        o, w = CBOFF[name]
        c[:m.shape[0], o:o + m.shape[1]] = m
    put("o1024", np.full((128, 128), 1.0 / 1024, np.float32))
    put("o256", np.full((128, 128), 1.0 / 256, np.float32))
    put("o128", np.full((128, 128), 1.0 / 128, np.float32))
    blk = np.zeros((96, 96), np.float32)
    blk[:64, :64] = 1.0 / 64
    blk[64:, 64:] = 1.0 / 32
    put("blk96", blk)
    put("o64", np.full((64, 64), 1.0 / 64, np.float32))
    p96 = np.zeros((96, 96), np.float32)
    for m in range(16):
        p96[64 + m + 16, 64 + m] = -1.0
        p96[64 + m, 64 + m + 16] = 1.0
    put("p96", p96)
    put("identb", np.eye(128, dtype=np.float32))
    put("onesb", np.ones((128, 128), np.float32))
    return c


CF_LAYOUT = [("identf", 128), ("p32", 32), ("o32", 32), ("onesf", 128), ("iota", NQ)]
CFOFF = {}
_o = 0
for _n, _w in CF_LAYOUT:
    CFOFF[_n] = (_o, _w)
    _o += _w
NCF = _o


def _consts_f():
    c = np.zeros((128, NCF), np.float32)
    def put(name, m):
        o, w = CFOFF[name]
        c[:m.shape[0], o:o + m.shape[1]] = m
    put("identf", np.eye(128, dtype=np.float32))
    p32 = np.zeros((32, 32), np.float32)
    for m in range(16):
        p32[m + 16, m] = -1.0
        p32[m, m + 16] = 1.0
    put("p32", p32)
    put("o32", np.full((32, 32), 1.0 / 32, np.float32))
    put("onesf", np.ones((128, 128), np.float32))
    put("iota", (np.arange(NQ)[None, :] - np.arange(128)[:, None]).astype(np.float32))
    return c


def _rope_tab(pos):
    inv = (10000.0 ** (-np.arange(16, dtype=np.float32) / 16)).astype(np.float32)
    ang = pos.astype(np.float32)[None, :] * inv[:, None]
    cos = np.cos(ang).astype(np.float32)
    sin = np.sin(ang).astype(np.float32)
    return np.concatenate([cos, cos], 0), np.concatenate([sin, sin], 0)


def build_program(n_phys):
    nc = bass.Bass("TRN2", target_bir_lowering=False)
    S = Sched(nc)

    def din(name, shape, dt=F32):
        return nc.dram_tensor(name, list(shape), dt, kind="ExternalInput").ap()

    def dout(name, shape, dt=F32):
        return nc.dram_tensor(name, list(shape), dt, kind="ExternalOutput").ap()

    xp_d = din("xp", [NT, 128, KC, TW])
    xs_d = din("xs", [128, KC, NSC])
    sca_d = din("sca", [128, KC, NS, 30])
    sff_d = din("sff", [2, 128, 44, NS, 2])
    cc_d = din("cache_c", [n_phys * 8, 2048])
    ckr_d = din("cache_kr", [n_phys * 2, 2048])
    pt_d = din("ptT", [128, NS], I32)
    cmask_d = din("colmask", [128, HALO])
    cosq_d = din("cosq", [96, NT, NQ])
    sinq_d = din("sinq", [96, NT, NQ])
    cosk_d = din("cosk", [32, NT, TS_])
    sink_d = din("sink", [32, NT, TS_])
    cosqs_d = din("cosqs", [96, NSC])
    sinqs_d = din("sinqs", [96, NSC])
    cosks_d = din("cosks", [32, NSC])
    sinks_d = din("sinks", [32, NSC])
    thr_d = din("thr", [128, NT * NKC])
    mnew_d = din("masknew", [64, NS, 64])
    vec_d = din("vec", [128, NVEC])
    cb_d = din("cb", [128, NCB])
    cf_d = din("cf", [128, NCF])
    w1_d = din("w1l", [8, 128, 2048])
    w2_d = din("w2l", [4, 128, 2048])
    wup_d = din("wupl", [2, NF, 128, 2048])
    wdn_d = din("wdnl", [2, 16, 128, 1408])
    wo_d = din("wol", [8, 128, 1024])
    wdq_d = din("wdq", [128, 2048])
    wuq_d = din("wuq", [2, 128, 1536])
    wdkv_d = din("wdkv", [128, 1024])
    wkr_d = din("wkr", [128, 256])
    wuk_d = din("wuk", [128, 1024])
    wuv_d = din("wuv", [128, 1024])
    wukT_d = din("wukT", [64, 2048])

    yp_o = dout("yp", [NT, 128, KC, TS_])
    ys_o = dout("ys", [128, KC, NSC])
    cap_o = dout("cap", [128, KC, 30])
    cas_o = dout("cas", [128, KC, NS, 30])
    ffp_o = dout("ffp", [2, 128, 44, 2])
    ffs_o = dout("ffs", [2, 128, 44, NS, 2])
    cp_o = dout("cp", [NT, 128, TS_])
    krp_o = dout("krp", [NT, 32, TS_])
    cs_o = dout("cs", [128, NSC])
    krs_o = dout("krs", [32, NSC])

    xch_in = nc.dram_tensor("xch_in", [160, NT * TS_], BF16)
    xch_out = nc.dram_tensor("xch_out", [640, NT * TS_], BF16)

    def MM(out, lhsT, rhs, start, stop, R, W, **kw):
        S.op("pe", lambda e: e.matmul(out, lhsT=lhsT, rhs=rhs, start=start, stop=stop, **kw), R, W)

    def TR(out, in_, ident, R, W):
        S.op("pe", lambda e: e.transpose(out=out, in_=in_, identity=ident), R, W)

    def ACT(out, in_, func, R, W, bias=None, scale=None):
        kw = {}
        if bias is not None:
            kw["bias"] = bias
        if scale is not None:
            kw["scale"] = scale
        S.op("act", lambda e: e.activation(out=out, in_=in_, func=func, **kw), R, W)

    def TSC(eng, out, in0, s1, s2, op0, op1, R, W):
        if op1 is None:
            S.op(eng, lambda e: e.tensor_scalar(out=out, in0=in0, scalar1=s1, scalar2=None, op0=op0), R, W)
        else:
            S.op(eng, lambda e: e.tensor_scalar(out=out, in0=in0, scalar1=s1, scalar2=s2, op0=op0, op1=op1), R, W)

    def STT(eng, out, in0, scalar, in1, op0, op1, R, W):
        S.op(eng, lambda e: e.scalar_tensor_tensor(out=out, in0=in0, scalar=scalar, in1=in1, op0=op0, op1=op1), R, W)

    def TT(eng, out, in0, in1, op, R, W):
        S.op(eng, lambda e: e.tensor_tensor(out=out, in0=in0, in1=in1, op=op), R, W)

    def CP(eng, out, in_, R, W):
        if eng == "act":
            S.op("act", lambda e: e.activation(out=out, in_=in_, func=AF.Copy), R, W)
        else:
            S.op(eng, lambda e: e.tensor_copy(out=out, in_=in_), R, W)

    def RECIP(out, in_, R, W):
        S.op("dve", lambda e: e.reciprocal(out=out, in_=in_), R, W)

    def MEMSET(eng, ap, val, W):
        S.op(eng, lambda e: e.memset(ap, val), (), W)

    def RED(out, in_, op, R, W):
        S.op("dve", lambda e: e.tensor_reduce(out=out, in_=in_, axis=AX.X, op=op), R, W)

    def DMA(q, out, in_, lane, R, W):
        S.dma(q, lambda e: e.dma_start(out=out, in_=in_), lane, R, W)

    with contextlib.ExitStack() as G:
        nctr = [0]

        def sb(name, shape, dt, st=G):
            nctr[0] += 1
            return st.enter_context(nc.sbuf_tensor("%s_%d" % (name, nctr[0]), list(shape), dt))

        banks = [G.enter_context(nc.psum_tensor("B%d" % i, [128, 512], F32)) for i in range(7)]
        bankb = G.enter_context(nc.psum_tensor("BB", [128, 1024], BF16))
        bctr = [0]

        def bank(lo=0, hi=7):
            i = lo + bctr[0] % (hi - lo)
            bctr[0] += 1
            return banks[i], "B%d" % i

        VEC = sb("VEC", [128, NVEC], F32)
        CB = sb("CB", [128, NCB], BF16)
        CF = sb("CF", [128, NCF], F32)
        WDQ = sb("WDQ", [128, 2048], BF16)
        WUQ = sb("WUQ", [128, 2, 1536], BF16)
        WDKV = sb("WDKV", [128, 1024], BF16)
        WKR = sb("WKR", [128, 256], BF16)
        WUK = sb("WUK", [128, 1024], BF16)
        WUV = sb("WUV", [128, 1024], BF16)
        DMA("sp", VEC[:], vec_d, "c0", (), ["VEC"])
        DMA("sp", CF[:], cf_d, "c0", (), ["CF"])
        DMA("pool", CB[:], cb_d, "c1", (), ["CB"])
        DMA("pool", WDQ[:], wdq_d, "c1", (), ["WDQ"])
        DMA("pool", WUQ[:, 0, :], wuq_d[0], "c1", (), ["WUQ"])
        DMA("pool", WUQ[:, 1, :], wuq_d[1], "c1", (), ["WUQ"])
        DMA("pool", WDKV[:], wdkv_d, "c1", (), ["WDKV"])
        DMA("pool", WKR[:], wkr_d, "c1", (), ["WKR"])
        DMA("pool", WUK[:], wuk_d, "c1", (), ["WUK"])
        DMA("pool", WUV[:], wuv_d, "c1", (), ["WUV"])
        for r in ("VEC", "CF", "CB", "WDQ", "WUQ", "WDKV", "WKR", "WUK", "WUV"):
            S.all_wait(r)

        def V(name, a=None, b=None):
            o, w = VOFF[name]
            if a is None:
                return VEC[:, o:o + w]
            return VEC[:, o + a:o + b]

        def CBm(name, rows=128):
            o, w = CBOFF[name]
            return CB[0:rows, o:o + w]

        def CFm(name, rows=128):
            o, w = CFOFF[name]
            return CF[0:rows, o:o + w]

        NSLOT = 4
        WS = [sb("WS%d" % i, [128, 2048], BF16) for i in range(NSLOT)]
        wctr = [0]

        def wload(src, ne):
            i = wctr[0] % NSLOT
            wctr[0] += 1
            DMA("pool", WS[i][:, 0:ne], src, "w%d" % i, (), ["WS%d" % i])
            return WS[i], "WS%d" % i

        def rstd_from_ms(ps_ap, rs_ap, Rps, Wrs):
            ACT(rs_ap, ps_ap, AF.Sqrt, [Rps], [Wrs], bias=EPS, scale=1.0)
            RECIP(rs_ap, rs_ap, [Wrs], [Wrs])

        def rstd_lnexp(ps_ap, rs_ap, Rps, Wrs):
            ACT(rs_ap, ps_ap, AF.Ln, [Rps], [Wrs], bias=EPS, scale=1.0)
            ACT(rs_ap, rs_ap, AF.Exp, [Wrs], [Wrs], scale=-0.5)

        def rmsnorm(L, Hh, Hres, c0, n, gain):
            ACT(L["SQ"][:, :, 0:n], Hh[:, :, c0:c0 + n], AF.Square, [Hres], ["SQ"])
            ps, pr = bank()
            for k in range(KC):
                MM(ps[:, 0:n], CBm("o1024"), L["SQ"][:, k, 0:n], k == 0, k == KC - 1, ["SQ"], [pr])
            rstd_from_ms(ps[:, 0:n], L["RS"][:, 0:n], pr, "RS")
            for k in range(KC):
                STT("dve", L["XN"][:, k, 0:n], Hh[:, k, c0:c0 + n], gain[:, k:k + 1], L["RS"][:, 0:n],
                    ALU.mult, ALU.mult, [Hres, "RS"], ["XN"])

        def conv_module(L, Hh, Hres, N, Sq, Lw, U, mask_first, tail_out, tail_lane):
            Lo = Lw - 30
            n = Sq * Lo
            newc = N // Sq
            rmsnorm(L, Hh, Hres, 0, N, V("nm0"))
            for m in range(KC):
                wt, wr = wload(w1_d[m], 2048)
                pa, pra = bank()
                pg, prg = bank()
                for k in range(KC):
                    MM(pa[:, 0:N], wt[:, k * 256:k * 256 + 128], L["XN"][:, k, 0:N], k == 0, k == KC - 1, [wr, "XN"], [pra])
                for k in range(KC):
                    MM(pg[:, 0:N], wt[:, k * 256 + 128:k * 256 + 256], L["XN"][:, k, 0:N], k == 0, k == KC - 1, [wr, "XN"], [prg])
                ACT(L["SG"][:, 0:N], pg[:, 0:N], AF.Sigmoid, [prg], ["SG"], bias=V("b1", 8 + m, 9 + m))
                uo = U[:, m, :, Lw - newc:Lw]
                STT("dve", uo, pa[:, 0:N].rearrange("p (s l) -> p s l", s=Sq), V("b1", m, m + 1),
                    L["SG"][:, 0:N].rearrange("p (s l) -> p s l", s=Sq), ALU.add, ALU.mult, [pra, "SG"], ["U"])
                if mask_first:
                    TT("dve", U[:, m, 0, 0:HALO], U[:, m, 0, 0:HALO], L["CMASK"][:, 0:HALO], ALU.mult, ["U", "CMASK"], ["U"])
                yb = L["YB"][:, m, 0:n].rearrange("p (s l) -> p s l", s=Sq)
                o, _ = VOFF["wdwa"]
                ACT(yb, U[:, m, :, 0:Lo], AF.Identity, ["U"], ["YB"], bias=V("bdw", m, m + 1), scale=VEC[:, o + m * 31:o + m * 31 + 1])
                yb2 = L["YB2"][:, 0:n].rearrange("p (s l) -> p s l", s=Sq)
                ACT(yb2, U[:, m, :, 1:1 + Lo], AF.Identity, ["U"], ["YB2"], scale=VEC[:, o + m * 31 + 1:o + m * 31 + 2])
                for kk in range(2, 31):
                    if kk % 2 == 0:
                        STT("dve", yb, U[:, m, :, kk:kk + Lo], VEC[:, o + m * 31 + kk:o + m * 31 + kk + 1], yb,
                            ALU.mult, ALU.add, ["U", "YB"], ["YB"])
                    else:
                        STT("dve", yb2, U[:, m, :, kk:kk + Lo], VEC[:, o + m * 31 + kk:o + m * 31 + kk + 1], yb2,
                            ALU.mult, ALU.add, ["U", "YB2"], ["YB2"])
                TT("dve", yb, yb, yb2, ALU.add, ["YB", "YB2"], ["YB"])
            if tail_out is not None:
                for m in range(KC):
                    CP("act", tail_lane[:, m], U[:, m, :, Lw - 30:Lw], ["U"], ["TSTG"])
                DMA("sp", tail_out, tail_lane[:] if Sq > 1 else tail_lane[:, :, 0, :], "x", ["TSTG"], [])
            ACT(L["SQ"][:, :, 0:n], L["YB"][:, :, 0:n], AF.Square, ["YB"], ["SQ"])
            CP("act", L["XN"][:, :, 0:n], L["YB"][:, :, 0:n], ["YB"], ["XN"])
            pm, prm = bank()
            pq, prq = bank()
            for k in range(KC):
                MM(pm[:, 0:n], CBm("o1024"), L["XN"][:, k, 0:n], k == 0, k == KC - 1, ["XN"], [prm])
            for k in range(KC):
                MM(pq[:, 0:n], CBm("o1024"), L["SQ"][:, k, 0:n], k == 0, k == KC - 1, ["SQ"], [prq])
            CP("dve", L["MEAN"][:, 0:n], pm[:, 0:n], [prm], ["MEAN"])
            TT("dve", L["RS"][:, 0:n], L["MEAN"][:, 0:n], L["MEAN"][:, 0:n], ALU.mult, ["MEAN"], ["RS"])
            TT("dve", L["RS"][:, 0:n], pq[:, 0:n], L["RS"][:, 0:n], ALU.subtract, [prq, "RS"], ["RS"])
            TSC("dve", L["RS"][:, 0:n], L["RS"][:, 0:n], 0.0, None, ALU.max, None, ["RS"], ["RS"])
            ACT(L["RS"][:, 0:n], L["RS"][:, 0:n], AF.Sqrt, ["RS"], ["RS"], bias=EPS, scale=1.0)
            RECIP(L["RS"][:, 0:n], L["RS"][:, 0:n], ["RS"], ["RS"])
            for k in range(KC):
                TT("dve", L["YB"][:, k, 0:n], L["YB"][:, k, 0:n], L["MEAN"][:, 0:n], ALU.subtract, ["YB", "MEAN"], ["YB"])
                TT("dve", L["YB"][:, k, 0:n], L["YB"][:, k, 0:n], L["RS"][:, 0:n], ALU.mult, ["YB", "RS"], ["YB"])
                ACT(L["XN"][:, k, 0:n], L["YB"][:, k, 0:n], AF.Silu, ["YB"], ["XN"], bias=V("lnb", k, k + 1), scale=V("lng", k, k + 1))
            c0 = N - n
            for i in range(4):
                wt, wr = wload(w2_d[i], 2048)
                for mo2 in range(2):
                    mo = 2 * i + mo2
                    po, pro = bank()
                    for k in range(KC):
                        MM(po[:, 0:n], wt[:, mo2 * 1024 + k * 128:mo2 * 1024 + k * 128 + 128], L["XN"][:, k, 0:n],
                           k == 0, k == KC - 1, [wr, "XN"], [pro])
                    STT("dve", Hh[:, mo, c0:N], po[:, 0:n], V("b2", mo, mo + 1), Hh[:, mo, c0:N], ALU.add, ALU.add,
                        [pro, Hres], [Hres])

        def conv_ffn(L, l, Hh, Hres, c1, N, Sq, UPG, UPU, mask_first, tail_out, tail_lane, pre_loaded):
            n1 = N - c1
            if pre_loaded:
                Lq = n1 // Sq
                nout = n1
            else:
                Lq = n1 - 2
                nout = Lq
            rmsnorm(L, Hh, Hres, c1, n1, V("nf%d" % l))
            wo_, _ = VOFF["wdwf%d" % l]
            for f in range(NF):
                wt, wr = wload(wup_d[l, f], 2048)
                pg, prg = bank()
                pu, pru = bank()
                for k in range(KC):
                    MM(pg[:, 0:n1], wt[:, k * 256:k * 256 + 128], L["XN"][:, k, 0:n1], k == 0, k == KC - 1, [wr, "XN"], [prg])
                for k in range(KC):
                    MM(pu[:, 0:n1], wt[:, k * 256 + 128:k * 256 + 256], L["XN"][:, k, 0:n1], k == 0, k == KC - 1, [wr, "XN"], [pru])
                chains = []
                for (ps_, pr_, UP, ci, cres0, cv) in ((pg, prg, UPG, f, "UPG", "CG"), (pu, pru, UPU, NF + f, "UPU", "CU")):
                    fx = f if pre_loaded else f % 2
                    cres = cres0 if pre_loaded else "%s%d" % (cres0, fx)
                    if pre_loaded:
                        CP("act", UP[:, fx, :, 2:2 + Lq], ps_[:, 0:n1].rearrange("p (s l) -> p s l", s=Sq), [pr_], [cres])
                    else:
                        CP("act", UP[:, fx, 0, 0:n1], ps_[:, 0:n1], [pr_], [cres])
                        if mask_first:
                            TT("dve", UP[:, fx, 0, 0:HALO - c1], UP[:, fx, 0, 0:HALO - c1], L["CMASK"][:, c1:HALO], ALU.mult, [cres, "CMASK"], [cres])
                        if tail_out is not None:
                            CP("dve", L["TAILB"][:, ci, :], UP[:, fx, 0, Lq:Lq + 2], [cres], ["TAILB"])
                    cvv = L[cv][:, 0:nout].rearrange("p (s l) -> p s l", s=Sq)
                    chains.append((UP, fx, cres, cv, cvv, wo_ + ci * 3))
                for tap in range(3):
                    for (UP, fx, cres, cv, cvv, wb) in chains:
                        if tap == 0:
                            ACT(cvv, UP[:, fx, :, 0:Lq], AF.Identity, [cres], [cv], scale=VEC[:, wb:wb + 1])
                        else:
                            STT("dve", cvv, UP[:, fx, :, tap:tap + Lq], VEC[:, wb + tap:wb + tap + 1], cvv, ALU.mult, ALU.add, [cres, cv], [cv])
                ACT(L["CG"][:, 0:nout], L["CG"][:, 0:nout], AF.Silu, ["CG"], ["CG"])
                TT("dve", L["AV"][:, f, 0:nout], L["CG"][:, 0:nout], L["CU"][:, 0:nout], ALU.mult, ["CG", "CU"], ["AV"])
            if tail_out is not None:
                if pre_loaded:
                    for (UP, cres, half) in ((UPG, "UPG", 0), (UPU, "UPU", 1)):
                        CP("act", tail_lane[:, half * NF:(half + 1) * NF], UP[:, :, :, Lq:Lq + 2], [cres], ["FSTG"])
                    DMA("sp", tail_out, tail_lane[:], "x", ["FSTG"], [])
                else:
                    DMA("sp", tail_out, L["TAILB"][:], tail_lane, ["TAILB"], [])
            co = N - nout
            for mo in range(KC):
                po, pro = bank()
                for half in range(2):
                    wt, wr = wload(wdn_d[l, mo * 2 + half], 1408)
                    for fi in range(11):
                        f = half * 11 + fi
                        MM(po[:, 0:nout], wt[:, fi * 128:fi * 128 + 128], L["AV"][:, f, 0:nout], f == 0, f == NF - 1,
                           [wr, "AV"], [pro])
                TT("dve", Hh[:, mo, co:N], po[:, 0:nout], Hh[:, mo, co:N], ALU.add, [pro, Hres], [Hres])

        def shared_kv(L, Hh, Hres, c0, n, cosk, sink, CF32, KRF32, CBF, KRBF):
            rmsnorm(L, Hh, Hres, c0, n, V("kvn"))
            pc, prc = bank()
            pk, prk = bank()
            for k in range(KC):
                MM(pc[:, 0:n], WDKV[:, k * 128:k * 128 + 128], L["XN"][:, k, 0:n], k == 0, k == KC - 1, ["XN"], [prc])
            for k in range(KC):
                MM(pk[0:32, 0:n], WKR[:, k * 32:k * 32 + 32], L["XN"][:, k, 0:n], k == 0, k == KC - 1, ["XN"], [prk])
            ACT(L["SQ"][:, 0, 0:n], pc[:, 0:n], AF.Square, [prc], ["SQ"])
            pm, prm = bank()
            MM(pm[:, 0:n], CBm("o128"), L["SQ"][:, 0, 0:n], True, True, ["SQ"], [prm])
            rstd_from_ms(pm[:, 0:n], L["RS"][:, 0:n], prm, "RS")
            STT("dve", CF32, pc[:, 0:n], V("latn"), L["RS"][:, 0:n], ALU.mult, ALU.mult, [prc, "RS"], ["CF32"])
            CP("act", CBF, CF32, ["CF32"], ["CBF"])
            ACT(L["T32"][0:32, 0:n], pk[0:32, 0:n], AF.Square, [prk], ["T32"])
            pm2, prm2 = bank()
            MM(pm2[0:32, 0:n], CFm("o32", 32), L["T32"][0:32, 0:n], True, True, ["T32"], [prm2])
            rstd_from_ms(pm2[0:32, 0:n], L["RS"][0:32, 0:n], prm2, "RS")
            STT("dve", L["T32"][0:32, 0:n], pk[0:32, 0:n], VEC[0:32, VOFF["knr"][0]:VOFF["knr"][0] + 1], L["RS"][0:32, 0:n],
                ALU.mult, ALU.mult, [prk, "RS"], ["T32"])
            pr_, prr = bank()
            MM(pr_[0:32, 0:n], CFm("p32", 32), L["T32"][0:32, 0:n], True, True, ["T32"], [prr])
            TT("dve", L["T32B"][0:32, 0:n], pr_[0:32, 0:n], sink, ALU.mult, [prr, "ROPE"], ["T32B"])
            TT("dve", L["T32"][0:32, 0:n], L["T32"][0:32, 0:n], cosk, ALU.mult, ["T32", "ROPE"], ["T32"])
            TT("dve", KRF32, L["T32"][0:32, 0:n], L["T32B"][0:32, 0:n], ALU.add, ["T32", "T32B"], ["KRF32"])
            CP("act", KRBF, KRF32, ["KRF32"], ["KRBF"])

        def qlat(L, Hh, Hres, c0, n, QL):
            rmsnorm(L, Hh, Hres, c0, n, V("nm1"))
            p0, pr0 = bank()
            p1, pr1 = bank()
            for cc, (pp, prr) in enumerate(((p0, pr0), (p1, pr1))):
                for k in range(KC):
                    MM(pp[:, 0:n], WDQ[:, k * 256 + cc * 128:k * 256 + cc * 128 + 128], L["XN"][:, k, 0:n],
                       k == 0, k == KC - 1, ["XN"], [prr])
            ACT(L["SQ"][:, 0, 0:n], p0[:, 0:n], AF.Square, [pr0], ["SQ"])
            ACT(L["SQ"][:, 1, 0:n], p1[:, 0:n], AF.Square, [pr1], ["SQ"])
            pm, prm = bank()
            MM(pm[:, 0:n], CBm("o256"), L["SQ"][:, 0, 0:n], True, False, ["SQ"], [prm])
            MM(pm[:, 0:n], CBm("o256"), L["SQ"][:, 1, 0:n], False, True, ["SQ"], [prm])
            rstd_from_ms(pm[:, 0:n], L["RS"][:, 0:n], prm, "RS")
            STT("dve", QL[:, 0, :], p0[:, 0:n], V("qln", 0, 1), L["RS"][:, 0:n], ALU.mult, ALU.mult, [pr0, "RS"], ["QL"])
            STT("dve", QL[:, 1, :], p1[:, 0:n], V("qln", 1, 2), L["RS"][:, 0:n], ALU.mult, ALU.mult, [pr1, "RS"], ["QL"])

        def qhead_gen(Q, h, QLv, n, cosq, sinq, QT, QTres, ropres):
            pq, prq = banks[6], "B6"
            for cc in range(2):
                MM(pq[0:96, 0:n], WUQ[:, cc, h * 96:h * 96 + 96], QLv[:, cc, :], cc == 0, cc == 1, ["QL"], [prq])
            ACT(Q["QSQ"][0:96, 0:n], pq[0:96, 0:n], AF.Square, [prq], ["QSQ"])
            yield
            pm, prm = bank(0, 3)
            MM(pm[0:96, 0:n], CBm("blk96", 96), Q["QSQ"][0:96, 0:n], True, True, ["QSQ"], [prm])
            rstd_lnexp(pm[0:96, 0:n], Q["QRS"][0:96, 0:n], prm, "QRS")
            STT("dve", Q["QX"][0:96, 0:n], pq[0:96, 0:n], VEC[0:96, VOFF["g96"][0]:VOFF["g96"][0] + 1], Q["QRS"][0:96, 0:n],
                ALU.mult, ALU.mult, [prq, "QRS"], ["QX"])
            CP("act", Q["QXB"][0:96, 0:n], Q["QX"][0:96, 0:n], ["QX"], ["QXB"])
            yield
            pr_, prr = bank(0, 3)
            MM(pr_[0:96, 0:n], CBm("p96", 96), Q["QXB"][0:96, 0:n], True, True, ["QXB"], [prr])
            TT("dve", Q["QRS"][0:96, 0:n], pr_[0:96, 0:n], sinq, ALU.mult, [prr, ropres, "QRS"], ["QRS"])
            TT("dve", Q["QX"][0:96, 0:n], Q["QX"][0:96, 0:n], cosq, ALU.mult, ["QX", ropres], ["QX"])
            TT("dve", QT, Q["QX"][0:96, 0:n], Q["QRS"][0:96, 0:n], ALU.add, ["QX", "QRS"], [QTres])
            yield

        def qhead(*a):
            for _ in qhead_gen(*a):
                pass

        def layer_bufs(st, ncol):
            L = {}
            L["XN"] = sb("L_XN", [128, KC, ncol], BF16, st)
            L["SQ"] = sb("L_SQ", [128, KC, ncol], BF16, st)
            L["RS"] = sb("L_RS", [128, ncol], F32, st)
            L["MEAN"] = sb("L_MEAN", [128, ncol], F32, st)
            L["SG"] = sb("L_SG", [128, ncol], F32, st)
            L["YB2"] = sb("L_YB2", [128, ncol], F32, st)
            L["YB"] = sb("L_YB", [128, KC, ncol], F32, st)
            L["CG"] = sb("L_CG", [128, ncol], F32, st)
            L["CU"] = sb("L_CU", [128, ncol], F32, st)
            L["AV"] = sb("L_AV", [128, NF, ncol], BF16, st)
            L["T32"] = sb("L_T32", [32, ncol], F32, st)
            L["T32B"] = sb("L_T32B", [32, ncol], F32, st)
            L["TAILB"] = sb("L_TAILB", [128, 44, 2], F32, st)
            L["CMASK"] = sb("L_CMASK", [128, HALO], F32, st)
            DMA("sp", L["CMASK"][:], cmask_d, "c0", (), ["CMASK"])
            return L

        with contextlib.ExitStack() as SP:
            HS = sb("HS", [128, KC, NSC], F32, SP)
            DMA("sp", HS[:], xs_d, "ld0", (), ["HS"])
            QLS = sb("QLS", [128, 2, NSC], BF16, SP)
            CSF = sb("CSF", [128, NSC], F32, SP)
            CSB = sb("CSB", [128, NSC], BF16, SP)
            KRSF = sb("KRSF", [32, NSC], F32, SP)
            KRSB = sb("KRSB", [32, NSC], BF16, SP)
            ATS = sb("ATS", [128, KC, NSC], BF16, SP)
            with contextlib.ExitStack() as SL:
                S.barrier()
                L = layer_bufs(SL, NSC)
                US = sb("US", [128, KC, NS, 34], F32, SL)
                TSTG = sb("TSTG", [128, KC, NS, 30], F32, SL)
                FSTG = sb("FSTG", [128, 44, NS, 2], F32, SL)
                DMA("sp", TSTG[:], sca_d, "ld0", (), ["TSTG"])
                for m in range(KC):
                    CP("act", US[:, m, :, 0:30], TSTG[:, m], ["TSTG"], ["U"])
                conv_module(L, HS, "HS", NSC, NS, 34, US, False, cas_o, TSTG)
                UPG = sb("S_UPG", [128, NF, NS, 6], F32, SL)
                UPU = sb("S_UPU", [128, NF, NS, 6], F32, SL)
                DMA("sp", FSTG[:], sff_d[0], "ld0", (), ["FSTG"])
                CP("act", UPG[:, :, :, 0:2], FSTG[:, 0:NF], ["FSTG"], ["UPG"])
                CP("act", UPU[:, :, :, 0:2], FSTG[:, NF:2 * NF], ["FSTG"], ["UPU"])
                conv_ffn(L, 0, HS, "HS", 0, NSC, NS, UPG, UPU, False, ffs_o[0], FSTG, True)
                ROPS = sb("ROPS", [32, 2, NSC], F32, SL)
                DMA("sp", ROPS[:, 0, :], cosks_d, "ld0", (), ["ROPE"])
                DMA("sp", ROPS[:, 1, :], sinks_d, "ld0", (), ["ROPE"])
                shared_kv(L, HS, "HS", 0, NSC, ROPS[:, 0, :], ROPS[:, 1, :], CSF[:], KRSF[:], CSB[:], KRSB[:])
                DMA("sp", cs_o, CSF[:], "o_cs", ["CF32"], [])
                DMA("sp", krs_o, KRSF[:], "o_cs", ["KRF32"], [])
                qlat(L, HS, "HS", 0, NSC, QLS)

            with contextlib.ExitStack() as SA:
                S.barrier()
                Q = {}
                Q["QSQ"] = sb("Q_SQ", [96, NSC], BF16, SA)
                Q["QRS"] = sb("Q_RS", [96, NSC], F32, SA)
                Q["QX"] = sb("Q_X", [96, NSC], F32, SA)
                Q["QXB"] = sb("Q_XB", [96, NSC], BF16, SA)
                RQ = sb("RQ", [96, 2, NSC], F32, SA)
                DMA("sp", RQ[:, 0, :], cosqs_d, "ld0", (), ["ROPQ"])
                DMA("sp", RQ[:, 1, :], sinqs_d, "ld0", (), ["ROPQ"])
                WUKT = sb("WUKT", [64, 2048], BF16, SA)
                DMA("pool", WUKT[:], wukT_d, "c1", (), ["WUKT"])
                QTS = sb("QTS", [96, NSC], BF16, SA)
                QG = sb("QG", [64, NSC], BF16, SA)
                QABS = sb("QABS", [128, NS, 16, 4], BF16, SA)
                QRT = sb("QRT", [96, NS, 16, 4], BF16, SA)
                QR0 = sb("QR0", [128, NS, 16, 4], BF16, SA)
                MEMSET("dve", QR0[:], 0.0, ["QR0"])
                o_gk = VOFF["gk64"][0]
                for h in range(16):
                    qhead(Q, h, QLS, NSC, RQ[:, 0, :], RQ[:, 1, :], QTS[:], "QTS", "ROPQ")
                    CP("act", QRT[64:96, :, h, :], QTS[64:96, :].rearrange("p (s q) -> p s q", q=4), ["QTS"], ["QRT"])
                    TSC("dve", QG[:], QTS[0:64, :], VEC[0:64, o_gk:o_gk + 1], None, ALU.mult, None, ["QTS"], ["QG"])
                    pa, pra = bank()
                    MM(pa[:, 0:NSC], WUKT[:, h * 128:h * 128 + 128], QG[:], True, True, ["QG", "WUKT"], [pra])
                    CP("act", QABS[:, :, h, :], pa[:, 0:NSC].rearrange("p (s q) -> p s q", q=4), [pra], ["QABS"])
                DMA("sp", QR0[0:32], QRT[64:96], "ld1", ["QRT"], ["QR0"])

                CNAT = sb("CNAT", [64, 128], BF16, SA)
                pt_, ptr_ = bank()
                TR(pt_[0:64, 0:128], CSF[:], CFm("identf"), ["CF32"], [ptr_])
                CP("act", CNAT[:], pt_[0:64, 0:128], [ptr_], ["CNAT"])
                RSTN = sb("RSTN", [64, 16], F32, SA)
                SQU = sb("SQU", [128, 1024], BF16, SA)
                KU = [banks[0], banks[1]]
                for hf in range(2):
                    MM(KU[hf][0:64, :], CSB[:], WUK[:, hf * 512:hf * 512 + 512], True, True, ["CBF"], ["B%d" % hf])
                    ACT(SQU[0:64, hf * 512:hf * 512 + 512], KU[hf][0:64, :], AF.Square, ["B%d" % hf], ["SQU"])
                RED(RSTN[:], SQU[0:64, :].rearrange("p (h d) -> p h d", d=64), ALU.add, ["SQU"], ["RSTN"])
                TSC("dve", RSTN[:], RSTN[:], 1.0 / 64, None, ALU.mult, None, ["RSTN"], ["RSTN"])
                ACT(RSTN[:], RSTN[:], AF.Sqrt, ["RSTN"], ["RSTN"], bias=EPS, scale=1.0)
                RECIP(RSTN[:], RSTN[:], ["RSTN"], ["RSTN"])
                MNEW = sb("MNEW", [64, NS, 64], F32, SA)
                DMA("sp", MNEW[:], mnew_d, "ld1", (), ["MNEW"])

                PT = sb("PT", [128, NS], I32, SA)
                DMA("sp", PT[:], pt_d, "ld1", (), ["PT"])
                IDXC = sb("IDXC", [128, NS, 8], I32, SA)
                IDXK = sb("IDXK", [128, NS, 2], I32, SA)
                for c8 in range(8):
                    TSC("dve", IDXC[:, :, c8], PT[:], 8, c8, ALU.mult, ALU.add, ["PT"], ["IDXC"])
                for c2 in range(2):
                    TSC("dve", IDXK[:, :, c2], PT[:], 2, c2, ALU.mult, ALU.add, ["PT"], ["IDXK"])

                GC = [sb("GC%d" % i, [128, 128, 128], BF16, SA) for i in range(2)]
                GK = [sb("GK%d" % i, [128, 128, 32], BF16, SA) for i in range(2)]
                SS_ = sb("SSEQ", [128, 129, 64], F32, SA)
                PP = sb("PSEQ", [128, 129, 64], BF16, SA)
                SSQ2 = [sb("SSQ%d" % i, [128, 4, 16], F32, SA) for i in range(2)]
                SQU2 = [sb("SQU2_%d" % i, [128, 1024], BF16, SA) for i in range(2)]
                CTS = [sb("CTS%d" % i, [128, 512], BF16, SA) for i in range(2)]
                KTS = [sb("KTS%d" % i, [128, 512], BF16, SA) for i in range(2)]
                for i in range(2):
                    MEMSET("dve", KTS[i][:], 0.0, ["KTS%d" % i])
                MX = sb("MX", [128, 64], F32, SA)
                MXC = sb("MXC", [64, 1], F32, SA)
                DG = sb("DG", [64, 64], F32, SA)
                MB = sb("MB", [128, 64], F32, SA)
                RLB = sb("RLB", [128, 64], F32, SA)
                OLT = sb("OLT", [128, NS, 16, 4], BF16, SA)
                MEMSET("dve", SS_[:, 128, :], -2000.0, ["SSEQ"])

                for s in range(NS):
                    g = s % 2
                    for c8 in range(8):
                        S.dma("pool", (lambda e, g=g, c8=c8, s=s: e.indirect_dma_start(
                            out=GC[g][:, c8 * 16:(c8 + 1) * 16, :].rearrange("p a b -> p (a b)"), out_offset=None, in_=cc_d,
                            in_offset=bass.IndirectOffsetOnAxis(ap=IDXC[:, s, c8:c8 + 1], axis=0))),
                            "g%d" % g, ["IDXC"], ["GC%d" % g])
                    for c2 in range(2):
                        S.dma("pool", (lambda e, g=g, c2=c2, s=s: e.indirect_dma_start(
                            out=GK[g][:, c2 * 64:(c2 + 1) * 64, :].rearrange("p a b -> p (a b)"), out_offset=None, in_=ckr_d,
                            in_offset=bass.IndirectOffsetOnAxis(ap=IDXK[:, s, c2:c2 + 1], axis=0))),
                            "g%d" % g, ["IDXK"], ["GK%d" % g])
                    qa = QABS[:, s, :, :].rearrange("p h q -> p (h q)")
                    qr = QR0[:, s, :, :].rearrange("p h q -> p (h q)")
                    qr32 = QR0[0:32, s, :, :].rearrange("p h q -> p (h q)")
                    def stT(ub):
                        b2 = ub % 2
                        for u4 in range(4):
                            u = ub * 4 + u4
                            TR(bankb[:, u4 * 128:(u4 + 1) * 128], GC[g][:, u, :], CBm("identb"), ["GC%d" % g], ["BB"])
                        for u4 in range(4):
                            u = ub * 4 + u4
                            TR(bankb[0:32, 512 + u4 * 128:512 + (u4 + 1) * 128], GK[g][:, u, :], CBm("identb"), ["GK%d" % g], ["BB"])
                        CP("act", CTS[b2][:], bankb[:, 0:512], ["BB"], ["CTS%d" % b2])
                        CP("dve", KTS[b2][0:32, :], bankb[0:32, 512:1024], ["BB"], ["KTS%d" % b2])

                    def stM(ub):
                        b2 = ub % 2
                        nd, ndr = banks[4 + b2], "B%d" % (4 + b2)
                        for u4 in range(4):
                            cT = CTS[b2][:, u4 * 128:(u4 + 1) * 128]
                            kk = 2 * (u4 % 2)
                            sq_, sqr = SQU2[u4 % 2], "SQU%d" % (u4 % 2)
                            for hf in range(2):
                                MM(banks[kk + hf][:, :], cT, WUK[:, hf * 512:hf * 512 + 512], True, True,
                                   ["CTS%d" % b2], ["B%d" % (kk + hf)])
                                ACT(sq_[:, hf * 512:hf * 512 + 512], banks[kk + hf][:, :], AF.Square, ["B%d" % (kk + hf)], [sqr])
                            RED(SSQ2[b2][:, u4, :], sq_[:].rearrange("p (h d) -> p h d", d=64), ALU.add, [sqr], ["SSQ%d" % b2])
                            MM(nd[:, u4 * 64:u4 * 64 + 64], cT, qa, True, True, ["CTS%d" % b2, "QABS"], [ndr], skip_group_check=True)
                            MM(nd[:, 256 + u4 * 64:256 + u4 * 64 + 64], KTS[b2][:, u4 * 128:(u4 + 1) * 128], qr, True, True,
                               ["KTS%d" % b2, "QR0"], [ndr], skip_group_check=True)

                    def stE(ub):
                        b2 = ub % 2
                        nd, ndr = banks[4 + b2], "B%d" % (4 + b2)
                        sq = SSQ2[b2]
                        sr = "SSQ%d" % b2
                        ACT(sq[:], sq[:], AF.Sqrt, [sr], [sr], bias=EPS, scale=1.0 / 64)
                        RECIP(sq[:], sq[:], [sr], [sr])
                        sv = SS_[:, ub * 4:ub * 4 + 4, :]
                        TT("dve", sv.rearrange("p u (h q) -> p u h q", q=4),
                           nd[:, 0:256].rearrange("p (u h q) -> p u h q", u=4, q=4),
                           sq[:].unsqueeze(3).to_broadcast([128, 4, 16, 4]), ALU.mult, [ndr, sr], ["SSEQ"])
                        TT("dve", sv, sv, nd[:, 256:512].rearrange("p (u c) -> p u c", u=4), ALU.add, [ndr, "SSEQ"], ["SSEQ"])

                    for ub in range(32):
                        stT(ub)
                        if ub >= 1:
                            stM(ub - 1)
                        if ub >= 2:
                            stE(ub - 2)
                    stM(31)
                    stE(30)
                    stE(31)
                    nd, ndr = bank(4, 6)
                    MM(nd[0:64, 0:64], CSB[:], qa, True, True, ["CBF", "QABS"], [ndr], skip_group_check=True)
                    MM(nd[0:64, 256:320], KRSB[:], qr32, True, True, ["KRBF", "QR0"], [ndr], skip_group_check=True)
                    svn = SS_[0:64, 128, :]
                    TT("dve", svn.rearrange("p (h q) -> p h q", q=4), nd[0:64, 0:64].rearrange("p (h q) -> p h q", q=4),
                       RSTN[:].unsqueeze(2).to_broadcast([64, 16, 4]), ALU.mult, [ndr, "RSTN"], ["SSEQ"])
                    TT("dve", svn, svn, nd[0:64, 256:320], ALU.add, [ndr, "SSEQ"], ["SSEQ"])
                    TT("dve", svn, svn, MNEW[:, s, :], ALU.add, ["SSEQ", "MNEW"], ["SSEQ"])
                    RED(MX[:, 0:1], SS_[:].rearrange("p u c -> p (u c)"), ALU.max, ["SSEQ"], ["MX"])
                    pm, prm = bank(4, 6)
                    TR(pm[0:1, 0:128], MX[:, 0:1], CFm("identf"), ["MX"], [prm])
                    RED(MXC[0:1, 0:1], pm[0:1, 0:128], ALU.max, [prm], ["MXC"])
                    TSC("dve", DG[0:1, 0:64], CFm("onesf")[0:1, 0:64], MXC[0:1, 0:1], None, ALU.mult, None, ["MXC"], ["DG"])
                    pb, pbr = bank(4, 6)
                    MM(pb[:, 0:64], CFm("onesf")[0:1, :], DG[0:1, 0:64], True, True, ["DG"], [pbr])
                    TSC("dve", MB[:, 0:1], pb[:, 0:1], -float(SM_SCALE), None, ALU.mult, None, [pbr], ["MB"])
                    ACT(PP[:], SS_[:], AF.Exp, ["SSEQ", "MB"], ["PSEQ"], bias=MB[:, 0:1], scale=float(SM_SCALE))
                    pacc, paccr = banks[6], "B6"
                    plb, plbr = banks[0], "B0"
                    for u in range(128):
                        MM(pacc[:, 0:64], GC[g][:, u, :], PP[:, u, :], u == 0, False, ["GC%d" % g, "PSEQ"], [paccr])
                    MM(pacc[:, 0:64], CNAT[:], PP[0:64, 128, :], False, True, ["CNAT", "PSEQ"], [paccr])
                    for u in range(128):
                        MM(plb[:, 0:64], CBm("onesb"), PP[:, u, :], u == 0, False, ["PSEQ"], [plbr])
                    MM(plb[:, 0:64], CBm("onesb", 64), PP[0:64, 128, :], False, True, ["PSEQ"], [plbr])
                    RECIP(RLB[:], plb[:, 0:64], [plbr], ["RLB"])
                    TT("dve", OLT[:, s, :, :].rearrange("p h q -> p (h q)"), pacc[:, 0:64], RLB[:], ALU.mult, [paccr, "RLB"], ["OLT"])
                for kq in range(KC):
                    po, por = bank()
                    for hh in range(2):
                        h = 2 * kq + hh
                        MM(po[hh * 64:hh * 64 + 64, 0:NSC], WUV[:, h * 64:h * 64 + 64], OLT[:, :, h, :], True, True, ["OLT"], [por])
                    CP("act", ATS[:, kq, :], po[:, 0:NSC], [por], ["ATS"])

            with contextlib.ExitStack() as SL:
                S.barrier()
                L = layer_bufs(SL, NSC)
                for kq in range(KC):
                    wt, wr = wload(wo_d[kq], 1024)
                    for mo in range(KC):
                        po, por = bank()
                        MM(po[:, 0:NSC], wt[:, mo * 128:mo * 128 + 128], ATS[:, kq, :], True, True, [wr, "ATS"], [por])
                        TT("dve", HS[:, mo, :], po[:, 0:NSC], HS[:, mo, :], ALU.add, [por, "HS"], ["HS"])
                UPG = sb("S_UPG", [128, NF, NS, 6], F32, SL)
                UPU = sb("S_UPU", [128, NF, NS, 6], F32, SL)
                FSTG = sb("FSTG", [128, 44, NS, 2], F32, SL)
                DMA("sp", FSTG[:], sff_d[1], "ld0", (), ["FSTG"])
                CP("act", UPG[:, :, :, 0:2], FSTG[:, 0:NF], ["FSTG"], ["UPG"])
                CP("act", UPU[:, :, :, 0:2], FSTG[:, NF:2 * NF], ["FSTG"], ["UPU"])
                conv_ffn(L, 1, HS, "HS", 0, NSC, NS, UPG, UPU, False, ffs_o[1], FSTG, True)
                DMA("sp", ys_o, HS[:], "o_ys", ["HS"], [])

        with contextlib.ExitStack() as PR:
            S.barrier()
            HT = [sb("H%d" % t, [128, KC, TW], F32, PR) for t in range(NT)]
            QLP = sb("QLP", [128, 2, NT, NQ], BF16, PR)
            with contextlib.ExitStack() as PA:
                S.barrier()
                L = layer_bufs(PA, TW)
                UP_ = sb("UP", [128, KC, 1, TW], F32, PA)
                UPG = sb("P_UPG", [128, 2, 1, TW], F32, PA)
                UPU = sb("P_UPU", [128, 2, 1, TW], F32, PA)
                ROPK = sb("ROPK", [32, 2, TS_], F32, PA)
                CPF = sb("CPF", [128, TS_], F32, PA)
                CPB = sb("CPB", [128, TS_], BF16, PA)
                KPF = sb("KPF", [32, TS_], F32, PA)
                KPB = sb("KPB", [32, TS_], BF16, PA)
                TSTGP = sb("TSTGP", [128, KC, 1, 30], F32, PA)
                for t in range(NT):
                    Hh, Hr = HT[t], "H%d" % t
                    DMA("sp", Hh[:], xp_d[t], "ld0", (), [Hr])
                    last = (t == NT - 1)
                    conv_module(L, Hh, Hr, TW, 1, TW, UP_, t == 0, cap_o if last else None, TSTGP)
                    conv_ffn(L, 0, Hh, Hr, 30, TW, 1, UPG, UPU, t == 0, ffp_o[0] if last else None, "o_ffp", False)
                    DMA("sp", ROPK[:, 0, :], cosk_d[:, t, :], "ld0", (), ["ROPE"])
                    DMA("sp", ROPK[:, 1, :], sink_d[:, t, :], "ld0", (), ["ROPE"])
                    shared_kv(L, Hh, Hr, HALO, TS_, ROPK[:, 0, :], ROPK[:, 1, :], CPF[:], KPF[:], CPB[:], KPB[:])
                    DMA("sp", cp_o[t], CPF[:], "o_cp", ["CF32"], [])
                    DMA("sp", krp_o[t], KPF[:], "o_cp", ["KRF32"], [])
                    DMA("sp", xch_in.ap()[0:128, t * TS_:(t + 1) * TS_], CPB[:], "xin", ["CBF"], ["XIN"])
                    DMA("sp", xch_in.ap()[128:160, t * TS_:(t + 1) * TS_], KPB[:], "xin", ["KRBF"], ["XIN"])
                    qlat(L, Hh, Hr, 32, NQ, QLP[:, :, t, :])
            S.op("pool", lambda e: e.collective_compute("AllGather", ALU.bypass, replica_groups=[[0, 1, 2, 3], [4, 5, 6, 7]],
                                                        ins=[xch_in.ap()], outs=[xch_out.ap()]), ["XIN"], ["XOUT"])

            with contextlib.ExitStack() as PB:
                S.barrier()
                NKP = NKC * 128
                CTA = sb("CTA", [128, NKP], BF16, PB)
                KT = sb("KT", [96, NKP], BF16, PB)
                VH = sb("VH", [128, NKC, 65], BF16, PB)
                AOP = sb("AOP", [128, NT, 3, 128], BF16, PB)
                ATT = sb("ATT", [128, NQ], BF16, PB)
                RQP = sb("RQP", [96, 2, NT, NQ], F32, PB)
                THR = sb("THR", [128, NT * NKC], F32, PB)
                PTb = [sb("PTb%d" % i, [128, NQ], BF16, PB) for i in range(4)]
                QT2 = [sb("QT%d" % i, [96, NQ], BF16, PB) for i in range(2)]
                KSQ = [sb("KSQ%d" % i, [64, 512], BF16, PB) for i in range(2)]
                KRS_ = [sb("KRS%d" % i, [64, 512], F32, PB) for i in range(2)]
                RL = sb("RL", [128, 1], F32, PB)
                Q = {}
                Q["QSQ"] = sb("QP_SQ", [96, NQ], BF16, PB)
                Q["QRS"] = sb("QP_RS", [96, NQ], F32, PB)
                Q["QX"] = sb("QP_X", [96, NQ], F32, PB)
                Q["QXB"] = sb("QP_XB", [96, NQ], BF16, PB)
                DMA("sp", RQP[:, 0], cosq_d, "ld0", (), ["ROPQ"])
                DMA("sp", RQP[:, 1], sinq_d, "ld0", (), ["ROPQ"])
                DMA("sp", THR[:], thr_d, "ld0", (), ["THR"])
                MEMSET("dve", CTA[:, NPOS:NKP], 0.0, ["CTA"])
                MEMSET("dve", KT[:, NPOS:NKP], 0.0, ["KT"])
                MEMSET("dve", VH[:, :, 64:65], 1.0, ["VH"])
                for gt in range(24):
                    r, t = gt % 4, gt // 4
                    DMA("sp", CTA[:, gt * TS_:(gt + 1) * TS_], xch_out.ap()[r * 160:r * 160 + 128, t * TS_:(t + 1) * TS_],
                        "ld1", ["XOUT"], ["CTA"])
                    DMA("sp", KT[64:96, gt * TS_:(gt + 1) * TS_], xch_out.ap()[r * 160 + 128:r * 160 + 160, t * TS_:(t + 1) * TS_],
                        "ld1", ["XOUT"], ["KT"])
                o_gk = VOFF["gk64"][0]
                qsub = [(0, 128), (128, 128), (256, NQ - 256)]
                pctr = 0
                qgen = [None]
                for h in range(16):
                    def ka(kt):
                        n = 512 if kt < 16 else NKP - 16 * 512
                        cs = kt * 512
                        pk, prk = bank(0, 7)
                        MM(pk[0:64, 0:n], WUK[:, h * 64:h * 64 + 64], CTA[:, cs:cs + n], True, True, ["CTA"], [prk])
                        ACT(KSQ[kt % 2][:, 0:n], pk[0:64, 0:n], AF.Square, [prk], ["KSQ%d" % (kt % 2)])
                        return (kt, n, cs, pk, prk)

                    def kb(kt, n, cs, pk, prk):
                        pm, prm = bank(0, 7)
                        kr_ = KRS_[kt % 2]
                        krr = "KRS%d" % (kt % 2)
                        MM(pm[0:64, 0:n], CBm("o64", 64), KSQ[kt % 2][:, 0:n], True, True, ["KSQ%d" % (kt % 2)], [prm])
                        rstd_lnexp(pm[0:64, 0:n], kr_[:, 0:n], prm, krr)
                        return (kt, n, cs, pk, prk)

                    def kc_(kt, n, cs, pk, prk):
                        kr_ = KRS_[kt % 2]
                        krr = "KRS%d" % (kt % 2)
                        STT("dve", KT[0:64, cs:cs + n], pk[0:64, 0:n], VEC[0:64, o_gk:o_gk + 1], kr_[:, 0:n], ALU.mult, ALU.mult,
                            [prk, krr], ["KT"])

                    kpa, kpb = [], []
                    for kt in range(17):
                        kpa.append(ka(kt))
                        if len(kpa) > 1:
                            kpb.append(kb(*kpa.pop(0)))
                        if len(kpb) > 1:
                            kc_(*kpb.pop(0))
                    while kpa:
                        kpb.append(kb(*kpa.pop(0)))
                        if len(kpb) > 1:
                            kc_(*kpb.pop(0))
                    while kpb:
                        kc_(*kpb.pop(0))
                    for vb in range(9):
                        nchk = 8 if vb < 8 else 1
                        pv, prv = bank(0, 4)
                        for ci in range(nchk):
                            kc = vb * 8 + ci
                            MM(pv[:, ci * 64:ci * 64 + 64], CTA[:, kc * 128:kc * 128 + 128], WUV[:, h * 64:h * 64 + 64], True, True,
                               ["CTA"], [prv], skip_group_check=True)
                        CP("act", VH[:, vb * 8:vb * 8 + nchk, 0:64], pv[:, 0:nchk * 64].rearrange("p (c d) -> p c d", d=64), [prv], ["VH"])
                    for t in range(NT):
                        qi_ = h * NT + t
                        QTt, QTr = QT2[qi_ % 2], "QT%d" % (qi_ % 2)
                        if qgen[0] is None:
                            qhead(Q, h, QLP[:, :, t, :], NQ, RQP[:, 0, t, :], RQP[:, 1, t, :], QTt[:], QTr, "ROPQ")
                        else:
                            for _ in qgen[0]:
                                pass
                        nh, nt_ = (h, t + 1) if t + 1 < NT else (h + 1, 0)
                        if nh < 16:
                            nq = qi_ + 1
                            qgen[0] = qhead_gen(Q, nh, QLP[:, :, nt_, :], NQ, RQP[:, 0, nt_, :], RQP[:, 1, nt_, :],
                                                QT2[nq % 2][:], "QT%d" % (nq % 2), "ROPQ")
                        else:
                            qgen[0] = iter(())
                        nkc = min(NKC, -(-((4 * t + 4) * TS_) // 128))
                        first_masked = max(0, (4 * t * TS_ - 2 - 127 + 127) // 128)
                        def s_stage(kc, pctr):
                            kn = 128 if kc < NKC - 1 else NPOS - 128 * (NKC - 1)
                            pst, pstr = bank(0, 3)
                            MM(pst[0:kn, 0:NQ], KT[:, kc * 128:kc * 128 + kn], QTt[:], True, True, ["KT", QTr], [pstr])
                            pb_ = PTb[pctr % 4]
                            pbr = "PTb%d" % (pctr % 4)
                            ACT(pb_[0:kn, :], pst[0:kn, 0:NQ], AF.Exp, [pstr], [pbr], scale=float(SM_SCALE))
                            if kc >= first_masked:
                                col = t * NKC + kc
                                STT("dve", pb_[0:kn, :], CFm("iota")[0:kn, :], THR[0:kn, col:col + 1], pb_[0:kn, :], ALU.is_ge, ALU.mult,
                                    [pbr, "THR"], [pbr])
                            return (kc, kn, pb_, pbr)

                        def pv_stage(kc, kn, pb_, pbr):
                            for qi, (q0, qn) in enumerate(qsub):
                                MM(banks[3 + qi][0:qn, 0:65], pb_[0:kn, q0:q0 + qn], VH[0:kn, kc, :], kc == 0, kc == nkc - 1,
                                   [pbr, "VH"], ["B%d" % (3 + qi)])

                        pend = []
                        for kc in range(nkc):
                            pend.append(s_stage(kc, pctr))
                            pctr += 1
                            if kc in (2, 5, 8):
                                next(qgen[0], None)
                            if len(pend) > 2:
                                pv_stage(*pend.pop(0))
                        while pend:
                            pv_stage(*pend.pop(0))
                        for qi, (q0, qn) in enumerate(qsub):
                            br = "B%d" % (3 + qi)
                            TSC("dve", RL[0:qn, :], banks[3 + qi][0:qn, 64:65], 1e-30, None, ALU.max, None, [br], ["RL"])
                            RECIP(RL[0:qn, :], RL[0:qn, :], ["RL"], ["RL"])
                            TSC("dve", AOP[0:qn, t, qi, (h % 2) * 64:(h % 2) * 64 + 64], banks[3 + qi][0:qn, 0:64], RL[0:qn, 0:1], None,
                                ALU.mult, None, [br, "RL"], ["AOP"])
                    if h % 2 == 1:
                        kq = h // 2
                        wt, wr = wload(wo_d[kq], 1024)
                        for t in range(NT):
                            for qi, (q0, qn) in enumerate(qsub):
                                TR(bankb[:, q0:q0 + qn], AOP[0:qn, t, qi, :], CBm("identb")[0:qn, 0:qn], ["AOP"], ["BB"])
                            CP("act", ATT[:], bankb[:, 0:NQ], ["BB"], ["ATT"])
                            for mo in range(KC):
                                po, por = bank(0, 3)
                                MM(po[:, 0:NQ], wt[:, mo * 128:mo * 128 + 128], ATT[:], True, True, [wr, "ATT"], [por])
                                TT("dve", HT[t][:, mo, 32:TW], po[:, 0:NQ], HT[t][:, mo, 32:TW], ALU.add, [por, "H%d" % t], ["H%d" % t])

            with contextlib.ExitStack() as PC:
                S.barrier()
                L = layer_bufs(PC, TW)
                UPG = sb("P_UPG", [128, 2, 1, TW], F32, PC)
                UPU = sb("P_UPU", [128, 2, 1, TW], F32, PC)
                for t in range(NT):
                    Hh, Hr = HT[t], "H%d" % t
                    last = (t == NT - 1)
                    conv_ffn(L, 1, Hh, Hr, 32, TW, 1, UPG, UPU, t == 0, ffp_o[1] if last else None, "o_ffp", False)
                    DMA("sp", yp_o[t], Hh[:, :, HALO:TW], "o_yp", [Hr], [])
        S.emit()
    return nc


def kernel(x_prompt, x_sample, state_conv_a, state_ffn_conv, cache_kv_latent, cache_k_rope, page_table,
           meta_tokens, norm_mix, norm_ffn,
           a_w_pw1, a_b_pw1, a_w_dw, a_b_dw, a_ln_g, a_ln_b, a_w_pw2, a_b_pw2,
           ffn_w_up, ffn_w_dw, ffn_w_down,
           kv_norm, mla_w_dkv, mla_lat_norm, mla_w_kr, mla_knorm_rope, mla_w_uk, mla_w_uv, mla_knorm_nope,
           mla_w_dq, mla_q_lat_norm, mla_w_uq, mla_qnorm_nope, mla_qnorm_rope, mla_w_o):
    f32 = np.float32
    A = lambda a: np.asarray(a)
    x_prompt, x_sample = A(x_prompt), A(x_sample)
    n_phys = int(A(cache_kv_latent).shape[0])
    cache_c = np.ascontiguousarray(A(cache_kv_latent), f32).reshape(n_phys * 8, 2048)
    cache_kr = np.ascontiguousarray(A(cache_k_rope), f32).reshape(n_phys * 2, 2048)
    page_table = A(page_table).astype(np.int32)

    vec = np.zeros((128, NVEC), f32)
    def putv(name, m):
        o, w = VOFF[name]
        m = np.asarray(m, f32)
        vec[:m.shape[0], o:o + w] = m.reshape(m.shape[0], w)
    putv("nm0", _fm(A(norm_mix)[0], 8)); putv("nm1", _fm(A(norm_mix)[1], 8))
    putv("nf0", _fm(A(norm_ffn)[0], 8)); putv("nf1", _fm(A(norm_ffn)[1], 8))
    putv("kvn", _fm(A(kv_norm), 8))
    putv("b1", _fm(A(a_b_pw1)[0], 16)); putv("bdw", _fm(A(a_b_dw)[0], 8))
    putv("lng", _fm(A(a_ln_g)[0], 8)); putv("lnb", _fm(A(a_ln_b)[0], 8)); putv("b2", _fm(A(a_b_pw2)[0], 8))
    putv("wdwa", np.ascontiguousarray(A(a_w_dw)[0].reshape(31, 8, 128).transpose(2, 1, 0)).reshape(128, 8 * 31))
    for l in range(2):
        putv("wdwf%d" % l, np.ascontiguousarray(A(ffn_w_dw)[l].reshape(3, 44, 128).transpose(2, 1, 0)).reshape(128, 44 * 3))
    putv("latn", A(mla_lat_norm).reshape(128, 1))
    putv("knr", A(mla_knorm_rope).reshape(32, 1))
    putv("qln", _fm(A(mla_q_lat_norm)[0], 2))
    putv("g96", np.concatenate([A(mla_qnorm_nope)[0], A(mla_qnorm_rope)[0]]).reshape(96, 1))
    putv("gk64", A(mla_knorm_nope).reshape(64, 1))
    cb = _consts_bf()
    cf = _consts_f()

    W1 = A(a_w_pw1)[0].astype(f32)
    w1r = W1.reshape(8, 128, 2, 8, 128)
    w1l = np.ascontiguousarray(w1r.transpose(3, 1, 0, 2, 4)).reshape(8, 128, 2048)
    W2 = A(a_w_pw2)[0].astype(f32).reshape(8, 128, 4, 2, 128)
    w2l = np.ascontiguousarray(W2.transpose(2, 1, 3, 0, 4)).reshape(4, 128, 2048)
    WU = A(ffn_w_up).astype(f32).reshape(2, 8, 128, 2, NF, 128)
    wupl = np.ascontiguousarray(WU.transpose(0, 4, 2, 1, 3, 5)).reshape(2, NF, 128, 2048)
    WD = A(ffn_w_down).astype(f32).reshape(2, 2, 11, 128, 8, 128)
    wdnl = np.ascontiguousarray(WD.transpose(0, 4, 1, 3, 2, 5)).reshape(2, 16, 128, 1408)
    wol = np.ascontiguousarray(A(mla_w_o)[0].astype(f32)).reshape(8, 128, 1024)
    wdq = np.ascontiguousarray(A(mla_w_dq)[0].astype(f32).reshape(8, 128, 256).transpose(1, 0, 2)).reshape(128, 2048)
    wuq = np.ascontiguousarray(A(mla_w_uq)[0].astype(f32).reshape(2, 128, 1536))
    wdkv = np.ascontiguousarray(A(mla_w_dkv).astype(f32).reshape(8, 128, 128).transpose(1, 0, 2)).reshape(128, 1024)
    wkr = np.ascontiguousarray(A(mla_w_kr).astype(f32).reshape(8, 128, 32).transpose(1, 0, 2)).reshape(128, 256)
    wuk = np.ascontiguousarray(A(mla_w_uk).astype(f32).reshape(128, 1024))
    wuv = np.ascontiguousarray(A(mla_w_uv).astype(f32).reshape(128, 1024))
    wukT = np.ascontiguousarray(A(mla_w_uk).astype(f32).transpose(2, 1, 0)).reshape(64, 2048)

    iota = np.arange(TW)
    masknew = np.full((64, NS, 64), -2000.0, f32)
    for s in range(NS):
        for k in range(4):
            for q in range(4):
                if k <= q:
                    masknew[s * 4 + k, s, np.arange(16) * 4 + q] = 0.0
    pos_s = PAST + (np.arange(NSC) % 4)
    cks, sks = _rope_tab(pos_s)
    cosqs = np.ones((96, NSC), f32); sinqs = np.zeros((96, NSC), f32)
    cosqs[64:], sinqs[64:] = cks, sks

    hp_all = np.concatenate([np.broadcast_to(A(meta_tokens).astype(f32)[None], (2, NMETA, D)), x_prompt.astype(f32)], axis=1)
    shared = dict(vec=vec, cb=cb, cf=cf, w1l=w1l, w2l=w2l, wupl=wupl, wdnl=wdnl, wol=wol, wdq=wdq, wuq=wuq, wdkv=wdkv,
                  wkr=wkr, wuk=wuk, wuv=wuv, wukT=wukT, cache_c=cache_c, cache_kr=cache_kr, masknew=masknew,
                  cosqs=cosqs, sinqs=sinqs, cosks=cks, sinks=sks)
    in_maps = []
    for c in range(8):
        b, j = c // 4, c % 4
        xp = np.zeros((NT, 128, KC, TW), f32)
        cosq = np.ones((96, NT, NQ), f32); sinq = np.zeros((96, NT, NQ), f32)
        cosk = np.zeros((32, NT, TS_), f32); sink = np.zeros((32, NT, TS_), f32)
        thr = np.zeros((128, NT * NKC), f32)
        for t in range(NT):
            g = 4 * t + j
            p0 = g * TS_ - HALO
            win = np.zeros((TW, D), f32)
            lo = max(0, p0)
            win[lo - p0:] = hp_all[b, lo:p0 + TW]
            xp[t] = win.T.reshape(KC, 128, TW).transpose(1, 0, 2)
            posq = p0 + 32 + np.arange(NQ)
            cq, sq_ = _rope_tab(posq)
            cosq[64:, t], sinq[64:, t] = cq, sq_
            cosk[:, t], sink[:, t] = cq[:, 2:], sq_[:, 2:]
            thr[:, t * NKC:(t + 1) * NKC] = (128 * np.arange(NKC) - (p0 + 32))[None, :]
        sl = slice(NS * c, NS * c + NS)
        xs = np.ascontiguousarray(x_sample[sl].astype(f32).reshape(NSC, D).T.reshape(KC, 128, NSC).transpose(1, 0, 2))
        sca = np.ascontiguousarray(A(state_conv_a)[0, sl].astype(f32).transpose(2, 0, 1).reshape(KC, 128, NS, 30).transpose(1, 0, 2, 3))
        sff = np.ascontiguousarray(A(state_ffn_conv)[:, sl].astype(f32).transpose(0, 3, 1, 2).reshape(2, 44, 128, NS, 2).transpose(0, 2, 1, 3, 4))
        cmask = np.ones((128, HALO), f32)
        if j == 0:
            cmask[:] = 0.0
        m = dict(shared)
        m.update(xp=xp, xs=xs, sca=sca, sff=sff, ptT=np.ascontiguousarray(page_table[sl].T), colmask=cmask,
                 cosq=cosq, sinq=sinq, cosk=cosk, sink=sink, thr=thr)
        in_maps.append(m)

    nc = build_program(n_phys)
    res = run_bass_kernel_spmd(nc, in_maps, core_ids=list(range(8)))
    R = res.results

    y_prompt = np.zeros((2, NPOS, D), f32)
    kvp = np.zeros((2, NPOS, 128), f32)
    krp = np.zeros((2, NPOS, 32), f32)
    y_sample = np.zeros((128, 4, D), f32)
    cas = np.zeros((1, 128, 30, D), f32)
    ffs = np.zeros((2, 128, 2, 2 * DFF), f32)
    kvs = np.zeros((128, 4, 128), f32)
    krs = np.zeros((128, 4, 32), f32)
    cap = np.zeros((1, 2, 30, D), f32)
    ffp = np.zeros((2, 2, 2, 2 * DFF), f32)
    for c in range(8):
        b, j = c // 4, c % 4
        r = R[c]
        for t in range(NT):
            g = 4 * t + j
            y_prompt[b, g * TS_:(g + 1) * TS_] = np.asarray(r["yp"][t]).transpose(1, 0, 2).reshape(D, TS_).T
            kvp[b, g * TS_:(g + 1) * TS_] = np.asarray(r["cp"][t]).T
            krp[b, g * TS_:(g + 1) * TS_] = np.asarray(r["krp"][t]).T
        sl = slice(NS * c, NS * c + NS)
        y_sample[sl] = np.asarray(r["ys"]).transpose(1, 0, 2).reshape(D, NS, 4).transpose(1, 2, 0)
        cas[0, sl] = np.asarray(r["cas"]).transpose(1, 0, 2, 3).reshape(D, NS, 30).transpose(1, 2, 0)
        ffs[:, sl] = np.asarray(r["ffs"]).transpose(0, 2, 1, 3, 4).reshape(2, 2 * DFF, NS, 2).transpose(0, 2, 3, 1)
        kvs[sl] = np.asarray(r["cs"]).T.reshape(NS, 4, 128)
        krs[sl] = np.asarray(r["krs"]).T.reshape(NS, 4, 32)
        if j == 3:
            cap[0, b] = np.asarray(r["cap"]).transpose(1, 0, 2).reshape(D, 30).T
            ffp[:, b] = np.asarray(r["ffp"]).transpose(0, 2, 1, 3).reshape(2, 2 * DFF, 2).transpose(0, 2, 1)
    return (np.ascontiguousarray(y_prompt[:, NMETA:]), y_sample, cap, cas, ffp, ffs, kvp, krp, kvs, krs)
```
